# Optimizing a Trainium2 kernel written in Bass

```python
import jax, jax.numpy as jnp
from jax import lax
import numpy as np

D_MODEL = 2048
BATCH = 2
SEQ = 8192
DEPTH = 2

MIX_WIDTH = D_MODEL
GROUP_WIDTH = MIX_WIDTH // 2
ATT_HEADS = 8
ATT_KV_HEADS = 2
ATT_HEAD_DIM = GROUP_WIDTH // ATT_HEADS
ATT_WINDOW = 128
ATT_BLOCK = 128
ROPE_THETA = 10000.0
ML_HEADS = 4
ML_V_DIM = GROUP_WIDTH // ML_HEADS
ML_QK_DIM = ML_V_DIM // 2
ML_CHUNK = 128
CONV_WIDTH = 3
M_INIT = -1e30
D_FF = 4 * D_MODEL
NORM_EPS = 1e-6

ATT_Q_COLS = ATT_HEADS * ATT_HEAD_DIM
ATT_KV_COLS = ATT_KV_HEADS * ATT_HEAD_DIM
ML_QK_COLS = ML_HEADS * ML_QK_DIM
ML_V_COLS = ML_HEADS * ML_V_DIM
GATE_COLS = 4 * ML_HEADS
IN_COLS = ATT_Q_COLS + 2 * ATT_KV_COLS + 2 * ML_QK_COLS + 2 * ML_V_COLS + GATE_COLS

kernel_name = "hymba_style_swa_mlstm_encoder"


def rms_norm(x, g):
    xf = x.astype(jnp.float32)
    y = xf * lax.rsqrt(jnp.mean(xf * xf, axis=-1, keepdims=True) + NORM_EPS)
    return (y * g.astype(jnp.float32)).astype(x.dtype)


def rope(x, pos):
    half = x.shape[-1] // 2
    inv_freq = ROPE_THETA ** (-jnp.arange(half, dtype=jnp.float32) / half)
    ang = pos.astype(jnp.float32)[:, None] * inv_freq[None, :]
    cos = jnp.cos(ang)[None, :, None, :]
    sin = jnp.sin(ang)[None, :, None, :]
    xf = x.astype(jnp.float32)
    x1, x2 = xf[..., :half], xf[..., half:]
    return jnp.concatenate([x1 * cos - x2 * sin, x2 * cos + x1 * sin], axis=-1).astype(x.dtype)


def window_attention(q, k, v, sink):
    B, S, Hq, D = q.shape
    Hkv = k.shape[2]
    G = Hq // Hkv
    W = ATT_BLOCK
    NB = S // W
    pos = jnp.arange(S)
    q = rope(q, pos)
    k = rope(k, pos)
    pad = ((0, 0), (W, W), (0, 0), (0, 0))
    kb = jnp.pad(k, pad).reshape(B, NB + 2, W, Hkv, D)
    vb = jnp.pad(v, pad).reshape(B, NB + 2, W, Hkv, D)
    kwin = jnp.concatenate([kb[:, :-2], kb[:, 1:-1], kb[:, 2:]], axis=2)
    vwin = jnp.concatenate([vb[:, :-2], vb[:, 1:-1], vb[:, 2:]], axis=2)
    qb = q.reshape(B, NB, W, Hkv, G, D)
    s = jnp.einsum('bnqhgd,bnkhd->bnhgqk', qb, kwin).astype(jnp.float32) * (D ** -0.5)
    qi = jnp.arange(W)[:, None]
    kj = jnp.arange(3 * W)[None, :]
    band = jnp.abs(kj - qi - W) <= ATT_WINDOW
    key_pos = jnp.arange(NB)[:, None] * W - W + jnp.arange(3 * W)[None, :]
    in_range = (key_pos >= 0) & (key_pos < S)
    mask = band[None, :, :] & in_range[:, None, :]
    s = jnp.where(mask[None, :, None, None], s, -jnp.inf)
    sink_f = sink.astype(jnp.float32).reshape(Hkv, G)[None, None, :, :, None, None]
    m = jnp.maximum(jnp.max(s, axis=-1, keepdims=True), sink_f)
    p = jnp.exp(s - m)
    p = p / (jnp.sum(p, axis=-1, keepdims=True) + jnp.exp(sink_f - m))
    o = jnp.einsum('bnhgqk,bnkhd->bnqhgd', p.astype(v.dtype), vwin)
    return o.reshape(B, S, Hq * D)


def mlstm_chunkwise(q, k, v, log_i, log_f):
    B, H, S, dk = q.shape
    dv = v.shape[-1]
    L = ML_CHUNK
    NC = S // L
    qc = q.reshape(B, H, NC, L, dk)
    kc = k.reshape(B, H, NC, L, dk)
    vc = v.reshape(B, H, NC, L, dv)
    li = log_i.reshape(B, H, NC, L)
    b = jnp.cumsum(log_f.reshape(B, H, NC, L), axis=-1)
    g = b[..., -1]
    a = g[..., None] - b + li
    m_loc = jnp.max(a, axis=-1)
    w_end = jnp.exp(a - m_loc[..., None])
    C_loc = jnp.einsum('bhcld,bhcle->bhcde', w_end[..., None] * kc, vc)
    n_loc = jnp.einsum('bhcl,bhcld->bhcd', w_end, kc)

    def step(carry, xs):
        C, n, m = carry
        g_c, m_l, C_l, n_l = xs
        m_new = jnp.maximum(g_c + m, m_l)
        s_prev = jnp.exp(g_c + m - m_new)
        s_loc = jnp.exp(m_l - m_new)
        C_new = s_prev[..., None, None] * C + s_loc[..., None, None] * C_l
        n_new = s_prev[..., None] * n + s_loc[..., None] * n_l
        return (C_new, n_new, m_new), (C, n, m)

    init = (jnp.zeros((B, H, dk, dv), jnp.float32),
            jnp.zeros((B, H, dk), jnp.float32),
            jnp.full((B, H), M_INIT, jnp.float32))
    xs = (jnp.moveaxis(g, 2, 0), jnp.moveaxis(m_loc, 2, 0),
          jnp.moveaxis(C_loc, 2, 0), jnp.moveaxis(n_loc, 2, 0))
    _, (C_prev, n_prev, m_prev) = lax.scan(step, init, xs)
    C_prev = jnp.moveaxis(C_prev, 0, 2)
    n_prev = jnp.moveaxis(n_prev, 0, 2)
    m_prev = jnp.moveaxis(m_prev, 0, 2)

    Dm = b[..., :, None] - b[..., None, :] + li[..., None, :]
    lower = jnp.tril(jnp.ones((L, L), dtype=bool))
    Dm = jnp.where(lower, Dm, -jnp.inf)
    inter = b + m_prev[..., None]
    m_t = jnp.maximum(inter, jnp.max(Dm, axis=-1))
    sc = jnp.einsum('bhctd,bhcsd->bhcts', qc, kc) * jnp.exp(Dm - m_t[..., None])
    w_inter = jnp.exp(inter - m_t)
    num = (jnp.einsum('bhcts,bhcse->bhcte', sc, vc)
           + w_inter[..., None] * jnp.einsum('bhctd,bhcde->bhcte', qc, C_prev))
    den = jnp.sum(sc, axis=-1) + w_inter * jnp.einsum('bhctd,bhcd->bhct', qc, n_prev)
    h = num / jnp.maximum(jnp.abs(den), jnp.exp(-m_t))[..., None]
    return h.reshape(B, H, S, dv)


def centred_depthwise_conv(x, w):
    K = w.shape[0]
    r = K // 2
    S = x.shape[1]
    xp = jnp.pad(x, ((0, 0), (r, r), (0, 0)))
    y = xp[:, 0:S] * w[0]
    for j in range(1, K):
        y = y + xp[:, j:j + S] * w[j]
    return y


def mlstm_mixer(mq, mk, mv, mo, gate_pre, conv_w, gate_bias, head_norm_g):
    B, S, _ = mq.shape
    qk = jax.nn.silu(centred_depthwise_conv(jnp.concatenate([mq, mk], axis=-1), conv_w))
    q, k = qk[..., :ML_QK_COLS], qk[..., ML_QK_COLS:]
    q = q.reshape(B, S, ML_HEADS, ML_QK_DIM).transpose(0, 2, 1, 3).astype(jnp.float32) * (ML_QK_DIM ** -0.5)
    k = k.reshape(B, S, ML_HEADS, ML_QK_DIM).transpose(0, 2, 1, 3).astype(jnp.float32)
    v = mv.reshape(B, S, ML_HEADS, ML_V_DIM).transpose(0, 2, 1, 3).astype(jnp.float32)
    gates = (gate_pre.astype(jnp.float32) + gate_bias.astype(jnp.float32)).reshape(B, S, 4, ML_HEADS)
    gates = jnp.transpose(gates, (2, 0, 3, 1))
    i_fwd, f_fwd, i_bwd, f_bwd = gates[0], gates[1], gates[2], gates[3]
    h_fwd = mlstm_chunkwise(q, k, v, i_fwd, jax.nn.log_sigmoid(f_fwd))
    flip = lambda t: jnp.flip(t, axis=2)
    h_bwd = flip(mlstm_chunkwise(flip(q), flip(k), flip(v), flip(i_bwd),
                                 jax.nn.log_sigmoid(flip(f_bwd))))
    h = (h_fwd + h_bwd).transpose(0, 2, 1, 3)
    h = h * lax.rsqrt(jnp.mean(h * h, axis=-1, keepdims=True) + NORM_EPS)
    h = h.reshape(B, S, ML_V_COLS) * head_norm_g.astype(jnp.float32)
    out = h * jax.nn.sigmoid(mo.astype(jnp.float32))
    return out.astype(mq.dtype)


def hybrid_layer(x, w_in, conv_w, gate_bias, ml_norm_g, attn_sink, w_out,
                 g_pre_mix, g_post_mix, g_pre_mlp, g_post_mlp, w_up, w_down):
    B, S, _ = x.shape
    h = rms_norm(x, g_pre_mix)
    proj = h @ w_in
    sizes = [ATT_Q_COLS, ATT_KV_COLS, ATT_KV_COLS, ML_QK_COLS, ML_QK_COLS, ML_V_COLS, ML_V_COLS]
    cuts = []
    acc = 0
    for sz in sizes:
        acc += sz
        cuts.append(acc)
    aq, ak, av, mq, mk, mv, mo, mg = jnp.split(proj, cuts, axis=-1)
    att = window_attention(aq.reshape(B, S, ATT_HEADS, ATT_HEAD_DIM),
                           ak.reshape(B, S, ATT_KV_HEADS, ATT_HEAD_DIM),
                           av.reshape(B, S, ATT_KV_HEADS, ATT_HEAD_DIM),
                           attn_sink)
    mem = mlstm_mixer(mq, mk, mv, mo, mg, conv_w, gate_bias, ml_norm_g)
    mix = jnp.concatenate([att, mem.astype(att.dtype)], axis=-1) @ w_out
    x = x + rms_norm(mix, g_post_mix)
    h = rms_norm(x, g_pre_mlp)
    u = jnp.square(jax.nn.relu(h @ w_up))
    x = x + rms_norm(u @ w_down, g_post_mlp)
    return x


def setup_inputs(seed: int = 0) -> dict:
    key = jax.random.key(seed)
    ks = jax.random.split(key, 16)
    f32 = jnp.float32
    x = jax.random.normal(ks[0], (BATCH, SEQ, D_MODEL), f32)
    w_in = jax.random.normal(ks[1], (DEPTH, D_MODEL, IN_COLS), f32) * D_MODEL ** -0.5
    conv_w = jax.random.normal(ks[2], (DEPTH, CONV_WIDTH, 2 * ML_QK_COLS), f32) * CONV_WIDTH ** -0.5
    gk = jax.random.split(ks[3], 4)
    i_bias = 0.1 * jax.random.normal(gk[0], (DEPTH, 2, ML_HEADS), f32)
    f_bias = (jnp.linspace(3.0, 6.0, ML_HEADS, dtype=f32)[None, None, :]
              + 0.1 * jax.random.normal(gk[1], (DEPTH, 2, ML_HEADS), f32))
    gate_bias = jnp.stack([i_bias[:, 0], f_bias[:, 0], i_bias[:, 1], f_bias[:, 1]], axis=1).reshape(DEPTH, GATE_COLS)
    ml_norm_g = 1.0 + 0.05 * jax.random.normal(ks[4], (DEPTH, ML_V_COLS), f32)
    attn_sink = 0.5 * jax.random.normal(ks[5], (DEPTH, ATT_HEADS), f32)
    w_out = jax.random.normal(ks[6], (DEPTH, MIX_WIDTH, D_MODEL), f32) * MIX_WIDTH ** -0.5
    g_pre_mix = 1.0 + 0.05 * jax.random.normal(ks[7], (DEPTH, D_MODEL), f32)
    g_post_mix = 1.0 + 0.05 * jax.random.normal(ks[8], (DEPTH, D_MODEL), f32)
    g_pre_mlp = 1.0 + 0.05 * jax.random.normal(ks[9], (DEPTH, D_MODEL), f32)
    g_post_mlp = 1.0 + 0.05 * jax.random.normal(ks[10], (DEPTH, D_MODEL), f32)
    w_up = jax.random.normal(ks[11], (DEPTH, D_MODEL, D_FF), f32) * D_MODEL ** -0.5
    w_down = jax.random.normal(ks[12], (DEPTH, D_FF, D_MODEL), f32) * D_FF ** -0.5
    return {"x": x, "w_in": w_in, "conv_w": conv_w, "gate_bias": gate_bias,
            "ml_norm_g": ml_norm_g, "attn_sink": attn_sink, "w_out": w_out,
            "g_pre_mix": g_pre_mix, "g_post_mix": g_post_mix,
            "g_pre_mlp": g_pre_mlp, "g_post_mlp": g_post_mlp,
            "w_up": w_up, "w_down": w_down}


def reference(x, w_in, conv_w, gate_bias, ml_norm_g, attn_sink, w_out,
              g_pre_mix, g_post_mix, g_pre_mlp, g_post_mlp, w_up, w_down):
    for l in range(DEPTH):
        x = hybrid_layer(x, w_in[l], conv_w[l], gate_bias[l], ml_norm_g[l], attn_sink[l], w_out[l],
                         g_pre_mix[l], g_post_mix[l], g_pre_mlp[l], g_post_mlp[l], w_up[l], w_down[l])
    return x
```

```python
import contextlib
import numpy as np
import ml_dtypes
import concourse.bass as bass
import concourse.mybir as mybir
from concourse.bass_utils import run_bass_kernel_spmd

F32 = mybir.dt.float32
BF16 = mybir.dt.bfloat16
AF = mybir.ActivationFunctionType
ALU = mybir.AluOpType
AX = mybir.AxisListType

D = 2048
S = 8192
B = 2
NCORE = 8
TOK = 2048
P = 128
KC = D // P
IN_COLS = 4624
DFF = 8192
EPS = 1e-6
WX_COLS = 5904

ENGS = ("sp", "act", "dve", "pool", "pe")
NSLOT = 8


class Op:
    __slots__ = ("eng", "fn", "deps", "sig", "val", "dma", "slot", "slotval", "nm")

    def __init__(self, eng, fn, dma):
        self.eng = eng
        self.fn = fn
        self.deps = []
        self.sig = False
        self.val = 0
        self.dma = dma
        self.slot = 0
        self.slotval = 0


class Prog:
    def __init__(self, nc):
        self.nc = nc
        self.ops = {e: [] for e in ENGS}
        self.res = {}
        self.stack = contextlib.ExitStack()
        self.gstack = contextlib.ExitStack()
        self.sem_eng = {e: self.gstack.enter_context(nc.semaphore("sem_" + e)) for e in ENGS}
        self.sem_dma = {e: [self.gstack.enter_context(nc.semaphore("dq_%s_%d" % (e, i))) for i in range(NSLOT)]
                        for e in ENGS if e != "pe"}
        self.cnt = {e: 0 for e in ENGS}
        self.dk = {e: 0 for e in ENGS}
        self.seen = {e: {} for e in ENGS}
        self.batch = 0
        self.uid = 0

    def sb(self, name, shape, dt):
        self.uid += 1
        return self.stack.enter_context(self.nc.sbuf_tensor("%s_%d" % (name, self.uid), list(shape), dt))

    def ps(self, name, shape, dt):
        self.uid += 1
        return self.stack.enter_context(self.nc.psum_tensor("%s_%d" % (name, self.uid), list(shape), dt))

    def add(self, eng, fn, reads=(), writes=(), dma=False):
        op = Op(eng, fn, dma)
        op.nm = self.batch
        deps = set()
        for r in reads:
            st = self.res.get(r)
            if st is not None and st[0] is not None:
                deps.add(st[0])
        for w in writes:
            st = self.res.get(w)
            if st is not None:
                if st[0] is not None:
                    deps.add(st[0])
                deps.update(st[1].values())
                deps.update(st[2])
        for r in reads:
            st = self.res.get(r)
            if st is None:
                st = [None, {}, []]
                self.res[r] = st
            if dma:
                st[2].append(op)
            else:
                st[1][eng] = op
        for w in writes:
            self.res[w] = [op, {}, []]
        deps.discard(op)
        for d in deps:
            if d.nm != self.batch:
                continue
            if d.eng == "pe" and eng == "pe" and not d.dma and not dma:
                continue
            d.sig = True
            op.deps.append(d)
        self.ops[eng].append(op)
        return op

    def op(self, eng, method, reads=(), writes=(), **kw):
        return self.add(eng, lambda e: getattr(e, method)(**kw), reads, writes)

    def dma(self, eng, out, in_, reads=(), writes=()):
        return self.add(eng, lambda e: e.dma_start(out=out, in_=in_), reads, writes, dma=True)

    def fence(self, eng, reads):
        return self.add(eng, None, reads=reads, writes=())

    def flush(self):
        nc = self.nc
        sem_eng, sem_dma = self.sem_eng, self.sem_dma
        for e in ENGS:
            last = None
            for op in self.ops[e]:
                if not op.dma and op.fn is not None:
                    last = op
            if last is not None:
                last.sig = True
            for op in self.ops[e]:
                if op.dma:
                    k = self.dk[e]
                    op.slot = k % NSLOT
                    op.slotval = 16 * (k // NSLOT + 1)
                    self.dk[e] = k + 1
                elif op.sig and op.fn is not None:
                    self.cnt[e] += 1
                    op.val = self.cnt[e]
        targets = []
        for e in ENGS:
            if self.cnt[e] > 0:
                targets.append((e, sem_eng[e], self.cnt[e]))
            if e != "pe":
                k = self.dk[e]
                for sl in range(NSLOT):
                    n_used = (k - sl + NSLOT - 1) // NSLOT if k > sl else 0
                    if n_used > 0:
                        targets.append((None, sem_dma[e][sl], 16 * n_used))

        def emit_engine(ename, eng):
            seen = self.seen[ename]
            for op in self.ops[ename]:
                waits = {}
                for d in op.deps:
                    if d.dma:
                        key = sem_dma[d.eng][d.slot]
                        v = d.slotval
                    else:
                        key = sem_eng[d.eng]
                        v = d.val
                    if waits.get(key, 0) < v:
                        waits[key] = v
                if op.dma and op.slotval > 16:
                    key = sem_dma[ename][op.slot]
                    v = op.slotval - 16
                    if waits.get(key, 0) < v:
                        waits[key] = v
                for key, v in waits.items():
                    if seen.get(key, 0) >= v:
                        continue
                    seen[key] = v
                    eng.wait_ge(key, v)
                if op.fn is None:
                    continue
                ins = op.fn(eng)
                if op.dma:
                    ins.then_inc(sem_dma[ename][op.slot], 16)
                elif op.sig:
                    ins.then_inc(sem_eng[ename], 1)
            for (te, key, v) in targets:
                if te == ename:
                    continue
                if seen.get(key, 0) >= v:
                    continue
                seen[key] = v
                eng.wait_ge(key, v)

        with nc.Block() as block:
            @block.sync
            def _(eng):
                emit_engine("sp", eng)

            @block.scalar
            def _(eng):
                emit_engine("act", eng)

            @block.vector
            def _(eng):
                emit_engine("dve", eng)

            @block.gpsimd
            def _(eng):
                emit_engine("pool", eng)

            @block.tensor
            def _(eng):
                emit_engine("pe", eng)
        self.ops = {e: [] for e in ENGS}
        self.stack.close()
        self.stack = contextlib.ExitStack()
        self.batch += 1

    def emit(self):
        self.flush()
        self.gstack.close()


class Banks:
    def __init__(self, prog, names):
        self.tiles = [(n, prog.ps(n, [P, 512], F32)) for n in names]
        self.i = 0

    def next(self):
        t = self.tiles[self.i % len(self.tiles)]
        self.i += 1
        return t


def mm_group(prog, out_ap, bank_key, pairs, extra_reads=()):
    n = len(pairs)
    last = None
    for i, (l, r, rk) in enumerate(pairs):
        def fn(e, l=l, r=r, i=i):
            return e.matmul(out_ap, l, r, start=(i == 0), stop=(i == n - 1))
        last = prog.add("pe", fn, reads=tuple(rk) + tuple(extra_reads), writes=(bank_key,))
    return last


def rmsnorm_to_featmajor(prog, pe_banks_t, x_tile, xkey, g_col, hT, hT_key, col0, ident_bf, scr, ti):
    ss, rstd, xs = scr["ss"], scr["rstd"], scr["xs"]
    sfx = ti % 2
    ssk, rsk, xsk = ("ss", sfx), ("rstd", sfx), ("xs", sfx)
    ss_c = ss[:, sfx:sfx + 1]
    rs_c = rstd[:, sfx:sfx + 1]
    xs_t = xs[:, sfx, :]
    prog.add("act", lambda e: e.activation(out=xs_t, in_=x_tile, func=AF.Square, accum_out=ss_c),
             reads=(xkey,), writes=(xsk, ssk))
    prog.add("dve", lambda e: e.tensor_scalar(out=rs_c, in0=ss_c, scalar1=1.0 / D, scalar2=EPS,
                                              op0=ALU.mult, op1=ALU.add),
             reads=(ssk,), writes=(rsk,))
    prog.add("act", lambda e: e.activation(out=rs_c, in_=rs_c, func=AF.Sqrt), reads=(rsk,), writes=(rsk,))
    prog.add("dve", lambda e: e.reciprocal(out=rs_c, in_=rs_c), reads=(rsk,), writes=(rsk,))
    prog.add("act", lambda e: e.activation(out=xs_t, in_=x_tile, func=AF.Copy, scale=rs_c),
             reads=(xkey, rsk), writes=(xsk,))
    for q in range(4):
        bname, bt = pe_banks_t.next()
        for j in range(4):
            kc = q * 4 + j
            prog.add("pe", lambda e, kc=kc, j=j, bt=bt: e.transpose(
                out=bt[:, j * P:(j + 1) * P], in_=xs_t[:, kc * P:(kc + 1) * P], identity=ident_bf[:]),
                reads=(xsk, "ident"), writes=(bname,))
        for j in range(4):
            kc = q * 4 + j
            prog.add("dve", lambda e, kc=kc, j=j, bt=bt: e.tensor_scalar(
                out=hT[:, kc, col0:col0 + P], in0=bt[:, j * P:(j + 1) * P],
                scalar1=g_col[:, kc:kc + 1], scalar2=None, op0=ALU.mult),
                reads=(bname, "gcol"), writes=(hT_key,))


def build_k1():
    nc = bass.Bass("TRN2", target_bir_lowering=False)
    d = {
        "x": nc.dram_tensor("x", [TOK, D], F32, kind="ExternalInput").ap(),
        "w": nc.dram_tensor("w_in", [D, WX_COLS], F32, kind="ExternalInput").ap(),
        "gcol": nc.dram_tensor("gcol", [P, KC], F32, kind="ExternalInput").ap(),
        "cos": nc.dram_tensor("cosT", [P, TOK], F32, kind="ExternalInput").ap(),
        "sin": nc.dram_tensor("sinT", [P, TOK], F32, kind="ExternalInput").ap(),
        "ident": nc.dram_tensor("ident", [P, P], F32, kind="ExternalInput").ap(),
        "aq": nc.dram_tensor("aq", [1024, TOK], BF16, kind="ExternalOutput").ap(),
        "ak": nc.dram_tensor("ak", [256, TOK], BF16, kind="ExternalOutput").ap(),
        "mqk": nc.dram_tensor("mqk", [1024, TOK], F32, kind="ExternalOutput").ap(),
        "av": nc.dram_tensor("av", [TOK, 256], BF16, kind="ExternalOutput").ap(),
        "mv": nc.dram_tensor("mv", [TOK, 1024], BF16, kind="ExternalOutput").ap(),
        "smo": nc.dram_tensor("smo", [TOK, 1024], F32, kind="ExternalOutput").ap(),
        "gt": nc.dram_tensor("gt", [TOK, 16], F32, kind="ExternalOutput").ap(),
    }
    pr = Prog(nc)
    emit_k1(pr, d, TOK, False)
    pr.emit()
    return nc


def emit_k1(pr, d, ntok, gt_tiled, extra_dmas=None):
    extra_dmas = list(extra_dmas or [])
    x, w, gcol_d, cos_d, sin_d, ident_d = d["x"], d["w"], d["gcol"], d["cos"], d["sin"], d["ident"]
    aq_o, ak_o, mqk_o, av_o, mv_o, smo_o, gt_o = d["aq"], d["ak"], d["mqk"], d["av"], d["mv"], d["smo"], d["gt"]
    ident_bf = pr.sb("ident_bf", [P, P], BF16)
    gcol = pr.sb("gcol_sb", [P, KC], F32)
    cst = pr.sb("cs_sb", [P, 2, 2, 512], F32)
    xt = pr.sb("xt", [P, 2, D], F32)
    scr = {
        "ss": pr.sb("ss", [P, 2], F32),
        "rstd": pr.sb("rstd", [P, 2], F32),
        "xs": pr.sb("xs", [P, 2, D], BF16),
    }
    hT = pr.sb("hT", [P, KC, 512], BF16)
    NW = 3
    wt = pr.sb("wt", [P, NW, KC, 512], BF16)
    ev32 = pr.sb("ev32", [P, 4, 512], F32)
    evbf = pr.sb("evbf", [P, 4, 512], BF16)
    t1 = pr.sb("t1", [P, 2, 512], F32)
    banks = Banks(pr, ["pb%d" % i for i in range(6)])
    tb_t = [("pt%d" % i, pr.ps("pt%d" % i, [P, 1024], BF16)) for i in range(2)]

    class TB:
        i = 0

        def next(self):
            t = tb_t[self.i % 2]
            self.i += 1
            return t
    tbanks = TB()

    pr.dma("pool", ident_bf[:], ident_d, writes=("ident",))
    pr.dma("sp", gcol[:], gcol_d, writes=("gcol",))

    w_r = w.rearrange("(kc p) c -> p kc c", p=P)
    wcount = [0]
    evc = [0]
    outs = []

    def load_w(c0, n):
        slot = wcount[0] % NW
        wcount[0] += 1
        key = ("wt", slot)
        pr.dma("pool", wt[:, slot, :, 0:n], w_r[:, :, c0:c0 + n], writes=(key,))
        if extra_dmas:
            dst, src, k = extra_dmas.pop(0)
            pr.dma("pool", dst, src, writes=(k,))
        return slot, key

    def ev_slot():
        s = evc[0] % 4
        evc[0] += 1
        return s

    for tb in range(ntok // 512):
        t0 = tb * 512
        csl = tb % 2
        pr.dma("sp", cst[:, csl, 0, :], cos_d[:, t0:t0 + 512], writes=(("cos", csl),))
        pr.dma("sp", cst[:, csl, 1, :], sin_d[:, t0:t0 + 512], writes=(("sin", csl),))
        for ti in range(4):
            g_ti = tb * 4 + ti
            xs_ = g_ti % 2
            xkey = ("xt", xs_)
            pr.dma("sp", xt[:, xs_, :], x[g_ti * P:(g_ti + 1) * P, :], writes=(xkey,))
            rmsnorm_to_featmajor(pr, tbanks, xt[:, xs_, :], xkey, gcol, hT, ("hT", ti), ti * P,
                                 ident_bf, scr, g_ti)
        hkeys = tuple(("hT", ti) for ti in range(4))

        for hh in range(10):
            slot, wkey = load_w(hh * 256, 256)
            (na, ba), (nb, bb) = banks.next(), banks.next()
            mm_group(pr, ba[:], na, [(wt[:, slot, kc, 0:128], hT[:, kc, :], hkeys + (wkey,)) for kc in range(KC)])
            mm_group(pr, bb[:], nb, [(wt[:, slot, kc, 128:256], hT[:, kc, :], hkeys + (wkey,))
                                     for kc in range(KC)])
            es = ev_slot()
            cs_ap = cst[:, csl, 0, :]
            sn_ap = cst[:, csl, 1, :]
            pr.add("dve", lambda e, ba=ba, cs_ap=cs_ap: e.tensor_tensor(out=t1[:, 0, :], in0=ba[:], in1=cs_ap,
                                                                        op=ALU.mult),
                   reads=(na, ("cos", csl)), writes=("t1a",))
            pr.add("dve", lambda e, bb=bb, sn_ap=sn_ap: e.tensor_tensor(out=t1[:, 1, :], in0=bb[:], in1=sn_ap,
                                                                        op=ALU.mult),
                   reads=(nb, ("sin", csl)), writes=("t1b",))
            pr.add("dve", lambda e, es=es: e.tensor_tensor(out=evbf[:, es, :], in0=t1[:, 0, :], in1=t1[:, 1, :],
                                                            op=ALU.add),
                   reads=("t1a", "t1b"), writes=(("evbf", es),))
            dst = aq_o[hh * P:(hh + 1) * P, t0:t0 + 512] if hh < 8 else ak_o[(hh - 8) * P:(hh - 7) * P, t0:t0 + 512]
            outs.append(pr.dma("sp", dst, evbf[:, es, :], reads=(("evbf", es),), writes=(("out", len(outs)),)))

        for grp in range(2):
            slot, wkey = load_w(2560 + grp * 512, 512)
            for j in range(4):
                nb_, bt = banks.next()
                mm_group(pr, bt[:], nb_, [(wt[:, slot, kc, j * P:(j + 1) * P], hT[:, kc, :], hkeys + (wkey,))
                                          for kc in range(KC)])
                es = ev_slot()
                pr.add("act", lambda e, bt=bt, es=es: e.copy(out=ev32[:, es, :], in_=bt[:]),
                       reads=(nb_,), writes=(("ev32", es),))
                r0 = grp * 512 + j * P
                outs.append(pr.dma("sp", mqk_o[r0:r0 + P, t0:t0 + 512], ev32[:, es, :],
                                   reads=(("ev32", es),), writes=(("out", len(outs)),)))

        tm_groups = [
            (3584, 272, "avg"),
            (3856, 512, "mv0"), (4368, 512, "mv1"),
            (4880, 512, "mo0"), (5392, 512, "mo1"),
        ]
        for c0, ncols, kind in tm_groups:
            slot, wkey = load_w(c0, ncols)
            for ti in range(4):
                r0 = t0 + ti * P
                nb_, bt = banks.next()
                mm_group(pr, bt[:, 0:ncols], nb_,
                         [(hT[:, kc, ti * P:(ti + 1) * P], wt[:, slot, kc, 0:ncols], (("hT", ti), wkey))
                          for kc in range(KC)])
                es = ev_slot()
                if kind == "avg":
                    pr.add("act", lambda e, bt=bt, es=es: e.copy(out=evbf[:, es, 0:256], in_=bt[:, 0:256]),
                           reads=(nb_,), writes=(("evbf", es),))
                    pr.add("act", lambda e, bt=bt, es=es: e.copy(out=ev32[:, es, 0:16], in_=bt[:, 256:272]),
                           reads=(nb_,), writes=(("ev32", es),))
                    outs.append(pr.dma("sp", av_o[r0:r0 + P, :], evbf[:, es, 0:256],
                                       reads=(("evbf", es),), writes=(("out", len(outs)),)))
                    gt_dst = gt_o[:, r0 // P, :] if gt_tiled else gt_o[r0:r0 + P, :]
                    outs.append(pr.dma("sp", gt_dst, ev32[:, es, 0:16],
                                       reads=(("ev32", es),), writes=(("out", len(outs)),)))
                elif kind.startswith("mv"):
                    c = int(kind[2]) * 512
                    pr.add("act", lambda e, bt=bt, es=es: e.copy(out=evbf[:, es, :], in_=bt[:]),
                           reads=(nb_,), writes=(("evbf", es),))
                    outs.append(pr.dma("sp", mv_o[r0:r0 + P, c:c + 512], evbf[:, es, :],
                                       reads=(("evbf", es),), writes=(("out", len(outs)),)))
                else:
                    c = int(kind[2]) * 512
                    pr.add("act", lambda e, bt=bt, es=es: e.activation(out=ev32[:, es, :], in_=bt[:], func=AF.Sigmoid),
                           reads=(nb_,), writes=(("ev32", es),))
                    outs.append(pr.dma("sp", smo_o[r0:r0 + P, c:c + 512], ev32[:, es, :],
                                       reads=(("ev32", es),), writes=(("out", len(outs)),)))

    for dst, src, k in extra_dmas:
        pr.dma("pool", dst, src, writes=(k,))
    pr.flush()


def rstd_chain(pr, ss_c, ssk, rs_c, rsk, n):
    pr.add("dve", lambda e: e.tensor_scalar(out=rs_c, in0=ss_c, scalar1=1.0 / n, scalar2=EPS,
                                            op0=ALU.mult, op1=ALU.add), reads=(ssk,), writes=(rsk,))
    pr.add("act", lambda e: e.activation(out=rs_c, in_=rs_c, func=AF.Sqrt), reads=(rsk,), writes=(rsk,))
    pr.add("dve", lambda e: e.reciprocal(out=rs_c, in_=rs_c), reads=(rsk,), writes=(rsk,))


def build_k3():
    nc = bass.Bass("TRN2", target_bir_lowering=False)
    d = {
        "mixT": nc.dram_tensor("mixT", [D, TOK], BF16, kind="ExternalInput").ap(),
        "x": nc.dram_tensor("x", [TOK, D], F32, kind="ExternalInput").ap(),
        "w_out": nc.dram_tensor("w_out", [D, D], F32, kind="ExternalInput").ap(),
        "w_up": nc.dram_tensor("w_up", [D, DFF], F32, kind="ExternalInput").ap(),
        "w_down": nc.dram_tensor("w_down", [DFF, D], F32, kind="ExternalInput").ap(),
        "g_pm": nc.dram_tensor("g_pm", [P, D], F32, kind="ExternalInput").ap(),
        "g_pl": nc.dram_tensor("g_pl", [P, D], F32, kind="ExternalInput").ap(),
        "gcol": nc.dram_tensor("gcol", [P, KC], F32, kind="ExternalInput").ap(),
        "ident": nc.dram_tensor("ident", [P, P], F32, kind="ExternalInput").ap(),
        "y": nc.dram_tensor("y", [TOK, D], F32, kind="ExternalOutput").ap(),
    }
    mixT_r = d["mixT"].rearrange("(kc p) t -> p kc t", p=P)
    d["mix_blk"] = lambda tb: mixT_r[:, :, tb * 512:(tb + 1) * 512]
    pr = Prog(nc)
    emit_k3(pr, d, TOK)
    pr.emit()
    return nc


def emit_k3(pr, d, ntok):
    x, w_out, w_up, w_down = d["x"], d["w_out"], d["w_up"], d["w_down"]
    gat = d.get("gather")
    gpm_d, gpl_d, gcol_d, ident_d, y_o = d["g_pm"], d["g_pl"], d["gcol"], d["ident"], d["y"]
    ident_bf = pr.sb("ident_bf", [P, P], BF16)
    gcol = pr.sb("gcol_sb", [P, KC], F32)
    gpm = pr.sb("gpm_sb", [P, D], F32)
    gpl = pr.sb("gpl_sb", [P, D], F32)
    scr = {
        "ss": pr.sb("ss", [P, 2], F32),
        "rstd": pr.sb("rstd", [P, 2], F32),
        "xs": pr.sb("xs", [P, 2, D], BF16),
    }
    ss4 = pr.sb("ss4", [P, 4, 4], F32)
    ssr = pr.sb("ssr", [P, 4], F32)
    rs2 = pr.sb("rs2", [P, 4], F32)
    mixb = pr.sb("mixb", [P, KC, 512], BF16)
    h2T = pr.sb("h2T", [P, KC, 512], BF16)
    NW = 3
    wt = pr.sb("wt", [P, NW, KC, 512], BF16)
    x1b = pr.sb("x1b", [P, 4, D], F32)
    yb = pr.sb("yb", [P, 4, D], F32)
    uT = pr.sb("uT", [P, 32, 512], BF16)
    r32 = pr.sb("r32", [P, 2, 512], F32)
    pa = Banks(pr, ["pa0", "pa1"])
    pd = [("pd%d" % i, pr.ps("pd%d" % i, [P, 512], F32)) for i in range(4)]
    tb_t = [("pt%d" % i, pr.ps("pt%d" % i, [P, 1024], BF16)) for i in range(2)]

    class TB:
        i = 0

        def next(self):
            t = tb_t[self.i % 2]
            self.i += 1
            return t
    tbanks = TB()

    pr.dma("pool", ident_bf[:], ident_d, writes=("ident",))
    pr.dma("sp", gcol[:], gcol_d, writes=("gcol",))
    pr.dma("sp", gpm[:], gpm_d, writes=("gpm",))
    pr.dma("sp", gpl[:], gpl_d, writes=("gpl",))

    w_out_r = w_out.rearrange("(kc p) c -> p kc c", p=P)
    w_up_r = w_up.rearrange("(kc p) c -> p kc c", p=P)
    w_down_r = w_down.rearrange("(fc p) c -> p fc c", p=P)
    mixkeys = tuple(("mixb", kc) for kc in range(KC))
    if gat is not None:
        I32 = mybir.dt.int32
        xidx = pr.sb("xidx", [P, ntok // P], I32)
        midx = pr.sb("midx", [P, (ntok // 512) * KC], I32)
        pr.dma("sp", xidx[:], gat["xidx"], writes=("xidx",))
        pr.dma("sp", midx[:], gat["midx"], writes=("midx",))
    wcount = [0]
    rc = [0]
    outs = []

    def load_w(src):
        slot = wcount[0] % NW
        wcount[0] += 1
        key = ("wt", slot)
        pr.dma("pool", wt[:, slot, :, :], src, writes=(key,))
        return slot, key

    def r32_slot():
        s = rc[0] % 2
        rc[0] += 1
        return s

    def sumsq(src_ap, src_key, acc_ap, acc_key):
        rs = r32_slot()
        pr.add("act", lambda e: e.activation(out=r32[:, rs, :], in_=src_ap, func=AF.Square, accum_out=acc_ap),
               reads=(src_key,), writes=(("r32", rs), acc_key))

    def norm_residual(ti, g_sb, gkey, base_ap, base_key, dst_ap, dst_key):
        ss_c = ssr[:, ti:ti + 1]
        rs_c = rs2[:, ti:ti + 1]
        ybk = [("yb", ti, cb) for cb in range(4)]
        pr.add("dve", lambda e: e.reduce_sum(out=ss_c, in_=ss4[:, ti, :], axis=AX.X),
               reads=tuple(("ss4", ti, cb) for cb in range(4)), writes=(("ssr", ti),))
        rstd_chain(pr, ss_c, ("ssr", ti), rs_c, ("rs2", ti), D)
        pr.add("dve", lambda e: e.scalar_tensor_tensor(out=yb[:, ti, :], in0=yb[:, ti, :], scalar=rs_c, in1=g_sb[:],
                                                       op0=ALU.mult, op1=ALU.mult),
               reads=tuple(ybk) + (("rs2", ti), gkey), writes=tuple(ybk))
        pr.add("dve", lambda e: e.tensor_tensor(out=dst_ap, in0=base_ap, in1=yb[:, ti, :], op=ALU.add),
               reads=tuple(ybk) + (base_key,), writes=(dst_key,))

    for tb in range(ntok // 512):
        t0 = tb * 512
        if gat is None:
            pr.dma("sp", mixb[:], d["mix_blk"](tb), writes=mixkeys)
            for ti in range(4):
                pr.dma("sp", x1b[:, ti, :], x[t0 + ti * P:t0 + (ti + 1) * P, :], writes=(("x1b", ti),))
        else:
            for kc in range(KC):
                pr.add("pool", lambda e, kc=kc, col=tb * KC + kc: e.indirect_dma_start(
                    out=mixb[:, kc, :], out_offset=None, in_=gat["mix_flat"],
                    in_offset=bass.IndirectOffsetOnAxis(midx[:, col:col + 1], 0)),
                    reads=("midx",), writes=(("mixb", kc),), dma=True)
            for ti in range(4):
                pr.add("pool", lambda e, ti=ti, col=tb * 4 + ti: e.indirect_dma_start(
                    out=x1b[:, ti, :], out_offset=None, in_=x,
                    in_offset=bass.IndirectOffsetOnAxis(xidx[:, col:col + 1], 0)),
                    reads=("xidx",), writes=(("x1b", ti),), dma=True)
        for cb in range(4):
            slot, wkey = load_w(w_out_r[:, :, cb * 512:(cb + 1) * 512])
            for ti in range(4):
                nb_, bt = pa.next()
                mm_group(pr, bt[:], nb_, [(mixb[:, kc, ti * P:(ti + 1) * P], wt[:, slot, kc, :], (("mixb", kc), wkey))
                                          for kc in range(KC)])
                ypiece = yb[:, ti, cb * 512:(cb + 1) * 512]
                pr.add("act", lambda e, bt=bt, ypiece=ypiece: e.copy(out=ypiece, in_=bt[:]),
                       reads=(nb_,), writes=(("yb", ti, cb),))
                sumsq(ypiece, ("yb", ti, cb), ss4[:, ti, cb:cb + 1], ("ss4", ti, cb))
        for ti in range(4):
            norm_residual(ti, gpm, "gpm", x1b[:, ti, :], ("x1b", ti), x1b[:, ti, :], ("x1b", ti))
            rmsnorm_to_featmajor(pr, tbanks, x1b[:, ti, :], ("x1b", ti), gcol, h2T, ("h2T", ti), ti * P,
                                 ident_bf, scr, ti)
        hkeys = tuple(("h2T", ti) for ti in range(4))

        for hf in range(2):
            for fgl in range(8):
                fg = hf * 8 + fgl
                slot, wkey = load_w(w_up_r[:, :, fg * 512:(fg + 1) * 512])
                for j in range(4):
                    fcl = fgl * 4 + j
                    nb_, bt = pa.next()
                    mm_group(pr, bt[:], nb_, [(wt[:, slot, kc, j * P:(j + 1) * P], h2T[:, kc, :], hkeys + (wkey,))
                                              for kc in range(KC)])
                    rs = r32_slot()
                    pr.add("act", lambda e, bt=bt, rs=rs: e.activation(out=r32[:, rs, :], in_=bt[:], func=AF.Relu),
                           reads=(nb_,), writes=(("r32", rs),))
                    sq_eng = "dve"
                    pr.add(sq_eng, lambda e, rs=rs, fcl=fcl: e.tensor_tensor(out=uT[:, fcl, :], in0=r32[:, rs, :],
                                                                            in1=r32[:, rs, :], op=ALU.mult),
                           reads=(("r32", rs),), writes=(("uT", fcl),))
            for cb in range(4):
                for qi in range(2):
                    q = hf * 2 + qi
                    slot, wkey = load_w(w_down_r[:, q * 16:(q + 1) * 16, cb * 512:(cb + 1) * 512])
                    for ti in range(4):
                        pname, pt = pd[ti]
                        for f16 in range(16):
                            fcl = qi * 16 + f16
                            pr.add("pe", lambda e, pt=pt, fcl=fcl, f16=f16, ti=ti, slot=slot, qi=qi: e.matmul(
                                pt[:], uT[:, fcl, ti * P:(ti + 1) * P], wt[:, slot, f16, :],
                                start=(qi == 0 and f16 == 0), stop=(qi == 1 and f16 == 15)),
                                reads=(("uT", fcl), wkey), writes=(pname,))
                for ti in range(4):
                    pname, pt = pd[ti]
                    ypiece = yb[:, ti, cb * 512:(cb + 1) * 512]
                    if hf == 0:
                        pr.add("act", lambda e, pt=pt, ypiece=ypiece: e.copy(out=ypiece, in_=pt[:]),
                               reads=(pname,), writes=(("yb", ti, cb),))
                    else:
                        pr.add("dve", lambda e, pt=pt, ypiece=ypiece: e.tensor_tensor(out=ypiece, in0=ypiece, in1=pt[:],
                                                                                      op=ALU.add),
                               reads=(pname, ("yb", ti, cb)), writes=(("yb", ti, cb),))
                        sumsq(ypiece, ("yb", ti, cb), ss4[:, ti, cb:cb + 1], ("ss4", ti, cb))
        for ti in range(4):
            norm_residual(ti, gpl, "gpl", x1b[:, ti, :], ("x1b", ti), yb[:, ti, :], ("ybo", ti))
            outs.append(pr.dma("sp", y_o[t0 + ti * P:t0 + (ti + 1) * P, :], yb[:, ti, :],
                               reads=(("ybo", ti),) + tuple(("yb", ti, cb) for cb in range(4)),
                               writes=(("out", len(outs)),)))

    pr.flush()


def run_k3(mixT_cores, x_flat, w_out_l, w_up_l, w_down_l, g_pm, g_pre_mlp, g_pl):
    nc = _get("k3", build_k3)
    gcol = np.ascontiguousarray(g_pre_mlp.reshape(KC, P).T)
    gpm = np.ascontiguousarray(np.broadcast_to(g_pm[None, :], (P, D)))
    gpl = np.ascontiguousarray(np.broadcast_to(g_pl[None, :], (P, D)))
    ident = np.eye(P, dtype=np.float32)
    in_maps = []
    for c in range(NCORE):
        in_maps.append({
            "mixT": mixT_cores[c],
            "x": np.ascontiguousarray(x_flat[c * TOK:(c + 1) * TOK]),
            "w_out": w_out_l, "w_up": w_up_l, "w_down": w_down_l,
            "g_pm": gpm, "g_pl": gpl, "gcol": gcol, "ident": ident,
        })
    res = run_bass_kernel_spmd(nc, in_maps, core_ids=list(range(NCORE)), **_RUNKW)
    _LAST["t"] = res.exec_time_ns
    return [r["y"] for r in res.results]


NCH = S // P
ATT_SCALE = float(128 ** -0.5)
QK_SCALE = float(128 ** -0.5)


def build_k2():
    nc = bass.Bass("TRN2", target_bir_lowering=False)

    def din(name, shape, dt):
        return nc.dram_tensor(name, list(shape), dt, kind="ExternalInput").ap()
    v = {
        "aq4": din("aq2", [P, NCH * 256], BF16).rearrange("p (n h t) -> p n h t", h=2, t=P),
        "ak": din("ak", [P, S], BF16),
        "av3": din("av3", [P, NCH * P], BF16).rearrange("p (n d) -> p n d", d=P),
        "mq": din("mq", [P, S], F32),
        "mk": din("mk", [P, S], F32),
        "mv3": din("mv3", [P, NCH * 256], BF16).rearrange("p (n e) -> p n e", e=256),
        "smo3": din("smo3", [P, NCH * 256], F32).rearrange("p (n e) -> p n e", e=256),
        "g4": din("g4", [P, 4 * NCH], F32).rearrange("p (g n) -> p g n", g=4),
        "gb4": din("gb4", [P, 4], F32),
        "cw": din("cw", [P, 6], F32),
        "gn": din("gn", [P, 256], F32),
        "sink2": din("sink2", [P, 256], F32),
        "identf": din("identf", [P, P], F32),
        "le": din("le", [P, P], F32),
        "ge": din("ge", [P, P], F32),
        "nmp": din("nmp", [P, 256], F32),
        "nmn": din("nmn", [P, 256], F32),
        "attT_r": nc.dram_tensor("attT", [256, S], BF16, kind="ExternalOutput").ap().rearrange("(h d) t -> d h t", h=2),
        "memT_r": nc.dram_tensor("memT", [256, S], BF16, kind="ExternalOutput").ap().rearrange("(h e) t -> e h t", h=2),
        "hb": nc.dram_tensor("hb_scratch", [NCH, P, 256], F32).ap(),
        "hfb": nc.dram_tensor("hfb_scratch", [2, NCH, P, 256], F32).ap(),
    }
    v["att_dst"] = lambda n: v["attT_r"][:, :, n * P:(n + 1) * P]
    v["mem_dst"] = lambda n: v["memT_r"][:, :, n * P:(n + 1) * P]
    pr = Prog(nc)
    emit_k2(pr, v, None)
    pr.emit()
    return nc


def emit_k2_v1(pr, v, gate_j):
    ak_d, mq_d, mk_d = v["ak"], v["mq"], v["mk"]
    gb4_d, cw_d, gn_d, sink_d = v["gb4"], v["cw"], v["gn"], v["sink2"]
    identf_d, le_d, ge_d, nmp_d, nmn_d = v["identf"], v["le"], v["ge"], v["nmp"], v["nmn"]
    hb_d = v["hb"]

    def sb(name, shape, dt):
        return pr.sb(name + "_s", shape, dt)
    nc = pr.nc
    identf = sb("identf", [P, P], F32)
    ident_bf = sb("ident_bf", [P, P], BF16)
    onesf = sb("onesf", [P, P], F32)
    ones_bf = sb("ones_bf", [P, P], BF16)
    le = sb("le", [P, P], F32)
    ge = sb("ge", [P, P], F32)
    nmp = sb("nmp", [P, 256], BF16)
    nmn = sb("nmn", [P, 256], BF16)
    gb4 = sb("gb4", [P, 4], F32)
    negb = sb("negb", [P, 4], F32)
    cw = sb("cw", [P, 6], F32)
    gn = sb("gn", [P, 256], F32)
    esink = sb("esink", [P, 256], F32)
    g4 = sb("g4", [P, 4, NCH], F32)
    li = sb("li", [P, 2, NCH], F32)
    lf = sb("lf", [P, 2, NCH], F32)
    cg = sb("cg", [P, 4, NCH], F32)
    ecol = sb("ecol", [P, 2, NCH], F32)
    wend = sb("wend", [P, 2, NCH], F32)
    eg = sb("eg", [P, 2, NCH], F32)
    gtmp = sb("gtmp", [P, 2, NCH], F32)
    ak = sb("ak", [P, S], BF16)
    av3 = sb("av3", [P, NCH, P], BF16)
    kT = sb("kT", [P, S], BF16)
    qs = [sb("qs_f", [P, S], BF16), sb("qs_b", [P, S], BF16)]
    kw = [sb("kw_f", [P, NCH, P], BF16), sb("kw_b", [P, NCH, P], BF16)]
    vext = sb("vext", [P, NCH, 258], BF16)
    PSZ = 1024
    NPC = S // PSZ
    xq = sb("xq", [P, PSZ + 2], F32)
    yq = sb("yq", [P, PSZ], F32)
    xk = sb("xk", [P, PSZ + 2], F32)
    yk = sb("yk", [P, PSZ], F32)
    lfbc = sb("lfbc", [P, 4, P], F32)
    eb = sb("eb", [P, 2, 512], F32)
    cst = [sb("C_f", [P, 257], F32), sb("C_b", [P, 257], F32)]
    cbf = [sb("Cbf_f", [P, 257], BF16), sb("Cbf_b", [P, 257], BF16)]
    ws = sb("ws", [P, 2, P], BF16)
    rr = sb("rr", [P, 4], F32)
    hbt = sb("hbt", [P, 2, 256], F32)
    hsum = sb("hsum", [P, 2, 256], F32)
    smo_t = sb("smo_t", [P, 2, 256], F32)
    obf = sb("obf", [P, 2, 256], BF16)
    junk = sb("junk", [P, 256], F32)
    ssn = sb("ssn", [P, 2], F32)
    rsn = sb("rsn", [P, 2], F32)
    mst = sb("mst", [P, 2, 256], BF16)
    qblk = sb("qblk", [P, 2, 256], BF16)
    pT_sb = sb("pT_sb", [P, 2, 3, 256], BF16)
    dent = sb("dent", [P, 2, 256], F32)
    ast = sb("ast", [P, 2, 256], BF16)

    pS = pr.ps("pS", [P, 512], F32)
    pH = [pr.ps("pH0", [P, 512], F32), pr.ps("pH1", [P, 512], F32)]
    pC = pr.ps("pC", [P, 512], F32)
    pT = pr.ps("pT", [P, 1024], BF16)
    pA = [pr.ps("pA0", [P, 512], F32), pr.ps("pA1", [P, 512], F32)]
    pO = pr.ps("pO", [P, 512], F32)

    pr.dma("sp", identf[:], identf_d, writes=("identf",))
    pr.dma("pool", ident_bf[:], identf_d, writes=("ident",))
    pr.dma("sp", le[:], le_d, writes=("le",))
    pr.dma("sp", ge[:], ge_d, writes=("ge",))
    pr.dma("pool", nmp[:], nmp_d, writes=("nmp",))
    pr.dma("pool", nmn[:], nmn_d, writes=("nmn",))
    pr.dma("sp", gb4[:], gb4_d, writes=("gb4",))
    pr.dma("sp", cw[:], cw_d, writes=("cw",))
    pr.dma("sp", gn[:], gn_d, writes=("gn",))
    pr.dma("sp", esink[:], sink_d, writes=("esink",))
    if gate_j is None:
        pr.dma("sp", g4[:], v["g4"], writes=("g4",))
    else:
        gall = sb("gall", [P, NCH, 16], F32)
        pr.dma("sp", gall[:], v["gall"], writes=("gall",))
        for g in range(4):
            pr.op("dve", "tensor_copy", reads=("gall",), writes=("g4",), out=g4[:, g, :],
                  in_=gall[:, :, g * 4 + gate_j])
    pr.dma("sp", ak[:], ak_d, writes=("ak",))
    pr.dma("sp", av3[:], v["av3"], writes=("av3",))
    pr.dma("sp", vext[:, :, 0:256], v["mv3"], writes=("vext",))
    pr.op("dve", "memset", writes=("vext1",), ap=vext[:, :, 256:257], constant=1.0)
    pr.op("dve", "memset", writes=("onesf",), ap=onesf[:], constant=1.0)
    pr.op("dve", "memset", writes=("ones_bf",), ap=ones_bf[:], constant=1.0)
    for di in range(2):
        pr.op("dve", "memset", writes=(("C", di),), ap=cst[di][:], constant=0.0)
        pr.op("dve", "memset", writes=(("Cbf", di),), ap=cbf[di][:], constant=0.0)
    pr.op("act", "activation", reads=("esink",), writes=("esink",), out=esink[:], in_=esink[:], func=AF.Exp)

    pr.op("dve", "tensor_scalar", reads=("gb4",), writes=("negb",), out=negb[:], in0=gb4[:], scalar1=-1.0,
          scalar2=None, op0=ALU.mult)
    for di in range(2):
        pr.op("dve", "tensor_scalar", reads=("g4", "gb4"), writes=(("li", di),), out=li[:, di, :],
              in0=g4[:, 2 * di, :], scalar1=gb4[:, 2 * di:2 * di + 1], scalar2=None, op0=ALU.add)
        pr.op("act", "activation", reads=("g4", "negb"), writes=(("gtmp", di),), out=gtmp[:, di, :],
              in_=g4[:, 2 * di + 1, :], func=AF.Exp, scale=-1.0, bias=negb[:, 2 * di + 1:2 * di + 2])
        pr.op("act", "activation", reads=(("gtmp", di),), writes=(("gtmp", di),), out=gtmp[:, di, :],
              in_=gtmp[:, di, :], func=AF.Ln, bias=1.0)
        pr.op("dve", "tensor_scalar", reads=(("gtmp", di),), writes=(("lf", di),), out=lf[:, di, :],
              in0=gtmp[:, di, :], scalar1=-1.0, scalar2=None, op0=ALU.mult)
    for di in range(2):
        tri = le if di == 0 else ge
        trik = "le" if di == 0 else "ge"
        pr.op("pe", "matmul", reads=(("lf", di), trik), writes=("pA0",), out=pA[0][:, di * 128:di * 128 + 64],
              lhsT=tri[:], rhs=lf[:, di, :], start=True, stop=True)
        pr.op("pe", "matmul", reads=(("lf", di), "onesf"), writes=("pA0",), out=pA[0][:, di * 128 + 64:di * 128 + 128],
              lhsT=onesf[:], rhs=lf[:, di, :], start=True, stop=True)
    pr.op("act", "copy", reads=("pA0",), writes=("cg",), out=cg[:].rearrange("p g n -> p (g n)"), in_=pA[0][:, 0:256])
    for di in range(2):
        bc = cg[:, 2 * di, :]
        gt_ = cg[:, 2 * di + 1, :]
        pr.op("dve", "tensor_tensor", reads=(("li", di), "cg"), writes=(("gtmp", di),), out=gtmp[:, di, :],
              in0=li[:, di, :], in1=bc, op=ALU.subtract)
        pr.op("act", "activation", reads=(("gtmp", di),), writes=(("ecol", di),), out=ecol[:, di, :],
              in_=gtmp[:, di, :], func=AF.Exp)
        pr.op("dve", "tensor_tensor", reads=(("gtmp", di), "cg"), writes=(("gtmp", di),), out=gtmp[:, di, :],
              in0=gtmp[:, di, :], in1=gt_, op=ALU.add)
        pr.op("act", "activation", reads=(("gtmp", di),), writes=(("wend", di),), out=wend[:, di, :],
              in_=gtmp[:, di, :], func=AF.Exp)
        pr.op("act", "activation", reads=("cg",), writes=(("eg", di),), out=eg[:, di, :], in_=gt_, func=AF.Exp)

    def conv_piece(src_d, xb, xkey, yb_, ykey, w0, pc):
        p0 = pc * PSZ
        lo = p0 - 1 if pc > 0 else 0
        hi = p0 + PSZ + 1 if pc < NPC - 1 else S
        c_lo = 0 if pc > 0 else 1
        if pc == 0:
            pr.op("dve", "memset", writes=(xkey,), ap=xb[:, 0:1], constant=0.0)
        if pc == NPC - 1:
            pr.op("dve", "memset", writes=(xkey,), ap=xb[:, PSZ + 1:PSZ + 2], constant=0.0)
        pr.dma("sp", xb[:, c_lo:c_lo + (hi - lo)], src_d[:, lo:hi], writes=(xkey,))
        pr.op("dve", "tensor_scalar", reads=(xkey, "cw"), writes=(ykey,), out=yb_[:], in0=xb[:, 1:PSZ + 1],
              scalar1=cw[:, w0 + 1:w0 + 2], scalar2=None, op0=ALU.mult)
        pr.op("dve", "scalar_tensor_tensor", reads=(xkey, "cw", ykey), writes=(ykey,), out=yb_[:], in0=xb[:, 0:PSZ],
              scalar=cw[:, w0:w0 + 1], in1=yb_[:], op0=ALU.mult, op1=ALU.add)
        pr.op("dve", "scalar_tensor_tensor", reads=(xkey, "cw", ykey), writes=(ykey,), out=yb_[:], in0=xb[:, 2:PSZ + 2],
              scalar=cw[:, w0 + 2:w0 + 3], in1=yb_[:], op0=ALU.mult, op1=ALU.add)
        pr.op("act", "activation", reads=(ykey,), writes=(ykey,), out=yb_[:], in_=yb_[:], func=AF.Silu)

    ebc = [0]
    for pc in range(NPC):
        p0 = pc * PSZ
        conv_piece(mq_d, xq, "xq", yq, "yq", 0, pc)
        conv_piece(mk_d, xk, "xk", yk, "yk", 3, pc)
        pr.op("act", "copy", reads=("yk",), writes=("kT",), out=kT[:, p0:p0 + PSZ], in_=yk[:])
        for sp_ in range(PSZ // 512):
            for di in range(2):
                tri = le if di == 0 else ge
                trik = "le" if di == 0 else "ge"
                bank = pA[ebc[0] % 2]
                bkey = "pA%d" % (ebc[0] % 2)
                es = ebc[0] % 2
                ebc[0] += 1
                for c4 in range(4):
                    n = pc * (PSZ // P) + sp_ * 4 + c4
                    pr.op("dve", "tensor_scalar", reads=("onesf", ("lf", di)), writes=(("lfbc", c4),), out=lfbc[:, c4, :],
                          in0=onesf[:], scalar1=lf[:, di, n:n + 1], scalar2=None, op0=ALU.mult)
                    pr.op("pe", "matmul", reads=(("lfbc", c4), trik), writes=(bkey,), out=bank[:, c4 * P:(c4 + 1) * P],
                          lhsT=lfbc[:, c4, :], rhs=tri[:], start=True, stop=True)
                pr.op("act", "activation", reads=(bkey,), writes=(("eb", es),), out=eb[:, es, :], in_=bank[:], func=AF.Exp)
                t0 = p0 + sp_ * 512
                pr.op("dve", "scalar_tensor_tensor", reads=("yq", ("eb", es)), writes=(("qs", di),),
                      out=qs[di][:, t0:t0 + 512], in0=yq[:, sp_ * 512:(sp_ + 1) * 512], scalar=QK_SCALE, in1=eb[:, es, :],
                      op0=ALU.mult, op1=ALU.mult)
        for c16 in range(PSZ // P):
            n = pc * (PSZ // P) + c16
            pr.op("pe", "transpose", reads=("yk", "identf"), writes=("pO",), out=pO[:, 0:P],
                  in_=yk[:, c16 * P:(c16 + 1) * P], identity=identf[:])
            for di in range(2):
                pr.op("act", "activation", reads=("pO", ("wend", di)), writes=(("kw", di),), out=kw[di][:, n, :],
                      in_=pO[:, 0:P], func=AF.Copy, scale=wend[:, di, n:n + 1])

    outs = []

    def mlstm_common(di, n):
        tok = slice(n * P, (n + 1) * P)
        msk = le if di == 0 else ge
        mskk = "le" if di == 0 else "ge"
        bank = pH[n % 2]
        bkey = "pH%d" % (n % 2)
        pr.op("pe", "matmul", reads=("kT", ("qs", di)), writes=("pS",), out=pS[:, 0:P], lhsT=kT[:, tok],
              rhs=qs[di][:, tok], start=True, stop=True)
        pr.op("dve", "scalar_tensor_tensor", reads=("pS", ("ecol", di), mskk), writes=(("ws", di),), out=ws[:, di, :],
              in0=pS[:, 0:P], scalar=ecol[:, di, n:n + 1], in1=msk[:], op0=ALU.mult, op1=ALU.mult)
        pr.op("pe", "matmul", reads=(("ws", di), "vext", "vext1"), writes=(bkey,), out=bank[:, 0:257], lhsT=ws[:, di, :],
              rhs=vext[:, n, 0:257], start=True, stop=False)
        pr.op("pe", "matmul", reads=(("qs", di), ("Cbf", di)), writes=(bkey,), out=bank[:, 0:257], lhsT=qs[di][:, tok],
              rhs=cbf[di][:], start=False, stop=True)
        pr.op("pe", "matmul", reads=(("kw", di), "vext", "vext1"), writes=("pC",), out=pC[:, 0:257], lhsT=kw[di][:, n, :],
              rhs=vext[:, n, 0:257], start=True, stop=True)
        pr.op("dve", "scalar_tensor_tensor", reads=("pC", ("eg", di), ("C", di)), writes=(("C", di),), out=cst[di][:],
              in0=cst[di][:], scalar=eg[:, di, n:n + 1], in1=pC[:, 0:257], op0=ALU.mult, op1=ALU.add)
        pr.op("act", "copy", reads=(("C", di),), writes=(("Cbf", di),), out=cbf[di][:], in_=cst[di][:])
        rc = rr[:, di:di + 1]
        pr.op("act", "activation", reads=(bkey,), writes=(("rr", di),), out=rc, in_=bank[:, 256:257], func=AF.Abs)
        pr.op("dve", "tensor_scalar", reads=(("rr", di),), writes=(("rr", di),), out=rc, in0=rc, scalar1=1.0,
              scalar2=None, op0=ALU.max)
        pr.op("dve", "reciprocal", reads=(("rr", di),), writes=(("rr", di),), out=rc, in_=rc)
        return bank, bkey, rc

    for n in range(NCH - 1, -1, -1):
        bank, bkey, rc = mlstm_common(1, n)
        s2 = n % 2
        pr.op("dve", "tensor_scalar", reads=(bkey, ("rr", 1)), writes=(("hbt", s2),), out=hbt[:, s2, :],
              in0=bank[:, 0:256], scalar1=rc, scalar2=None, op0=ALU.mult)
        pr.dma("sp", hb_d[n], hbt[:, s2, :], reads=(("hbt", s2),), writes=(("hbd", n),))

    aq4 = v["aq4"]
    smo3_r = v["smo3"]
    attT_r = v["attT_r"]
    memT_r = v["memT_r"]
    for n in range(NCH):
        s2 = n % 2
        tok = slice(n * P, (n + 1) * P)
        pr.dma("sp", qblk[:, s2, :].rearrange("p (h t) -> p h t", h=2), aq4[:, n, :, :], writes=(("qblk", s2),))
        kbs = [kb for kb in (n - 1, n, n + 1) if 0 <= kb < NCH]
        for i, kb in enumerate(kbs):
            bank = pA[i // 2]
            bkey = "pA%d" % (i // 2)
            reg = bank[:, (i % 2) * 256:(i % 2) * 256 + 256]
            masked = kb != n
            pr.op("pe", "matmul", reads=("ak", ("qblk", s2)), writes=(bkey,), out=reg, lhsT=ak[:, kb * P:(kb + 1) * P],
                  rhs=qblk[:, s2, :], start=True, stop=not masked)
            if masked:
                nm = nmp if kb < n else nmn
                nmk = "nmp" if kb < n else "nmn"
                pr.op("pe", "matmul", reads=("ident", nmk), writes=(bkey,), out=reg, lhsT=ident_bf[:], rhs=nm[:],
                      start=False, stop=True)
            pr.op("act", "activation", reads=(bkey,), writes=(("pT_sb", s2, i),), out=pT_sb[:, s2, i, :], in_=reg,
                  func=AF.Exp, scale=ATT_SCALE)
        for i, kb in enumerate(kbs):
            pr.op("pe", "matmul", reads=("av3", ("pT_sb", s2, i)), writes=("pO",), out=pO[:, 0:256], lhsT=av3[:, kb, :],
                  rhs=pT_sb[:, s2, i, :], start=(i == 0), stop=(i == len(kbs) - 1))
        for i, kb in enumerate(kbs):
            pr.op("pe", "matmul", reads=("ones_bf", ("pT_sb", s2, i)), writes=("pO",), out=pO[:, 256:512], lhsT=ones_bf[:],
                  rhs=pT_sb[:, s2, i, :], start=(i == 0), stop=(i == len(kbs) - 1))
        pr.op("dve", "tensor_tensor", reads=("pO", "esink"), writes=(("dent", s2),), out=dent[:, s2, :], in0=pO[:, 256:512],
              in1=esink[:], op=ALU.add)
        pr.op("dve", "reciprocal", reads=(("dent", s2),), writes=(("dent", s2),), out=dent[:, s2, :], in_=dent[:, s2, :])
        pr.op("dve", "tensor_tensor", reads=("pO", ("dent", s2)), writes=(("ast", s2),), out=ast[:, s2, :], in0=pO[:, 0:256],
              in1=dent[:, s2, :], op=ALU.mult)
        outs.append(pr.dma("sp", attT_r[:, :, tok], ast[:, s2, :].rearrange("d (h t) -> d h t", h=2),
                           reads=(("ast", s2),), writes=(("out", len(outs)),)))

        pr.dma("sp", hbt[:, s2, :], hb_d[n], reads=(("hbd", n),), writes=(("hbt", s2),))
        pr.dma("sp", smo_t[:, s2, :], smo3_r[:, n, :], writes=(("smo", s2),))
        bank, bkey, rc = mlstm_common(0, n)
        pr.op("dve", "scalar_tensor_tensor", reads=(bkey, ("rr", 0), ("hbt", s2)), writes=(("hsum", s2),),
              out=hsum[:, s2, :], in0=bank[:, 0:256], scalar=rc, in1=hbt[:, s2, :], op0=ALU.mult, op1=ALU.add)
        ss_c = ssn[:, s2:s2 + 1]
        rs_c = rsn[:, s2:s2 + 1]
        pr.op("act", "activation", reads=(("hsum", s2),), writes=("junk", ("ssn", s2)), out=junk[:], in_=hsum[:, s2, :],
              func=AF.Square, accum_out=ss_c)
        rstd_chain(pr, ss_c, ("ssn", s2), rs_c, ("rsn", s2), 256)
        pr.op("dve", "tensor_tensor", reads=(("smo", s2), "gn"), writes=(("smo", s2),), out=smo_t[:, s2, :],
              in0=smo_t[:, s2, :], in1=gn[:], op=ALU.mult)
        pr.op("dve", "scalar_tensor_tensor", reads=(("hsum", s2), ("rsn", s2), ("smo", s2)), writes=(("obf", s2),),
              out=obf[:, s2, :], in0=hsum[:, s2, :], scalar=rs_c, in1=smo_t[:, s2, :], op0=ALU.mult, op1=ALU.mult)
        for h2 in range(2):
            pr.op("pe", "transpose", reads=(("obf", s2), "ident"), writes=("pT",), out=pT[:, h2 * P:(h2 + 1) * P],
                  in_=obf[:, s2, h2 * P:(h2 + 1) * P], identity=ident_bf[:])
        pr.op("act", "copy", reads=("pT",), writes=(("mst", s2),), out=mst[:, s2, :], in_=pT[:, 0:256])
        outs.append(pr.dma("sp", memT_r[:, :, tok], mst[:, s2, :].rearrange("e (h t) -> e h t", h=2),
                           reads=(("mst", s2),), writes=(("out", len(outs)),)))

    pr.flush()


def emit_k2(pr, v, gate_j):
    for dst, src, k in v.get("extra_dmas", []):
        pr.dma("pool", dst, src, writes=(k,))
    ak_d, mq_d, mk_d = v["ak"], v["mq"], v["mk"]
    gb4_d, cw_d, gn_d, sink_d = v["gb4"], v["cw"], v["gn"], v["sink2"]
    identf_d, le_d, ge_d, nmp_d, nmn_d = v["identf"], v["le"], v["ge"], v["nmp"], v["nmn"]

    def sb(name, shape, dt):
        return pr.sb(name + "_s", shape, dt)
    nc = pr.nc
    identf = sb("identf", [P, P], F32)
    ident_bf = sb("ident_bf", [P, P], BF16)
    onesf = sb("onesf", [P, P], F32)
    ones_bf = sb("ones_bf", [P, P], BF16)
    le = sb("le", [P, P], F32)
    ge = sb("ge", [P, P], F32)
    nmp = sb("nmp", [P, 256], BF16)
    nmn = sb("nmn", [P, 256], BF16)
    gb4 = sb("gb4", [P, 4], F32)
    negb = sb("negb", [P, 4], F32)
    cw = sb("cw", [P, 6], F32)
    gn = sb("gn", [P, 256], F32)
    esink = sb("esink", [P, 256], F32)
    g4 = sb("g4", [P, 4, NCH], F32)
    li = sb("li", [P, 2, NCH], F32)
    lf = sb("lf", [P, 2, NCH], F32)
    cg = sb("cg", [P, 4, NCH], F32)
    ecol = sb("ecol", [P, 2, NCH], F32)
    wend = sb("wend", [P, 2, NCH], F32)
    eg = sb("eg", [P, 2, NCH], F32)
    gtmp = sb("gtmp", [P, 2, NCH], F32)
    ak = sb("ak", [P, S], BF16)
    av3 = sb("av3", [P, NCH, P], BF16)
    kT = sb("kT", [P, S], BF16)
    qs = [sb("qs_f", [P, S], BF16), sb("qs_b", [P, S], BF16)]
    kw = [sb("kw_f", [P, NCH, P], BF16), sb("kw_b", [P, NCH, P], BF16)]
    vext = sb("vext", [P, NCH, 258], BF16)
    PSZ = 512
    NPC = S // PSZ
    xq = sb("xq", [P, PSZ + 2], F32)
    yq = sb("yq", [P, PSZ], F32)
    xk = sb("xk", [P, PSZ + 2], F32)
    yk = sb("yk", [P, PSZ], F32)
    lfbc = sb("lfbc", [P, 2, 4, P], F32)
    eb = sb("eb", [P, 2, 512], F32)
    cst = [sb("C_f", [P, 257], F32), sb("C_b", [P, 257], F32)]
    cbf = [sb("Cbf_f", [P, 257], BF16), sb("Cbf_b", [P, 257], BF16)]
    ws = sb("ws", [P, 2, P], BF16)
    rr = sb("rr", [P, 4], F32)
    hbt = sb("hbt", [P, 2, 2, 256], F32)
    NCS = 3
    hcm = sb("hcm", [P, NCS, 2, 256], F32)
    hsum = sb("hsum", [P, NCS, 256], F32)
    smo_t = sb("smo_t", [P, NCS, 256], F32)
    obf = sb("obf", [P, NCS, 256], BF16)
    junk = sb("junk", [P, NCS, 256], F32)
    ssn = sb("ssn", [P, NCS], F32)
    rsn = sb("rsn", [P, NCS], F32)
    mst = sb("mst", [P, NCS, 256], BF16)
    qblk = sb("qblk", [P, 2, 256], BF16)
    pT_sb = sb("pT_sb", [P, 2, 3, 256], BF16)
    dent = sb("dent", [P, 2, 256], F32)
    ast = sb("ast", [P, 2, 256], BF16)

    pX = [pr.ps("pXf", [P, 512], F32), pr.ps("pXb", [P, 512], F32)]
    pH = [pr.ps("pHf", [P, 512], F32), pr.ps("pHb", [P, 512], F32)]
    pT = pr.ps("pT", [P, 1024], BF16)
    hfb_d = v["hfb"]
    pA = [pr.ps("pA0", [P, 512], F32), pr.ps("pA1", [P, 512], F32)]
    pO = pr.ps("pO", [P, 512], F32)

    pr.dma("sp", identf[:], identf_d, writes=("identf",))
    pr.dma("pool", ident_bf[:], identf_d, writes=("ident",))
    pr.dma("sp", le[:], le_d, writes=("le",))
    pr.dma("sp", ge[:], ge_d, writes=("ge",))
    pr.dma("pool", nmp[:], nmp_d, writes=("nmp",))
    pr.dma("pool", nmn[:], nmn_d, writes=("nmn",))
    pr.dma("sp", gb4[:], gb4_d, writes=("gb4",))
    pr.dma("sp", cw[:], cw_d, writes=("cw",))
    pr.dma("sp", gn[:], gn_d, writes=("gn",))
    pr.dma("sp", esink[:], sink_d, writes=("esink",))
    if gate_j is None:
        pr.dma("sp", g4[:], v["g4"], writes=("g4",))
    else:
        gall = sb("gall", [P, NCH, 16], F32)
        pr.dma("sp", gall[:], v["gall"], writes=("gall",))
        for g in range(4):
            pr.op("dve", "tensor_copy", reads=("gall",), writes=("g4",), out=g4[:, g, :],
                  in_=gall[:, :, g * 4 + gate_j])
    pr.dma("sp", ak[:], ak_d, writes=("ak",))
    pr.dma("sp", av3[:], v["av3"], writes=("av3",))
    pr.dma("sp", vext[:, :, 0:256], v["mv3"], writes=("vext",))
    pr.op("dve", "memset", writes=("vext1",), ap=vext[:, :, 256:257], constant=1.0)
    pr.op("dve", "memset", writes=("onesf",), ap=onesf[:], constant=1.0)
    pr.op("dve", "memset", writes=("ones_bf",), ap=ones_bf[:], constant=1.0)
    for di in range(2):
        pr.op("dve", "memset", writes=(("C", di),), ap=cst[di][:], constant=0.0)
        pr.op("dve", "memset", writes=(("Cbf", di),), ap=cbf[di][:], constant=0.0)
    pr.op("act", "activation", reads=("esink",), writes=("esink",), out=esink[:], in_=esink[:], func=AF.Exp)

    def conv_piece(src_d, xb, xkey, yb_, ykey, w0, pc):
        p0 = pc * PSZ
        lo = p0 - 1 if pc > 0 else 0
        hi = p0 + PSZ + 1 if pc < NPC - 1 else S
        c_lo = 0 if pc > 0 else 1
        if pc == 0:
            pr.op("dve", "memset", writes=(xkey,), ap=xb[:, 0:1], constant=0.0)
        if pc == NPC - 1:
            pr.op("dve", "memset", writes=(xkey,), ap=xb[:, PSZ + 1:PSZ + 2], constant=0.0)
        pr.dma("sp", xb[:, c_lo:c_lo + (hi - lo)], src_d[:, lo:hi], writes=(xkey,))
        pr.op("dve", "tensor_scalar", reads=(xkey, "cw"), writes=(ykey,), out=yb_[:], in0=xb[:, 1:PSZ + 1],
              scalar1=cw[:, w0 + 1:w0 + 2], scalar2=None, op0=ALU.mult)
        pr.op("dve", "scalar_tensor_tensor", reads=(xkey, "cw", ykey), writes=(ykey,), out=yb_[:], in0=xb[:, 0:PSZ],
              scalar=cw[:, w0:w0 + 1], in1=yb_[:], op0=ALU.mult, op1=ALU.add)
        pr.op("dve", "scalar_tensor_tensor", reads=(xkey, "cw", ykey), writes=(ykey,), out=yb_[:], in0=xb[:, 2:PSZ + 2],
              scalar=cw[:, w0 + 2:w0 + 3], in1=yb_[:], op0=ALU.mult, op1=ALU.add)
        pr.op("act", "activation", reads=(ykey,), writes=(ykey,), out=yb_[:], in_=yb_[:], func=AF.Silu)


    def preproc():
        pr.op("dve", "tensor_scalar", reads=("gb4",), writes=("negb",), out=negb[:], in0=gb4[:], scalar1=-1.0,
              scalar2=None, op0=ALU.mult)
        for di in range(2):
            pr.op("dve", "tensor_scalar", reads=("g4", "gb4"), writes=(("li", di),), out=li[:, di, :],
                  in0=g4[:, 2 * di, :], scalar1=gb4[:, 2 * di:2 * di + 1], scalar2=None, op0=ALU.add)
            pr.op("act", "activation", reads=("g4", "negb"), writes=(("gtmp", di),), out=gtmp[:, di, :],
                  in_=g4[:, 2 * di + 1, :], func=AF.Exp, scale=-1.0, bias=negb[:, 2 * di + 1:2 * di + 2])
            pr.op("act", "activation", reads=(("gtmp", di),), writes=(("gtmp", di),), out=gtmp[:, di, :],
                  in_=gtmp[:, di, :], func=AF.Ln, bias=1.0)
            pr.op("dve", "tensor_scalar", reads=(("gtmp", di),), writes=(("lf", di),), out=lf[:, di, :],
                  in0=gtmp[:, di, :], scalar1=-1.0, scalar2=None, op0=ALU.mult)
        for di in range(2):
            tri = le if di == 0 else ge
            trik = "le" if di == 0 else "ge"
            pr.op("pe", "matmul", reads=(("lf", di), trik), writes=(("pH", 0),), out=pH[0][:, di * 128:di * 128 + 64],
                  lhsT=tri[:], rhs=lf[:, di, :], start=True, stop=True)
            pr.op("pe", "matmul", reads=(("lf", di), "onesf"), writes=(("pH", 0),), out=pH[0][:, di * 128 + 64:di * 128 + 128],
                  lhsT=onesf[:], rhs=lf[:, di, :], start=True, stop=True)
        pr.op("act", "copy", reads=(("pH", 0),), writes=("cg",), out=cg[:].rearrange("p g n -> p (g n)"), in_=pH[0][:, 0:256])
        for di in range(2):
            bc = cg[:, 2 * di, :]
            gt_ = cg[:, 2 * di + 1, :]
            pr.op("dve", "tensor_tensor", reads=(("li", di), "cg"), writes=(("gtmp", di),), out=gtmp[:, di, :],
                  in0=li[:, di, :], in1=bc, op=ALU.subtract)
            pr.op("act", "activation", reads=(("gtmp", di),), writes=(("ecol", di),), out=ecol[:, di, :],
                  in_=gtmp[:, di, :], func=AF.Exp)
            pr.op("dve", "tensor_tensor", reads=(("gtmp", di), "cg"), writes=(("gtmp", di),), out=gtmp[:, di, :],
                  in0=gtmp[:, di, :], in1=gt_, op=ALU.add)
            pr.op("act", "activation", reads=(("gtmp", di),), writes=(("wend", di),), out=wend[:, di, :],
                  in_=gtmp[:, di, :], func=AF.Exp)
            pr.op("act", "activation", reads=("cg",), writes=(("eg", di),), out=eg[:, di, :], in_=gt_, func=AF.Exp)

        yield

    def q_stream():
        ebc = [0]
        for pc in range(NPC):
            p0 = pc * PSZ
            conv_piece(mq_d, xq, "xq", yq, "yq", 0, pc)
            yield
            for sp_ in range(PSZ // 512):
                for di in range(2):
                    tri = le if di == 0 else ge
                    trik = "le" if di == 0 else "ge"
                    bank = pX[ebc[0] % 2]
                    bkeys = (("pX", ebc[0] % 2),)
                    es = ebc[0] % 2
                    ebc[0] += 1
                    for c4 in range(4):
                        n = pc * (PSZ // P) + sp_ * 4 + c4
                        pr.op("dve", "tensor_scalar", reads=("onesf", ("lf", di)), writes=(("lfbc", di, c4),),
                              out=lfbc[:, di, c4, :], in0=onesf[:], scalar1=lf[:, di, n:n + 1], scalar2=None, op0=ALU.mult)
                        pr.op("pe", "matmul", reads=(("lfbc", di, c4), trik), writes=bkeys, out=bank[:, c4 * P:(c4 + 1) * P],
                              lhsT=lfbc[:, di, c4, :], rhs=tri[:], start=True, stop=True)
                    yield
                    pr.op("act", "activation", reads=bkeys, writes=(("eb", es),), out=eb[:, es, :], in_=bank[:], func=AF.Exp)
                    yield
                    t0 = p0 + sp_ * 512
                    pr.op("dve", "scalar_tensor_tensor", reads=("yq", ("eb", es)), writes=(("qs", di),),
                          out=qs[di][:, t0:t0 + 512], in0=yq[:, sp_ * 512:(sp_ + 1) * 512], scalar=QK_SCALE, in1=eb[:, es, :],
                          op0=ALU.mult, op1=ALU.mult)
                    yield

    def k_stream():
        for pc in range(NPC):
            p0 = pc * PSZ
            conv_piece(mk_d, xk, "xk", yk, "yk", 3, pc)
            yield
            pr.op("act", "copy", reads=("yk",), writes=("kT",), out=kT[:, p0:p0 + PSZ], in_=yk[:])
            for c16 in range(PSZ // P):
                n = pc * (PSZ // P) + c16
                pr.op("pe", "transpose", reads=("yk", "identf"), writes=(("pH", 1),), out=pH[1][:, 0:P],
                      in_=yk[:, c16 * P:(c16 + 1) * P], identity=identf[:])
                yield
                for di in range(2):
                    pr.op("act", "activation", reads=(("pH", 1), ("wend", di)), writes=(("kw", di),), out=kw[di][:, n, :],
                          in_=pH[1][:, 0:P], func=AF.Copy, scale=wend[:, di, n:n + 1])
                yield

    outs = []
    smo3_r = v["smo3"]
    aq4 = v["aq4"]
    att_dst = v["att_dst"]
    mem_dst = v["mem_dst"]

    progress = [0, 0]
    LAG = 3

    def scan_stream(di):
        order = range(NCH) if di == 0 else range(NCH - 1, -1, -1)
        msk = le if di == 0 else ge
        mskk = "le" if di == 0 else "ge"
        X = pX[di]
        H = pH[di]
        for n in order:
            tok = slice(n * P, (n + 1) * P)
            s2 = n % 2
            pr.op("pe", "matmul", reads=("kT", ("qs", di)), writes=(("pX", di),), out=X[:, 0:P], lhsT=kT[:, tok],
                  rhs=qs[di][:, tok], start=True, stop=True)
            yield
            pr.op("dve", "scalar_tensor_tensor", reads=(("pX", di), ("ecol", di), mskk), writes=(("ws", di),),
                  out=ws[:, di, :], in0=X[:, 0:P], scalar=ecol[:, di, n:n + 1], in1=msk[:], op0=ALU.mult, op1=ALU.mult)
            yield
            pr.op("pe", "matmul", reads=(("ws", di), "vext", "vext1"), writes=(("pH", di),), out=H[:, 0:257],
                  lhsT=ws[:, di, :], rhs=vext[:, n, 0:257], start=True, stop=False)
            pr.op("pe", "matmul", reads=(("qs", di), ("Cbf", di)), writes=(("pH", di),), out=H[:, 0:257],
                  lhsT=qs[di][:, tok], rhs=cbf[di][:], start=False, stop=True)
            pr.op("pe", "matmul", reads=(("kw", di), "vext", "vext1"), writes=(("pX", di),), out=X[:, 128:385],
                  lhsT=kw[di][:, n, :], rhs=vext[:, n, 0:257], start=True, stop=True)
            yield
            pr.op("dve", "scalar_tensor_tensor", reads=(("pX", di), ("eg", di), ("C", di)), writes=(("C", di),),
                  out=cst[di][:], in0=cst[di][:], scalar=eg[:, di, n:n + 1], in1=X[:, 128:385], op0=ALU.mult, op1=ALU.add)
            yield
            pr.op("act", "copy", reads=(("C", di),), writes=(("Cbf", di),), out=cbf[di][:], in_=cst[di][:])
            rc = rr[:, di:di + 1]
            pr.op("act", "activation", reads=(("pH", di),), writes=(("rr", di),), out=rc, in_=H[:, 256:257], func=AF.Abs)
            yield
            pr.op("dve", "tensor_scalar", reads=(("rr", di),), writes=(("rr", di),), out=rc, in0=rc, scalar1=1.0,
                  scalar2=None, op0=ALU.max)
            pr.op("dve", "reciprocal", reads=(("rr", di),), writes=(("rr", di),), out=rc, in_=rc)
            yield
            pr.op("act", "activation", reads=(("pH", di), ("rr", di)), writes=(("hbt", di, s2),), out=hbt[:, di, s2, :],
                  in_=H[:, 0:256], func=AF.Copy, scale=rc)
            pr.dma("sp", hfb_d[di, n], hbt[:, di, s2, :], reads=(("hbt", di, s2),), writes=(("hfd", di, n),))
            progress[di] += 1
            yield

    def attention_stream():
        for n in range(NCH):
            s2 = n % 2
            tok = slice(n * P, (n + 1) * P)
            pr.dma("sp", qblk[:, s2, :].rearrange("p (h t) -> p h t", h=2), aq4[:, n, :, :], writes=(("qblk", s2),))
            kbs = [kb for kb in (n - 1, n, n + 1) if 0 <= kb < NCH]
            for i, kb in enumerate(kbs):
                bank = pA[i // 2]
                bkey = "pA%d" % (i // 2)
                reg = bank[:, (i % 2) * 256:(i % 2) * 256 + 256]
                masked = kb != n
                pr.op("pe", "matmul", reads=("ak", ("qblk", s2)), writes=(bkey,), out=reg, lhsT=ak[:, kb * P:(kb + 1) * P],
                      rhs=qblk[:, s2, :], start=True, stop=not masked)
                if masked:
                    nm = nmp if kb < n else nmn
                    nmk = "nmp" if kb < n else "nmn"
                    pr.op("pe", "matmul", reads=("ident", nmk), writes=(bkey,), out=reg, lhsT=ident_bf[:], rhs=nm[:],
                          start=False, stop=True)
                yield
                pr.op("act", "activation", reads=(bkey,), writes=(("pT_sb", s2, i),), out=pT_sb[:, s2, i, :], in_=reg,
                      func=AF.Exp, scale=ATT_SCALE)
                yield
            for i, kb in enumerate(kbs):
                pr.op("pe", "matmul", reads=("av3", ("pT_sb", s2, i)), writes=("pO",), out=pO[:, 0:256], lhsT=av3[:, kb, :],
                      rhs=pT_sb[:, s2, i, :], start=(i == 0), stop=(i == len(kbs) - 1))
            for i, kb in enumerate(kbs):
                pr.op("pe", "matmul", reads=("ones_bf", ("pT_sb", s2, i)), writes=("pO",), out=pO[:, 256:512],
                      lhsT=ones_bf[:], rhs=pT_sb[:, s2, i, :], start=(i == 0), stop=(i == len(kbs) - 1))
            yield
            pr.op("dve", "tensor_tensor", reads=("pO", "esink"), writes=(("dent", s2),), out=dent[:, s2, :],
                  in0=pO[:, 256:512], in1=esink[:], op=ALU.add)
            pr.op("dve", "reciprocal", reads=(("dent", s2),), writes=(("dent", s2),), out=dent[:, s2, :], in_=dent[:, s2, :])
            yield
            pr.op("dve", "tensor_tensor", reads=("pO", ("dent", s2)), writes=(("ast", s2),), out=ast[:, s2, :],
                  in0=pO[:, 0:256], in1=dent[:, s2, :], op=ALU.mult)
            outs.append(pr.dma("sp", att_dst(n), ast[:, s2, :].rearrange("d (h t) -> d h t", h=2),
                               reads=(("ast", s2),), writes=(("out", len(outs)),)))
            yield

    comb_order = sorted(range(NCH), key=lambda n: (max(n, NCH - 1 - n), n))

    def combine_stream(r):
        for n in comb_order[r::NCS]:
            need = min(NCH, max(n, NCH - 1 - n) + 1 + LAG)
            while min(progress) < need:
                yield
            tok = slice(n * P, (n + 1) * P)
            for di in range(2):
                pr.dma("sp", hcm[:, r, di, :], hfb_d[di, n], reads=(("hfd", di, n),), writes=(("hcm", r, di),))
            pr.dma("sp", smo_t[:, r, :], smo3_r[:, n, :], writes=(("smo", r),))
            yield
            pr.op("dve", "tensor_tensor", reads=(("hcm", r, 0), ("hcm", r, 1)), writes=(("hsum", r),), out=hsum[:, r, :],
                  in0=hcm[:, r, 0, :], in1=hcm[:, r, 1, :], op=ALU.add)
            yield
            ss_c = ssn[:, r:r + 1]
            rs_c = rsn[:, r:r + 1]
            pr.op("act", "activation", reads=(("hsum", r),), writes=(("junk", r), ("ssn", r)), out=junk[:, r, :],
                  in_=hsum[:, r, :], func=AF.Square, accum_out=ss_c)
            pr.op("dve", "tensor_tensor", reads=(("smo", r), "gn"), writes=(("smo", r),), out=smo_t[:, r, :],
                  in0=smo_t[:, r, :], in1=gn[:], op=ALU.mult)
            yield
            pr.op("dve", "tensor_scalar", reads=(("ssn", r),), writes=(("rsn", r),), out=rs_c, in0=ss_c,
                  scalar1=1.0 / 256, scalar2=EPS, op0=ALU.mult, op1=ALU.add)
            yield
            pr.op("act", "activation", reads=(("rsn", r),), writes=(("rsn", r),), out=rs_c, in_=rs_c, func=AF.Sqrt)
            yield
            pr.op("dve", "reciprocal", reads=(("rsn", r),), writes=(("rsn", r),), out=rs_c, in_=rs_c)
            yield
            pr.op("dve", "scalar_tensor_tensor", reads=(("hsum", r), ("rsn", r), ("smo", r)), writes=(("obf", r),),
                  out=obf[:, r, :], in0=hsum[:, r, :], scalar=rs_c, in1=smo_t[:, r, :], op0=ALU.mult, op1=ALU.mult)
            yield
            for h2 in range(2):
                pr.op("pe", "transpose", reads=(("obf", r), "ident"), writes=("pT",),
                      out=pT[:, r * 256 + h2 * P:r * 256 + (h2 + 1) * P], in_=obf[:, r, h2 * P:(h2 + 1) * P],
                      identity=ident_bf[:])
            yield
            pr.op("act", "copy", reads=("pT",), writes=(("mst", r),), out=mst[:, r, :], in_=pT[:, r * 256:(r + 1) * 256])
            outs.append(pr.dma("sp", mem_dst(n), mst[:, r, :].rearrange("e (h t) -> e h t", h=2),
                               reads=(("mst", r),), writes=(("out", len(outs)),)))
            yield

    pre = preproc()
    active = [pre, attention_stream()]
    while active:
        for g in list(active):
            try:
                next(g)
            except StopIteration:
                active.remove(g)
                if g is pre:
                    qg, kg = q_stream(), k_stream()
                    pend = {id(qg), id(kg)}
                    active.extend([qg, kg])
                elif "pend" in dir() and id(g) in pend:
                    pend.discard(id(g))
                    if not pend:
                        active.extend([scan_stream(0), scan_stream(1)] + [combine_stream(r) for r in range(NCS)])
    pr.flush()


def _wx_index():
    cols = []
    for hh in range(10):
        c0 = hh * 128 if hh < 8 else 1024 + (hh - 8) * 128
        cols += [np.arange(c0, c0 + 128), np.arange(c0 + 64, c0 + 128), np.arange(c0, c0 + 64)]
    cols += [np.arange(1536, 2560), np.arange(1280, 1536), np.arange(4608, 4624), np.arange(2560, 4608)]
    return np.concatenate(cols)


def _rope_tables():
    half = 64
    inv_freq = (np.float32(10000.0) ** (-np.arange(half, dtype=np.float32) / np.float32(half))).astype(np.float32)
    pos = np.arange(S, dtype=np.float32)
    ang = (pos[:, None] * inv_freq[None, :]).astype(np.float32)
    cos = np.cos(ang).astype(np.float32)
    sin = np.sin(ang).astype(np.float32)
    cosT = np.concatenate([cos, cos], axis=1).T
    sinT = np.concatenate([-sin, sin], axis=1).T
    return np.ascontiguousarray(cosT), np.ascontiguousarray(sinT)


_CACHE = {}
_RUNKW = {}
_LAST = {}


def _get(name, builder):
    if name not in _CACHE:
        _CACHE[name] = builder()
    return _CACHE[name]


def run_k1(x_flat, w_in_l, g_pre_l):
    nc = _get("k1", build_k1)
    wx = np.ascontiguousarray(w_in_l[:, _get("wxi", _wx_index)])
    cosT, sinT = _get("rope", _rope_tables)
    gcol = np.ascontiguousarray(g_pre_l.reshape(KC, P).T)
    ident = np.eye(P, dtype=np.float32)
    in_maps = []
    for c in range(NCORE):
        p0 = (c % 4) * TOK
        in_maps.append({
            "x": np.ascontiguousarray(x_flat[c * TOK:(c + 1) * TOK]),
            "w_in": wx, "gcol": gcol,
            "cosT": np.ascontiguousarray(cosT[:, p0:p0 + TOK]),
            "sinT": np.ascontiguousarray(sinT[:, p0:p0 + TOK]),
            "ident": ident,
        })
    res = run_bass_kernel_spmd(nc, in_maps, core_ids=list(range(NCORE)), **_RUNKW)
    _LAST["t"] = res.exec_time_ns
    return res.results


def _k2_consts():
    r = np.arange(P)
    le = (r[:, None] <= r[None, :]).astype(np.float32)
    ge = (r[:, None] >= r[None, :]).astype(np.float32)
    nmp1 = np.where(r[:, None] < r[None, :], np.float32(-30000.0), np.float32(0.0)).astype(np.float32)
    nmn1 = np.where(r[:, None] > r[None, :], np.float32(-30000.0), np.float32(0.0)).astype(np.float32)
    return {
        "identf": np.eye(P, dtype=np.float32), "le": le, "ge": ge,
        "nmp": np.ascontiguousarray(np.concatenate([nmp1, nmp1], axis=1)),
        "nmn": np.ascontiguousarray(np.concatenate([nmn1, nmn1], axis=1)),
    }


def run_k2(k1, conv_w_l, gate_bias_l, ml_norm_g_l, attn_sink_l):
    nc = _get("k2", build_k2)
    consts = _get("k2c", _k2_consts)
    ca = np.ascontiguousarray
    in_maps = []
    for c in range(NCORE):
        b, j = c // 4, c % 4
        kv = j // 2
        cores = k1[4 * b:4 * b + 4]
        AQ = np.concatenate([r["aq"][2 * j * P:(2 * j + 2) * P] for r in cores], axis=1)
        AK = np.concatenate([r["ak"][kv * P:(kv + 1) * P] for r in cores], axis=1)
        AV = np.concatenate([r["av"][:, kv * P:(kv + 1) * P] for r in cores], axis=0)
        MQ = np.concatenate([r["mqk"][j * P:(j + 1) * P] for r in cores], axis=1)
        MK = np.concatenate([r["mqk"][512 + j * P:512 + (j + 1) * P] for r in cores], axis=1)
        MV = np.concatenate([r["mv"][:, j * 256:(j + 1) * 256] for r in cores], axis=0)
        SMO = np.concatenate([r["smo"][:, j * 256:(j + 1) * 256] for r in cores], axis=0)
        GT = np.concatenate([r["gt"][:, [j, 4 + j, 8 + j, 12 + j]] for r in cores], axis=0)
        m = dict(consts)
        m["aq2"] = ca(AQ.reshape(2, P, NCH, P).transpose(1, 2, 0, 3).reshape(P, NCH * 256))
        m["ak"] = ca(AK)
        m["av3"] = ca(AV.reshape(NCH, P, P).transpose(1, 0, 2).reshape(P, NCH * P))
        m["mq"] = ca(MQ)
        m["mk"] = ca(MK)
        m["mv3"] = ca(MV.reshape(NCH, P, 256).transpose(1, 0, 2).reshape(P, NCH * 256))
        m["smo3"] = ca(SMO.reshape(NCH, P, 256).transpose(1, 0, 2).reshape(P, NCH * 256))
        m["g4"] = ca(GT.reshape(NCH, P, 4).transpose(1, 2, 0).reshape(P, 4 * NCH))
        m["gb4"] = ca(np.broadcast_to(gate_bias_l[[j, 4 + j, 8 + j, 12 + j]][None, :], (P, 4)))
        cwq = conv_w_l[:, j * P:(j + 1) * P].T
        cwk = conv_w_l[:, 512 + j * P:512 + (j + 1) * P].T
        m["cw"] = ca(np.concatenate([cwq, cwk], axis=1))
        m["gn"] = ca(np.broadcast_to(ml_norm_g_l[j * 256:(j + 1) * 256][None, :], (P, 256)))
        m["sink2"] = ca(np.broadcast_to(np.repeat(attn_sink_l[2 * j:2 * j + 2], P)[None, :], (P, 256)))
        in_maps.append(m)
    res = run_bass_kernel_spmd(nc, in_maps, core_ids=list(range(NCORE)), **_RUNKW)
    _LAST["t"] = res.exec_time_ns
    out = res.results
    mixT = []
    for c in range(NCORE):
        b, part = c // 4, c % 4
        sl = slice(part * TOK, (part + 1) * TOK)
        att = [out[4 * b + j]["attT"][:, sl] for j in range(4)]
        mem = [out[4 * b + j]["memT"][:, sl] for j in range(4)]
        mixT.append(ca(np.concatenate(att + mem, axis=0)))
    return mixT


def kernel_unfused(x, w_in, conv_w, gate_bias, ml_norm_g, attn_sink, w_out,
                   g_pre_mix, g_post_mix, g_pre_mlp, g_post_mlp, w_up, w_down):
    f = lambda a: np.ascontiguousarray(np.asarray(a, dtype=np.float32))
    x = f(x)
    xf = x.reshape(B * S, D)
    depth = w_in.shape[0]
    for l in range(depth):
        k1 = run_k1(xf, f(w_in[l]), f(g_pre_mix[l]))
        mixT = run_k2(k1, f(conv_w[l]), f(gate_bias[l]), f(ml_norm_g[l]), f(attn_sink[l]))
        del k1
        ys = run_k3(mixT, xf, f(w_out[l]), f(w_up[l]), f(w_down[l]), f(g_post_mix[l]), f(g_pre_mlp[l]),
                    f(g_post_mlp[l]))
        xf = np.concatenate(ys, axis=0)
    return xf.reshape(B, S, D).astype(np.float32)


DEPTH = 2


def build_fused():
    nc = bass.Bass("TRN2", target_bir_lowering=False)

    def din(name, shape, dt=F32):
        return nc.dram_tensor(name, list(shape), dt, kind="ExternalInput").ap()

    def dint(name, shape, dt):
        return nc.dram_tensor(name, list(shape), dt).ap()
    x_in = din("x", [S, D])
    w_in = din("w_in", [DEPTH, D, WX_COLS])
    w_out = din("w_out", [DEPTH, D, D])
    w_up = din("w_up", [DEPTH, D, DFF])
    w_down = din("w_down", [DEPTH, DFF, D])
    gcol1 = din("gcol1", [DEPTH, P, KC])
    gcol2 = din("gcol2", [DEPTH, P, KC])
    g_pm = din("g_pm", [DEPTH, P, D])
    g_pl = din("g_pl", [DEPTH, P, D])
    cosT = din("cosT", [P, S])
    sinT = din("sinT", [P, S])
    ident = din("ident", [P, P])
    gb4 = din("gb4", [DEPTH, 4, P, 4])
    cw = din("cw", [DEPTH, 4, P, 6])
    gn = din("gn", [DEPTH, 4, P, 256])
    sink2 = din("sink2", [DEPTH, 4, P, 256])
    le = din("le", [P, P])
    ge = din("ge", [P, P])
    nmp = din("nmp", [P, 256])
    nmn = din("nmn", [P, 256])
    y_out = nc.dram_tensor("y", [TOK, D], F32, kind="ExternalOutput").ap()
    xidx_d = din("xidx", [P, TOK // P], mybir.dt.int32)
    midx_d = din("midx", [P, (TOK // 512) * KC], mybir.dt.int32)
    aq = dint("aq_s", [1024, S], BF16)
    ak = dint("ak_s", [256, S], BF16)
    mqk = dint("mqk_s", [1024, S], F32)
    av = dint("av_s", [S, 256], BF16)
    mv = dint("mv_s", [S, 1024], BF16)
    smo = dint("smo_s", [S, 1024], F32)
    gt = dint("gt_s", [P, NCH, 16], F32)
    NBLK = S // 512
    mixB = dint("mixB_s", [NBLK, D, 512], BF16)
    x1 = dint("x1_s", [S, D], F32)
    hb = dint("hb_s", [NCH, P, 256], F32)
    hfb = dint("hfb_s", [2, NCH, P, 256], F32)

    wb = []
    casts = []
    for l in range(DEPTH):
        wl = {"w_in": dint("wb_in%d" % l, [D, WX_COLS], BF16), "w_out": dint("wb_out%d" % l, [D, D], BF16),
              "w_up": dint("wb_up%d" % l, [D, DFF], BF16), "w_down": dint("wb_down%d" % l, [DFF, D], BF16)}
        wb.append(wl)
        cl = {}
        for nm, src in (("w_in", w_in[l]), ("w_out", w_out[l]), ("w_up", w_up[l]), ("w_down", w_down[l])):
            rows = src.shape[0]
            cl[nm] = [(wl[nm][r0:r0 + P, :], src[r0:r0 + P, :], ("wcast", l, nm, r0)) for r0 in range(0, rows, P)]
        casts.append(cl)

    pr = Prog(nc)
    for dst, src, k in casts[0]["w_in"]:
        pr.dma("pool", dst, src, writes=(k,))
    pr.flush()
    rest0 = casts[0]["w_out"] + casts[0]["w_up"] + casts[0]["w_down"]
    all1 = casts[1]["w_in"] + casts[1]["w_out"] + casts[1]["w_up"] + casts[1]["w_down"]
    q1 = (len(all1) + 3) // 4
    for l in range(DEPTH):
        xsrc = x_in if l == 0 else x1
        ydst = x1 if l == 0 else y_out
        emit_k1(pr, {"x": xsrc, "w": wb[l]["w_in"], "gcol": gcol1[l], "cos": cosT, "sin": sinT, "ident": ident,
                     "aq": aq, "ak": ak, "mqk": mqk, "av": av, "mv": mv, "smo": smo, "gt": gt}, S, True,
                extra_dmas=(rest0 if l == 0 else None))
        for j in range(4):
            kv = j // 2
            v = {
                "aq4": aq[2 * j * P:(2 * j + 2) * P, :].rearrange("(h d) (n t) -> d n h t", h=2, t=P),
                "ak": ak[kv * P:(kv + 1) * P, :],
                "av3": av[:, kv * P:(kv + 1) * P].rearrange("(n t) d -> t n d", t=P),
                "mq": mqk[j * P:(j + 1) * P, :],
                "mk": mqk[512 + j * P:512 + (j + 1) * P, :],
                "mv3": mv[:, j * 256:(j + 1) * 256].rearrange("(n t) e -> t n e", t=P),
                "smo3": smo[:, j * 256:(j + 1) * 256].rearrange("(n t) e -> t n e", t=P),
                "gall": gt,
                "gb4": gb4[l, j], "cw": cw[l, j], "gn": gn[l, j], "sink2": sink2[l, j],
                "identf": ident, "le": le, "ge": ge, "nmp": nmp, "nmn": nmn,
                "att_dst": (lambda n, j=j: mixB[n // 4, 2 * j * P:(2 * j + 2) * P, (n % 4) * P:(n % 4 + 1) * P]
                            .rearrange("(h d) t -> d h t", h=2)),
                "mem_dst": (lambda n, j=j: mixB[n // 4, 1024 + j * 256:1024 + (j + 1) * 256, (n % 4) * P:(n % 4 + 1) * P]
                            .rearrange("(h e) t -> e h t", h=2)),
                "hb": hb, "hfb": hfb,
                "extra_dmas": (all1[j * q1:(j + 1) * q1] if l == 0 else []),
            }
            emit_k2(pr, v, j)
        k3d = {"x": xsrc, "w_out": wb[l]["w_out"], "w_up": wb[l]["w_up"], "w_down": wb[l]["w_down"],
               "g_pm": g_pm[l], "g_pl": g_pl[l], "gcol": gcol2[l], "ident": ident, "y": ydst,
               "mix_blk": lambda tb: mixB[tb].rearrange("(kc p) t -> p kc t", p=P)}
        if l == DEPTH - 1:
            k3d["gather"] = {"xidx": xidx_d, "midx": midx_d, "mix_flat": mixB.rearrange("b f t -> (b f) t")}
            emit_k3(pr, k3d, TOK)
        else:
            emit_k3(pr, k3d, S)
    pr.emit()
    return nc


def kernel(x, w_in, conv_w, gate_bias, ml_norm_g, attn_sink, w_out,
           g_pre_mix, g_post_mix, g_pre_mlp, g_post_mlp, w_up, w_down):
    f = lambda a: np.ascontiguousarray(np.asarray(a, dtype=np.float32))
    ca = np.ascontiguousarray
    nc = _get("fused", build_fused)
    x = f(x)
    w_in, conv_w, gate_bias, ml_norm_g, attn_sink = f(w_in), f(conv_w), f(gate_bias), f(ml_norm_g), f(attn_sink)
    g_pre_mix, g_post_mix, g_pre_mlp, g_post_mlp = f(g_pre_mix), f(g_post_mix), f(g_pre_mlp), f(g_post_mlp)
    wxi = _get("wxi", _wx_index)
    cosT, sinT = _get("rope", _rope_tables)
    consts = _get("k2c", _k2_consts)
    shared = {
        "w_in": ca(w_in[:, :, wxi]), "w_out": f(w_out), "w_up": f(w_up), "w_down": f(w_down),
        "gcol1": ca(g_pre_mix.reshape(DEPTH, KC, P).transpose(0, 2, 1)),
        "gcol2": ca(g_pre_mlp.reshape(DEPTH, KC, P).transpose(0, 2, 1)),
        "g_pm": ca(np.broadcast_to(g_post_mix[:, None, :], (DEPTH, P, D))),
        "g_pl": ca(np.broadcast_to(g_post_mlp[:, None, :], (DEPTH, P, D))),
        "cosT": cosT, "sinT": sinT, "ident": consts["identf"],
        "le": consts["le"], "ge": consts["ge"], "nmp": consts["nmp"], "nmn": consts["nmn"],
    }
    gb4 = np.zeros((DEPTH, 4, P, 4), np.float32)
    cw = np.zeros((DEPTH, 4, P, 6), np.float32)
    gn = np.zeros((DEPTH, 4, P, 256), np.float32)
    sink2 = np.zeros((DEPTH, 4, P, 256), np.float32)
    for l in range(DEPTH):
        for j in range(4):
            gb4[l, j] = gate_bias[l][[j, 4 + j, 8 + j, 12 + j]][None, :]
            cw[l, j, :, 0:3] = conv_w[l][:, j * P:(j + 1) * P].T
            cw[l, j, :, 3:6] = conv_w[l][:, 512 + j * P:512 + (j + 1) * P].T
            gn[l, j] = ml_norm_g[l][j * 256:(j + 1) * 256][None, :]
            sink2[l, j] = np.repeat(attn_sink[l][2 * j:2 * j + 2], P)[None, :]
    shared.update({"gb4": gb4, "cw": cw, "gn": gn, "sink2": sink2})
    in_maps = []
    pp = np.arange(P, dtype=np.int32)[:, None]
    for c in range(NCORE):
        part = c % 4
        m = dict(shared)
        m["x"] = ca(x[c // 4])
        m["xidx"] = ca((part * TOK + np.arange(TOK // P, dtype=np.int32)[None, :] * P + pp).astype(np.int32))
        tb = np.arange(TOK // 512, dtype=np.int32)[:, None]
        kc = np.arange(KC, dtype=np.int32)[None, :]
        rows = ((part * (TOK // 512) + tb) * D + kc * P).reshape(1, -1)
        m["midx"] = ca((rows + pp).astype(np.int32))
        in_maps.append(m)
    res = run_bass_kernel_spmd(nc, in_maps, core_ids=list(range(NCORE)), **_RUNKW)
    _LAST["t"] = res.exec_time_ns
    out = np.empty((B, S, D), np.float32)
    for c in range(NCORE):
        out[c // 4, (c % 4) * TOK:(c % 4 + 1) * TOK] = res.results[c]["y"]
    return out
```

```python
import contextlib
import numpy as np
import ml_dtypes
import concourse.bass as bass
import concourse.mybir as mybir
from concourse.bass_utils import run_bass_kernel_spmd

F32 = mybir.dt.float32
BF16 = mybir.dt.bfloat16
AF = mybir.ActivationFunctionType
ALU = mybir.AluOpType
AX = mybir.AxisListType

D = 2048
S = 8192
B = 2
NCORE = 8
TOK = 2048
P = 128
KC = D // P
IN_COLS = 4624
DFF = 8192
EPS = 1e-6
WX_COLS = 4624

ENGS = ("sp", "act", "dve", "pool", "pe")
NSLOT = 8


class Op:
    __slots__ = ("eng", "fn", "deps", "sig", "val", "dma", "slot", "slotval", "nm")

    def __init__(self, eng, fn, dma):
        self.eng = eng
        self.fn = fn
        self.deps = []
        self.sig = False
        self.val = 0
        self.dma = dma
        self.slot = 0
        self.slotval = 0


class Prog:
    def __init__(self, nc):
        self.nc = nc
        self.ops = {e: [] for e in ENGS}
        self.res = {}
        self.stack = contextlib.ExitStack()
        self.gstack = contextlib.ExitStack()
        self.sem_eng = {e: self.gstack.enter_context(nc.semaphore("sem_" + e)) for e in ENGS}
        self.sem_dma = {e: [self.gstack.enter_context(nc.semaphore("dq_%s_%d" % (e, i))) for i in range(NSLOT)]
                        for e in ENGS if e != "pe"}
        self.cnt = {e: 0 for e in ENGS}
        self.dk = {e: 0 for e in ENGS}
        self.seen = {e: {} for e in ENGS}
        self.batch = 0
        self.uid = 0

    def sb(self, name, shape, dt):
        self.uid += 1
        return self.stack.enter_context(self.nc.sbuf_tensor("%s_%d" % (name, self.uid), list(shape), dt))

    def ps(self, name, shape, dt):
        self.uid += 1
        return self.stack.enter_context(self.nc.psum_tensor("%s_%d" % (name, self.uid), list(shape), dt))

    def add(self, eng, fn, reads=(), writes=(), dma=False):
        op = Op(eng, fn, dma)
        op.nm = self.batch
        deps = set()
        for r in reads:
            st = self.res.get(r)
            if st is not None and st[0] is not None:
                deps.add(st[0])
        for w in writes:
            st = self.res.get(w)
            if st is not None:
                if st[0] is not None:
                    deps.add(st[0])
                deps.update(st[1].values())
                deps.update(st[2])
        for r in reads:
            st = self.res.get(r)
            if st is None:
                st = [None, {}, []]
                self.res[r] = st
            if dma:
                st[2].append(op)
            else:
                st[1][eng] = op
        for w in writes:
            self.res[w] = [op, {}, []]
        deps.discard(op)
        for d in deps:
            if d.nm != self.batch:
                continue
            if d.eng == "pe" and eng == "pe" and not d.dma and not dma:
                continue
            d.sig = True
            op.deps.append(d)
        self.ops[eng].append(op)
        return op

    def op(self, eng, method, reads=(), writes=(), **kw):
        return self.add(eng, lambda e: getattr(e, method)(**kw), reads, writes)

    def dma(self, eng, out, in_, reads=(), writes=()):
        return self.add(eng, lambda e: e.dma_start(out=out, in_=in_), reads, writes, dma=True)

    def fence(self, eng, reads):
        return self.add(eng, None, reads=reads, writes=())

    def flush(self):
        nc = self.nc
        sem_eng, sem_dma = self.sem_eng, self.sem_dma
        for e in ENGS:
            last = None
            for op in self.ops[e]:
                if not op.dma and op.fn is not None:
                    last = op
            if last is not None:
                last.sig = True
            for op in self.ops[e]:
                if op.dma:
                    k = self.dk[e]
                    op.slot = k % NSLOT
                    op.slotval = 16 * (k // NSLOT + 1)
                    self.dk[e] = k + 1
                elif op.sig and op.fn is not None:
                    self.cnt[e] += 1
                    op.val = self.cnt[e]
        targets = []
        for e in ENGS:
            if self.cnt[e] > 0:
                targets.append((e, sem_eng[e], self.cnt[e]))
            if e != "pe":
                k = self.dk[e]
                for sl in range(NSLOT):
                    n_used = (k - sl + NSLOT - 1) // NSLOT if k > sl else 0
                    if n_used > 0:
                        targets.append((None, sem_dma[e][sl], 16 * n_used))

        def emit_engine(ename, eng):
            seen = self.seen[ename]
            for op in self.ops[ename]:
                waits = {}
                for d in op.deps:
                    if d.dma:
                        key = sem_dma[d.eng][d.slot]
                        v = d.slotval
                    else:
                        key = sem_eng[d.eng]
                        v = d.val
                    if waits.get(key, 0) < v:
                        waits[key] = v
                if op.dma and op.slotval > 16:
                    key = sem_dma[ename][op.slot]
                    v = op.slotval - 16
                    if waits.get(key, 0) < v:
                        waits[key] = v
                for key, v in waits.items():
                    if seen.get(key, 0) >= v:
                        continue
                    seen[key] = v
                    eng.wait_ge(key, v)
                if op.fn is None:
                    continue
                ins = op.fn(eng)
                if op.dma:
                    ins.then_inc(sem_dma[ename][op.slot], 16)
                elif op.sig:
                    ins.then_inc(sem_eng[ename], 1)
            for (te, key, v) in targets:
                if te == ename:
                    continue
                if seen.get(key, 0) >= v:
                    continue
                seen[key] = v
                eng.wait_ge(key, v)

        with nc.Block() as block:
            @block.sync
            def _(eng):
                emit_engine("sp", eng)

            @block.scalar
            def _(eng):
                emit_engine("act", eng)

            @block.vector
            def _(eng):
                emit_engine("dve", eng)

            @block.gpsimd
            def _(eng):
                emit_engine("pool", eng)

            @block.tensor
            def _(eng):
                emit_engine("pe", eng)
        self.ops = {e: [] for e in ENGS}
        self.stack.close()
        self.stack = contextlib.ExitStack()
        self.batch += 1

    def emit(self):
        self.flush()
        self.gstack.close()


class Banks:
    def __init__(self, prog, names):
        self.tiles = [(n, prog.ps(n, [P, 512], F32)) for n in names]
        self.i = 0

    def next(self):
        t = self.tiles[self.i % len(self.tiles)]
        self.i += 1
        return t


def mm_group(prog, out_ap, bank_key, pairs, extra_reads=()):
    n = len(pairs)
    last = None
    for i, (l, r, rk) in enumerate(pairs):
        def fn(e, l=l, r=r, i=i):
            return e.matmul(out_ap, l, r, start=(i == 0), stop=(i == n - 1))
        last = prog.add("pe", fn, reads=tuple(rk) + tuple(extra_reads), writes=(bank_key,))
    return last


def rmsnorm_to_featmajor(prog, pe_banks_t, x_tile, xkey, g_col, hT, hT_key, col0, ident_bf, scr, ti):
    ss, rstd, xs = scr["ss"], scr["rstd"], scr["xs"]
    sfx = ti % 2
    ssk, rsk, xsk = ("ss", sfx), ("rstd", sfx), ("xs", sfx)
    ss_c = ss[:, sfx:sfx + 1]
    rs_c = rstd[:, sfx:sfx + 1]
    xs_t = xs[:, sfx, :]
    prog.add("act", lambda e: e.activation(out=xs_t, in_=x_tile, func=AF.Square, accum_out=ss_c),
             reads=(xkey,), writes=(xsk, ssk))
    prog.add("dve", lambda e: e.tensor_scalar(out=rs_c, in0=ss_c, scalar1=1.0 / D, scalar2=EPS,
                                              op0=ALU.mult, op1=ALU.add),
             reads=(ssk,), writes=(rsk,))
    prog.add("act", lambda e: e.activation(out=rs_c, in_=rs_c, func=AF.Sqrt), reads=(rsk,), writes=(rsk,))
    prog.add("dve", lambda e: e.reciprocal(out=rs_c, in_=rs_c), reads=(rsk,), writes=(rsk,))
    prog.add("act", lambda e: e.activation(out=xs_t, in_=x_tile, func=AF.Copy, scale=rs_c),
             reads=(xkey, rsk), writes=(xsk,))
    for q in range(4):
        bname, bt = pe_banks_t.next()
        for j in range(4):
            kc = q * 4 + j
            prog.add("pe", lambda e, kc=kc, j=j, bt=bt: e.transpose(
                out=bt[:, j * P:(j + 1) * P], in_=xs_t[:, kc * P:(kc + 1) * P], identity=ident_bf[:]),
                reads=(xsk, "ident"), writes=(bname,))
        for j in range(4):
            kc = q * 4 + j
            prog.add("dve", lambda e, kc=kc, j=j, bt=bt: e.tensor_scalar(
                out=hT[:, kc, col0:col0 + P], in0=bt[:, j * P:(j + 1) * P],
                scalar1=g_col[:, kc:kc + 1], scalar2=None, op0=ALU.mult),
                reads=(bname, "gcol"), writes=(hT_key,))


def build_k1():
    nc = bass.Bass("TRN2", target_bir_lowering=False)
    d = {
        "x": nc.dram_tensor("x", [TOK, D], F32, kind="ExternalInput").ap(),
        "w": nc.dram_tensor("w_in", [D, WX_COLS], F32, kind="ExternalInput").ap(),
        "gcol": nc.dram_tensor("gcol", [P, KC], F32, kind="ExternalInput").ap(),
        "cos": nc.dram_tensor("cosT", [P, TOK], F32, kind="ExternalInput").ap(),
        "sin": nc.dram_tensor("sinT", [P, TOK], F32, kind="ExternalInput").ap(),
        "ident": nc.dram_tensor("ident", [P, P], F32, kind="ExternalInput").ap(),
        "aq": nc.dram_tensor("aq", [1024, TOK], BF16, kind="ExternalOutput").ap(),
        "ak": nc.dram_tensor("ak", [256, TOK], BF16, kind="ExternalOutput").ap(),
        "mqk": nc.dram_tensor("mqk", [1024, TOK], F32, kind="ExternalOutput").ap(),
        "av": nc.dram_tensor("av", [TOK, 256], BF16, kind="ExternalOutput").ap(),
        "mv": nc.dram_tensor("mv", [TOK, 1024], BF16, kind="ExternalOutput").ap(),
        "smo": nc.dram_tensor("smo", [TOK, 1024], F32, kind="ExternalOutput").ap(),
        "gt": nc.dram_tensor("gt", [TOK, 16], F32, kind="ExternalOutput").ap(),
    }
    pr = Prog(nc)
    emit_k1(pr, d, TOK, False)
    pr.emit()
    return nc


def emit_k1(pr, d, ntok, gt_tiled, extra_dmas=None):
    extra_dmas = list(extra_dmas or [])
    x, w, gcol_d, cos_d, sin_d, ident_d = d["x"], d["w"], d["gcol"], d["cos"], d["sin"], d["ident"]
    aq_o, ak_o, mqk_o, av_o, mv_o, smo_o, gt_o = d["aq"], d["ak"], d["mqk"], d["av"], d["mv"], d["smo"], d["gt"]
    ident_bf = pr.sb("ident_bf", [P, P], BF16)
    gcol = pr.sb("gcol_sb", [P, KC], F32)
    cst = pr.sb("cs_sb", [P, 2, 2, 512], F32)
    xt = pr.sb("xt", [P, 2, D], F32)
    scr = {
        "ss": pr.sb("ss", [P, 2], F32),
        "rstd": pr.sb("rstd", [P, 2], F32),
        "xs": pr.sb("xs", [P, 2, D], BF16),
    }
    hT = pr.sb("hT", [P, KC, 512], BF16)
    NW = 3
    wt = pr.sb("wt", [P, NW, KC, 512], BF16)
    ev32 = pr.sb("ev32", [P, 4, 512], F32)
    evbf = pr.sb("evbf", [P, 4, 512], BF16)
    t1 = pr.sb("t1", [P, 2, 512], F32)
    banks = Banks(pr, ["pb%d" % i for i in range(6)])
    tb_t = [("pt%d" % i, pr.ps("pt%d" % i, [P, 1024], BF16)) for i in range(2)]

    class TB:
        i = 0

        def next(self):
            t = tb_t[self.i % 2]
            self.i += 1
            return t
    tbanks = TB()

    pr.dma("pool", ident_bf[:], ident_d, writes=("ident",))
    pr.dma("sp", gcol[:], gcol_d, writes=("gcol",))

    w_r = w.rearrange("(kc p) c -> p kc c", p=P)
    wcount = [0]
    evc = [0]
    outs = []

    def load_w(c0, n):
        slot = wcount[0] % NW
        wcount[0] += 1
        key = ("wt", slot)
        pr.dma("pool", wt[:, slot, :, 0:n], w_r[:, :, c0:c0 + n], writes=(key,))
        if extra_dmas:
            dst, src, k = extra_dmas.pop(0)
            pr.dma("pool", dst, src, writes=(k,))
        return slot, key

    def ev_slot():
        s = evc[0] % 4
        evc[0] += 1
        return s

    for tb in range(ntok // 512):
        t0 = tb * 512
        csl = tb % 2
        pr.dma("sp", cst[:, csl, 0, :], cos_d[:, t0:t0 + 512], writes=(("cos", csl),))
        pr.dma("sp", cst[:, csl, 1, :], sin_d[:, t0:t0 + 512], writes=(("sin", csl),))
        for ti in range(4):
            g_ti = tb * 4 + ti
            xs_ = g_ti % 2
            xkey = ("xt", xs_)
            pr.dma("sp", xt[:, xs_, :], x[g_ti * P:(g_ti + 1) * P, :], writes=(xkey,))
            rmsnorm_to_featmajor(pr, tbanks, xt[:, xs_, :], xkey, gcol, hT, ("hT", ti), ti * P,
                                 ident_bf, scr, g_ti)
        hkeys = tuple(("hT", ti) for ti in range(4))

        for grp, (c0g, nh) in enumerate(((0, 4), (512, 4), (1024, 2))):
            slot, wkey = load_w(c0g, nh * P)
            for hl in range(nh):
                hh = grp * 4 + hl
                na, ba = banks.next()
                mm_group(pr, ba[:], na, [(wt[:, slot, kc, hl * P:(hl + 1) * P], hT[:, kc, :], hkeys + (wkey,))
                                         for kc in range(KC)])
                es = ev_slot()
                cs_ap = cst[:, csl, 0, :]
                sn_ap = cst[:, csl, 1, :]
                pr.op("dve", "tensor_tensor", reads=(na, ("cos", csl)), writes=("t1a",), out=t1[:, 0, :], in0=ba[:],
                      in1=cs_ap, op=ALU.mult)
                pr.op("dve", "tensor_tensor", reads=(na, ("sin", csl)), writes=("t1b",), out=t1[0:64, 1, :],
                      in0=ba[64:128, :], in1=sn_ap[0:64, :], op=ALU.mult)
                pr.op("dve", "tensor_tensor", reads=(na, ("sin", csl), "t1b"), writes=("t1b",), out=t1[64:128, 1, :],
                      in0=ba[0:64, :], in1=sn_ap[64:128, :], op=ALU.mult)
                pr.op("dve", "tensor_tensor", reads=("t1a", "t1b"), writes=(("evbf", es),), out=evbf[:, es, :],
                      in0=t1[:, 0, :], in1=t1[:, 1, :], op=ALU.add)
                dst = aq_o[hh * P:(hh + 1) * P, t0:t0 + 512] if hh < 8 else ak_o[(hh - 8) * P:(hh - 7) * P, t0:t0 + 512]
                outs.append(pr.dma("sp", dst, evbf[:, es, :], reads=(("evbf", es),), writes=(("out", len(outs)),)))

        for grp in range(2):
            slot, wkey = load_w(1280 + grp * 512, 512)
            for j in range(4):
                nb_, bt = banks.next()
                mm_group(pr, bt[:], nb_, [(wt[:, slot, kc, j * P:(j + 1) * P], hT[:, kc, :], hkeys + (wkey,))
                                          for kc in range(KC)])
                es = ev_slot()
                pr.add("act", lambda e, bt=bt, es=es: e.copy(out=ev32[:, es, :], in_=bt[:]),
                       reads=(nb_,), writes=(("ev32", es),))
                r0 = grp * 512 + j * P
                outs.append(pr.dma("sp", mqk_o[r0:r0 + P, t0:t0 + 512], ev32[:, es, :],
                                   reads=(("ev32", es),), writes=(("out", len(outs)),)))

        tm_groups = [
            (2304, 272, "avg"),
            (2576, 512, "mv0"), (3088, 512, "mv1"),
            (3600, 512, "mo0"), (4112, 512, "mo1"),
        ]
        for c0, ncols, kind in tm_groups:
            slot, wkey = load_w(c0, ncols)
            for ti in range(4):
                r0 = t0 + ti * P
                nb_, bt = banks.next()
                mm_group(pr, bt[:, 0:ncols], nb_,
                         [(hT[:, kc, ti * P:(ti + 1) * P], wt[:, slot, kc, 0:ncols], (("hT", ti), wkey))
                          for kc in range(KC)])
                es = ev_slot()
                if kind == "avg":
                    pr.add("act", lambda e, bt=bt, es=es: e.copy(out=evbf[:, es, 0:256], in_=bt[:, 0:256]),
                           reads=(nb_,), writes=(("evbf", es),))
                    pr.add("act", lambda e, bt=bt, es=es: e.copy(out=ev32[:, es, 0:16], in_=bt[:, 256:272]),
                           reads=(nb_,), writes=(("ev32", es),))
                    outs.append(pr.dma("sp", av_o[r0:r0 + P, :], evbf[:, es, 0:256],
                                       reads=(("evbf", es),), writes=(("out", len(outs)),)))
                    gt_dst = gt_o[:, r0 // P, :] if gt_tiled else gt_o[r0:r0 + P, :]
                    outs.append(pr.dma("sp", gt_dst, ev32[:, es, 0:16],
                                       reads=(("ev32", es),), writes=(("out", len(outs)),)))
                elif kind.startswith("mv"):
                    c = int(kind[2]) * 512
                    pr.add("act", lambda e, bt=bt, es=es: e.copy(out=evbf[:, es, :], in_=bt[:]),
                           reads=(nb_,), writes=(("evbf", es),))
                    outs.append(pr.dma("sp", mv_o[r0:r0 + P, c:c + 512], evbf[:, es, :],
                                       reads=(("evbf", es),), writes=(("out", len(outs)),)))
                else:
                    c = int(kind[2]) * 512
                    pr.add("act", lambda e, bt=bt, es=es: e.activation(out=ev32[:, es, :], in_=bt[:], func=AF.Sigmoid),
                           reads=(nb_,), writes=(("ev32", es),))
                    outs.append(pr.dma("sp", smo_o[r0:r0 + P, c:c + 512], ev32[:, es, :],
                                       reads=(("ev32", es),), writes=(("out", len(outs)),)))

    for dst, src, k in extra_dmas:
        pr.dma("pool", dst, src, writes=(k,))
    pr.flush()


def rstd_chain(pr, ss_c, ssk, rs_c, rsk, n):
    pr.add("dve", lambda e: e.tensor_scalar(out=rs_c, in0=ss_c, scalar1=1.0 / n, scalar2=EPS,
                                            op0=ALU.mult, op1=ALU.add), reads=(ssk,), writes=(rsk,))
    pr.add("act", lambda e: e.activation(out=rs_c, in_=rs_c, func=AF.Sqrt), reads=(rsk,), writes=(rsk,))
    pr.add("dve", lambda e: e.reciprocal(out=rs_c, in_=rs_c), reads=(rsk,), writes=(rsk,))


def build_k3():
    nc = bass.Bass("TRN2", target_bir_lowering=False)
    d = {
        "mixT": nc.dram_tensor("mixT", [D, TOK], BF16, kind="ExternalInput").ap(),
        "x": nc.dram_tensor("x", [TOK, D], F32, kind="ExternalInput").ap(),
        "w_out": nc.dram_tensor("w_out", [D, D], F32, kind="ExternalInput").ap(),
        "w_up": nc.dram_tensor("w_up", [D, DFF], F32, kind="ExternalInput").ap(),
        "w_down": nc.dram_tensor("w_down", [DFF, D], F32, kind="ExternalInput").ap(),
        "g_pm": nc.dram_tensor("g_pm", [P, D], F32, kind="ExternalInput").ap(),
        "g_pl": nc.dram_tensor("g_pl", [P, D], F32, kind="ExternalInput").ap(),
        "gcol": nc.dram_tensor("gcol", [P, KC], F32, kind="ExternalInput").ap(),
        "ident": nc.dram_tensor("ident", [P, P], F32, kind="ExternalInput").ap(),
        "y": nc.dram_tensor("y", [TOK, D], F32, kind="ExternalOutput").ap(),
    }
    mixT_r = d["mixT"].rearrange("(kc p) t -> p kc t", p=P)
    d["mix_blk"] = lambda tb: mixT_r[:, :, tb * 512:(tb + 1) * 512]
    pr = Prog(nc)
    emit_k3(pr, d, TOK)
    pr.emit()
    return nc


def emit_k3(pr, d, ntok):
    x, w_out, w_up, w_down = d["x"], d["w_out"], d["w_up"], d["w_down"]
    gat = d.get("gather")
    gpm_d, gpl_d, gcol_d, ident_d, y_o = d["g_pm"], d["g_pl"], d["gcol"], d["ident"], d["y"]
    ident_bf = pr.sb("ident_bf", [P, P], BF16)
    gcol = pr.sb("gcol_sb", [P, KC], F32)
    gpm = pr.sb("gpm_sb", [P, D], F32)
    gpl = pr.sb("gpl_sb", [P, D], F32)
    scr = {
        "ss": pr.sb("ss", [P, 2], F32),
        "rstd": pr.sb("rstd", [P, 2], F32),
        "xs": pr.sb("xs", [P, 2, D], BF16),
    }
    ss4 = pr.sb("ss4", [P, 4, 4], F32)
    ssr = pr.sb("ssr", [P, 4], F32)
    rs2 = pr.sb("rs2", [P, 4], F32)
    mixb = pr.sb("mixb", [P, KC, 512], BF16)
    h2T = pr.sb("h2T", [P, KC, 512], BF16)
    NW = 3
    wt = pr.sb("wt", [P, NW, KC, 512], BF16)
    x1b = pr.sb("x1b", [P, 4, D], F32)
    yb = pr.sb("yb", [P, 4, D], F32)
    uT = pr.sb("uT", [P, 32, 512], BF16)
    r32 = pr.sb("r32", [P, 2, 512], F32)
    pa = Banks(pr, ["pa0", "pa1"])
    pd = [("pd%d" % i, pr.ps("pd%d" % i, [P, 512], F32)) for i in range(4)]
    tb_t = [("pt%d" % i, pr.ps("pt%d" % i, [P, 1024], BF16)) for i in range(2)]

    class TB:
        i = 0

        def next(self):
            t = tb_t[self.i % 2]
            self.i += 1
            return t
    tbanks = TB()

    pr.dma("pool", ident_bf[:], ident_d, writes=("ident",))
    pr.dma("sp", gcol[:], gcol_d, writes=("gcol",))
    pr.dma("sp", gpm[:], gpm_d, writes=("gpm",))
    pr.dma("sp", gpl[:], gpl_d, writes=("gpl",))

    w_out_r = w_out.rearrange("(kc p) c -> p kc c", p=P)
    w_up_r = w_up.rearrange("(kc p) c -> p kc c", p=P)
    w_down_r = w_down.rearrange("(fc p) c -> p fc c", p=P)
    mixkeys = tuple(("mixb", kc) for kc in range(KC))
    if gat is not None:
        I32 = mybir.dt.int32
        xidx = pr.sb("xidx", [P, ntok // P], I32)
        midx = pr.sb("midx", [P, (ntok // 512) * KC], I32)
        pr.dma("sp", xidx[:], gat["xidx"], writes=("xidx",))
        pr.dma("sp", midx[:], gat["midx"], writes=("midx",))
    wcount = [0]
    rc = [0]
    outs = []

    def load_w(src):
        slot = wcount[0] % NW
        wcount[0] += 1
        key = ("wt", slot)
        pr.dma("pool", wt[:, slot, :, :], src, writes=(key,))
        return slot, key

    def r32_slot():
        s = rc[0] % 2
        rc[0] += 1
        return s

    def sumsq(src_ap, src_key, acc_ap, acc_key):
        rs = r32_slot()
        pr.add("act", lambda e: e.activation(out=r32[:, rs, :], in_=src_ap, func=AF.Square, accum_out=acc_ap),
               reads=(src_key,), writes=(("r32", rs), acc_key))

    def norm_residual(ti, g_sb, gkey, base_ap, base_key, dst_ap, dst_key):
        ss_c = ssr[:, ti:ti + 1]
        rs_c = rs2[:, ti:ti + 1]
        ybk = [("yb", ti, cb) for cb in range(4)]
        pr.add("dve", lambda e: e.reduce_sum(out=ss_c, in_=ss4[:, ti, :], axis=AX.X),
               reads=tuple(("ss4", ti, cb) for cb in range(4)), writes=(("ssr", ti),))
        rstd_chain(pr, ss_c, ("ssr", ti), rs_c, ("rs2", ti), D)
        pr.add("dve", lambda e: e.scalar_tensor_tensor(out=yb[:, ti, :], in0=yb[:, ti, :], scalar=rs_c, in1=g_sb[:],
                                                       op0=ALU.mult, op1=ALU.mult),
               reads=tuple(ybk) + (("rs2", ti), gkey), writes=tuple(ybk))
        pr.add("dve", lambda e: e.tensor_tensor(out=dst_ap, in0=base_ap, in1=yb[:, ti, :], op=ALU.add),
               reads=tuple(ybk) + (base_key,), writes=(dst_key,))

    for tb in range(ntok // 512):
        t0 = tb * 512
        if gat is None:
            pr.dma("sp", mixb[:], d["mix_blk"](tb), writes=mixkeys)
            for ti in range(4):
                pr.dma("sp", x1b[:, ti, :], x[t0 + ti * P:t0 + (ti + 1) * P, :], writes=(("x1b", ti),))
        else:
            for kc in range(KC):
                pr.add("pool", lambda e, kc=kc, col=tb * KC + kc: e.indirect_dma_start(
                    out=mixb[:, kc, :], out_offset=None, in_=gat["mix_flat"],
                    in_offset=bass.IndirectOffsetOnAxis(midx[:, col:col + 1], 0)),
                    reads=("midx",), writes=(("mixb", kc),), dma=True)
            for ti in range(4):
                pr.add("pool", lambda e, ti=ti, col=tb * 4 + ti: e.indirect_dma_start(
                    out=x1b[:, ti, :], out_offset=None, in_=x,
                    in_offset=bass.IndirectOffsetOnAxis(xidx[:, col:col + 1], 0)),
                    reads=("xidx",), writes=(("x1b", ti),), dma=True)
        for cb in range(4):
            slot, wkey = load_w(w_out_r[:, :, cb * 512:(cb + 1) * 512])
            for ti in range(4):
                nb_, bt = pa.next()
                mm_group(pr, bt[:], nb_, [(mixb[:, kc, ti * P:(ti + 1) * P], wt[:, slot, kc, :], (("mixb", kc), wkey))
                                          for kc in range(KC)])
                ypiece = yb[:, ti, cb * 512:(cb + 1) * 512]
                pr.add("act", lambda e, bt=bt, ypiece=ypiece: e.copy(out=ypiece, in_=bt[:]),
                       reads=(nb_,), writes=(("yb", ti, cb),))
                sumsq(ypiece, ("yb", ti, cb), ss4[:, ti, cb:cb + 1], ("ss4", ti, cb))
        for ti in range(4):
            norm_residual(ti, gpm, "gpm", x1b[:, ti, :], ("x1b", ti), x1b[:, ti, :], ("x1b", ti))
            rmsnorm_to_featmajor(pr, tbanks, x1b[:, ti, :], ("x1b", ti), gcol, h2T, ("h2T", ti), ti * P,
                                 ident_bf, scr, ti)
        hkeys = tuple(("h2T", ti) for ti in range(4))

        for hf in range(2):
            for fgl in range(8):
                fg = hf * 8 + fgl
                slot, wkey = load_w(w_up_r[:, :, fg * 512:(fg + 1) * 512])
                for j in range(4):
                    fcl = fgl * 4 + j
                    nb_, bt = pa.next()
                    mm_group(pr, bt[:], nb_, [(wt[:, slot, kc, j * P:(j + 1) * P], h2T[:, kc, :], hkeys + (wkey,))
                                              for kc in range(KC)])
                    rs = r32_slot()
                    pr.add("act", lambda e, bt=bt, rs=rs: e.activation(out=r32[:, rs, :], in_=bt[:], func=AF.Relu),
                           reads=(nb_,), writes=(("r32", rs),))
                    sq_eng = "dve"
                    pr.add(sq_eng, lambda e, rs=rs, fcl=fcl: e.tensor_tensor(out=uT[:, fcl, :], in0=r32[:, rs, :],
                                                                            in1=r32[:, rs, :], op=ALU.mult),
                           reads=(("r32", rs),), writes=(("uT", fcl),))
            for cb in range(4):
                for qi in range(2):
                    q = hf * 2 + qi
                    slot, wkey = load_w(w_down_r[:, q * 16:(q + 1) * 16, cb * 512:(cb + 1) * 512])
                    for ti in range(4):
                        pname, pt = pd[ti]
                        for f16 in range(16):
                            fcl = qi * 16 + f16
                            pr.add("pe", lambda e, pt=pt, fcl=fcl, f16=f16, ti=ti, slot=slot, qi=qi: e.matmul(
                                pt[:], uT[:, fcl, ti * P:(ti + 1) * P], wt[:, slot, f16, :],
                                start=(qi == 0 and f16 == 0), stop=(qi == 1 and f16 == 15)),
                                reads=(("uT", fcl), wkey), writes=(pname,))
                for ti in range(4):
                    pname, pt = pd[ti]
                    ypiece = yb[:, ti, cb * 512:(cb + 1) * 512]
                    if hf == 0:
                        pr.add("act", lambda e, pt=pt, ypiece=ypiece: e.copy(out=ypiece, in_=pt[:]),
                               reads=(pname,), writes=(("yb", ti, cb),))
                    else:
                        pr.add("dve", lambda e, pt=pt, ypiece=ypiece: e.tensor_tensor(out=ypiece, in0=ypiece, in1=pt[:],
                                                                                      op=ALU.add),
                               reads=(pname, ("yb", ti, cb)), writes=(("yb", ti, cb),))
                        sumsq(ypiece, ("yb", ti, cb), ss4[:, ti, cb:cb + 1], ("ss4", ti, cb))
        for ti in range(4):
            norm_residual(ti, gpl, "gpl", x1b[:, ti, :], ("x1b", ti), yb[:, ti, :], ("ybo", ti))
            outs.append(pr.dma("sp", y_o[t0 + ti * P:t0 + (ti + 1) * P, :], yb[:, ti, :],
                               reads=(("ybo", ti),) + tuple(("yb", ti, cb) for cb in range(4)),
                               writes=(("out", len(outs)),)))

    pr.flush()


def run_k3(mixT_cores, x_flat, w_out_l, w_up_l, w_down_l, g_pm, g_pre_mlp, g_pl):
    nc = _get("k3", build_k3)
    gcol = np.ascontiguousarray(g_pre_mlp.reshape(KC, P).T)
    gpm = np.ascontiguousarray(np.broadcast_to(g_pm[None, :], (P, D)))
    gpl = np.ascontiguousarray(np.broadcast_to(g_pl[None, :], (P, D)))
    ident = np.eye(P, dtype=np.float32)
    in_maps = []
    for c in range(NCORE):
        in_maps.append({
            "mixT": mixT_cores[c],
            "x": np.ascontiguousarray(x_flat[c * TOK:(c + 1) * TOK]),
            "w_out": w_out_l, "w_up": w_up_l, "w_down": w_down_l,
            "g_pm": gpm, "g_pl": gpl, "gcol": gcol, "ident": ident,
        })
    res = run_bass_kernel_spmd(nc, in_maps, core_ids=list(range(NCORE)), **_RUNKW)
    _LAST["t"] = res.exec_time_ns
    return [r["y"] for r in res.results]


NCH = S // P
ATT_SCALE = float(128 ** -0.5)
QK_SCALE = float(128 ** -0.5)


def build_k2():
    nc = bass.Bass("TRN2", target_bir_lowering=False)

    def din(name, shape, dt):
        return nc.dram_tensor(name, list(shape), dt, kind="ExternalInput").ap()
    v = {
        "aq4": din("aq2", [P, NCH * 256], BF16).rearrange("p (n h t) -> p n h t", h=2, t=P),
        "ak": din("ak", [P, S], BF16),
        "av3": din("av3", [P, NCH * P], BF16).rearrange("p (n d) -> p n d", d=P),
        "mq": din("mq", [P, S], F32),
        "mk": din("mk", [P, S], F32),
        "mv3": din("mv3", [P, NCH * 256], BF16).rearrange("p (n e) -> p n e", e=256),
        "smo3": din("smo3", [P, NCH * 256], F32).rearrange("p (n e) -> p n e", e=256),
        "g4": din("g4", [P, 4 * NCH], F32).rearrange("p (g n) -> p g n", g=4),
        "gb4": din("gb4", [P, 4], F32),
        "cw": din("cw", [P, 6], F32),
        "gn": din("gn", [P, 256], F32),
        "sink2": din("sink2", [P, 256], F32),
        "identf": din("identf", [P, P], F32),
        "le": din("le", [P, P], F32),
        "ge": din("ge", [P, P], F32),
        "nmp": din("nmp", [P, 256], F32),
        "nmn": din("nmn", [P, 256], F32),
        "attT_r": nc.dram_tensor("attT", [256, S], BF16, kind="ExternalOutput").ap().rearrange("(h d) t -> d h t", h=2),
        "memT_r": nc.dram_tensor("memT", [256, S], BF16, kind="ExternalOutput").ap().rearrange("(h e) t -> e h t", h=2),
        "hb": nc.dram_tensor("hb_scratch", [NCH, P, 256], F32).ap(),
        "hfb": nc.dram_tensor("hfb_scratch", [2, NCH, P, 258], F32).ap(),
    }
    v["att_dst"] = lambda n: v["attT_r"][:, :, n * P:(n + 1) * P]
    v["mem_dst"] = lambda n: v["memT_r"][:, :, n * P:(n + 1) * P]
    pr = Prog(nc)
    emit_k2(pr, v, None)
    pr.emit()
    return nc


def emit_k2_v1(pr, v, gate_j):
    ak_d, mq_d, mk_d = v["ak"], v["mq"], v["mk"]
    gb4_d, cw_d, gn_d, sink_d = v["gb4"], v["cw"], v["gn"], v["sink2"]
    identf_d, le_d, ge_d, nmp_d, nmn_d = v["identf"], v["le"], v["ge"], v["nmp"], v["nmn"]
    hb_d = v["hb"]

    def sb(name, shape, dt):
        return pr.sb(name + "_s", shape, dt)
    nc = pr.nc
    identf = sb("identf", [P, P], F32)
    ident_bf = sb("ident_bf", [P, P], BF16)
    onesf = sb("onesf", [P, P], F32)
    ones_bf = sb("ones_bf", [P, P], BF16)
    le = sb("le", [P, P], F32)
    ge = sb("ge", [P, P], F32)
    nmp = sb("nmp", [P, 256], BF16)
    nmn = sb("nmn", [P, 256], BF16)
    gb4 = sb("gb4", [P, 4], F32)
    negb = sb("negb", [P, 4], F32)
    cw = sb("cw", [P, 6], F32)
    gn = sb("gn", [P, 256], F32)
    esink = sb("esink", [P, 256], F32)
    g4 = sb("g4", [P, 4, NCH], F32)
    li = sb("li", [P, 2, NCH], F32)
    lf = sb("lf", [P, 2, NCH], F32)
    cg = sb("cg", [P, 4, NCH], F32)
    ecol = sb("ecol", [P, 2, NCH], F32)
    wend = sb("wend", [P, 2, NCH], F32)
    eg = sb("eg", [P, 2, NCH], F32)
    gtmp = sb("gtmp", [P, 2, NCH], F32)
    ak = sb("ak", [P, S], BF16)
    av3 = sb("av3", [P, NCH, P], BF16)
    kT = sb("kT", [P, S], BF16)
    qs = [sb("qs_f", [P, S], BF16), sb("qs_b", [P, S], BF16)]
    kw = [sb("kw_f", [P, NCH, P], BF16), sb("kw_b", [P, NCH, P], BF16)]
    vext = sb("vext", [P, NCH, 258], BF16)
    PSZ = 1024
    NPC = S // PSZ
    xq = sb("xq", [P, PSZ + 2], F32)
    yq = sb("yq", [P, PSZ], F32)
    xk = sb("xk", [P, PSZ + 2], F32)
    yk = sb("yk", [P, PSZ], F32)
    lfbc = sb("lfbc", [P, 4, P], F32)
    eb = sb("eb", [P, 2, 512], F32)
    cst = [sb("C_f", [P, 257], F32), sb("C_b", [P, 257], F32)]
    cbf = [sb("Cbf_f", [P, 257], BF16), sb("Cbf_b", [P, 257], BF16)]
    ws = sb("ws", [P, 2, P], BF16)
    rr = sb("rr", [P, 4], F32)
    hbt = sb("hbt", [P, 2, 256], F32)
    hsum = sb("hsum", [P, 2, 256], F32)
    smo_t = sb("smo_t", [P, 2, 256], F32)
    obf = sb("obf", [P, 2, 256], BF16)
    junk = sb("junk", [P, 256], F32)
    ssn = sb("ssn", [P, 2], F32)
    rsn = sb("rsn", [P, 2], F32)
    mst = sb("mst", [P, 2, 256], BF16)
    qblk = sb("qblk", [P, 2, 256], BF16)
    pT_sb = sb("pT_sb", [P, 2, 3, 256], BF16)
    dent = sb("dent", [P, 2, 256], F32)
    ast = sb("ast", [P, 2, 256], BF16)

    pS = pr.ps("pS", [P, 512], F32)
    pH = [pr.ps("pH0", [P, 512], F32), pr.ps("pH1", [P, 512], F32)]
    pC = pr.ps("pC", [P, 512], F32)
    pT = pr.ps("pT", [P, 1024], BF16)
    pA = [pr.ps("pA0", [P, 512], F32), pr.ps("pA1", [P, 512], F32)]
    pO = pr.ps("pO", [P, 512], F32)

    pr.dma("sp", identf[:], identf_d, writes=("identf",))
    pr.dma("pool", ident_bf[:], identf_d, writes=("ident",))
    pr.dma("sp", le[:], le_d, writes=("le",))
    pr.dma("sp", ge[:], ge_d, writes=("ge",))
    pr.dma("pool", nmp[:], nmp_d, writes=("nmp",))
    pr.dma("pool", nmn[:], nmn_d, writes=("nmn",))
    pr.dma("sp", gb4[:], gb4_d, writes=("gb4",))
    pr.dma("sp", cw[:], cw_d, writes=("cw",))
    pr.dma("sp", gn[:], gn_d, writes=("gn",))
    pr.dma("sp", esink[:], sink_d, writes=("esink",))
    if gate_j is None:
        pr.dma("sp", g4[:], v["g4"], writes=("g4",))
    else:
        gall = sb("gall", [P, NCH, 16], F32)
        pr.dma("sp", gall[:], v["gall"], writes=("gall",))
        for g in range(4):
            pr.op("dve", "tensor_copy", reads=("gall",), writes=("g4",), out=g4[:, g, :],
                  in_=gall[:, :, g * 4 + gate_j])
    pr.dma("sp", ak[:], ak_d, writes=("ak",))
    pr.dma("sp", av3[:], v["av3"], writes=("av3",))
    pr.dma("sp", vext[:, :, 0:256], v["mv3"], writes=("vext",))
    pr.op("dve", "memset", writes=("vext1",), ap=vext[:, :, 256:257], constant=1.0)
    pr.op("dve", "memset", writes=("onesf",), ap=onesf[:], constant=1.0)
    pr.op("dve", "memset", writes=("ones_bf",), ap=ones_bf[:], constant=1.0)
    for di in range(2):
        pr.op("dve", "memset", writes=(("C", di),), ap=cst[di][:], constant=0.0)
        pr.op("dve", "memset", writes=(("Cbf", di),), ap=cbf[di][:], constant=0.0)
    pr.op("act", "activation", reads=("esink",), writes=("esink",), out=esink[:], in_=esink[:], func=AF.Exp)

    pr.op("dve", "tensor_scalar", reads=("gb4",), writes=("negb",), out=negb[:], in0=gb4[:], scalar1=-1.0,
          scalar2=None, op0=ALU.mult)
    for di in range(2):
        pr.op("dve", "tensor_scalar", reads=("g4", "gb4"), writes=(("li", di),), out=li[:, di, :],
              in0=g4[:, 2 * di, :], scalar1=gb4[:, 2 * di:2 * di + 1], scalar2=None, op0=ALU.add)
        pr.op("act", "activation", reads=("g4", "negb"), writes=(("gtmp", di),), out=gtmp[:, di, :],
              in_=g4[:, 2 * di + 1, :], func=AF.Exp, scale=-1.0, bias=negb[:, 2 * di + 1:2 * di + 2])
        pr.op("act", "activation", reads=(("gtmp", di),), writes=(("gtmp", di),), out=gtmp[:, di, :],
              in_=gtmp[:, di, :], func=AF.Ln, bias=1.0)
        pr.op("dve", "tensor_scalar", reads=(("gtmp", di),), writes=(("lf", di),), out=lf[:, di, :],
              in0=gtmp[:, di, :], scalar1=-1.0, scalar2=None, op0=ALU.mult)
    for di in range(2):
        tri = le if di == 0 else ge
        trik = "le" if di == 0 else "ge"
        pr.op("pe", "matmul", reads=(("lf", di), trik), writes=("pA0",), out=pA[0][:, di * 128:di * 128 + 64],
              lhsT=tri[:], rhs=lf[:, di, :], start=True, stop=True)
        pr.op("pe", "matmul", reads=(("lf", di), "onesf"), writes=("pA0",), out=pA[0][:, di * 128 + 64:di * 128 + 128],
              lhsT=onesf[:], rhs=lf[:, di, :], start=True, stop=True)
    pr.op("act", "copy", reads=("pA0",), writes=("cg",), out=cg[:].rearrange("p g n -> p (g n)"), in_=pA[0][:, 0:256])
    for di in range(2):
        bc = cg[:, 2 * di, :]
        gt_ = cg[:, 2 * di + 1, :]
        pr.op("dve", "tensor_tensor", reads=(("li", di), "cg"), writes=(("gtmp", di),), out=gtmp[:, di, :],
              in0=li[:, di, :], in1=bc, op=ALU.subtract)
        pr.op("act", "activation", reads=(("gtmp", di),), writes=(("ecol", di),), out=ecol[:, di, :],
              in_=gtmp[:, di, :], func=AF.Exp)
        pr.op("dve", "tensor_tensor", reads=(("gtmp", di), "cg"), writes=(("gtmp", di),), out=gtmp[:, di, :],
              in0=gtmp[:, di, :], in1=gt_, op=ALU.add)
        pr.op("act", "activation", reads=(("gtmp", di),), writes=(("wend", di),), out=wend[:, di, :],
              in_=gtmp[:, di, :], func=AF.Exp)
        pr.op("act", "activation", reads=("cg",), writes=(("eg", di),), out=eg[:, di, :], in_=gt_, func=AF.Exp)

    def conv_piece(src_d, xb, xkey, yb_, ykey, w0, pc):
        p0 = pc * PSZ
        lo = p0 - 1 if pc > 0 else 0
        hi = p0 + PSZ + 1 if pc < NPC - 1 else S
        c_lo = 0 if pc > 0 else 1
        if pc == 0:
            pr.op("dve", "memset", writes=(xkey,), ap=xb[:, 0:1], constant=0.0)
        if pc == NPC - 1:
            pr.op("dve", "memset", writes=(xkey,), ap=xb[:, PSZ + 1:PSZ + 2], constant=0.0)
        pr.dma("sp", xb[:, c_lo:c_lo + (hi - lo)], src_d[:, lo:hi], writes=(xkey,))
        pr.op("dve", "tensor_scalar", reads=(xkey, "cw"), writes=(ykey,), out=yb_[:], in0=xb[:, 1:PSZ + 1],
              scalar1=cw[:, w0 + 1:w0 + 2], scalar2=None, op0=ALU.mult)
        pr.op("dve", "scalar_tensor_tensor", reads=(xkey, "cw", ykey), writes=(ykey,), out=yb_[:], in0=xb[:, 0:PSZ],
              scalar=cw[:, w0:w0 + 1], in1=yb_[:], op0=ALU.mult, op1=ALU.add)
        pr.op("dve", "scalar_tensor_tensor", reads=(xkey, "cw", ykey), writes=(ykey,), out=yb_[:], in0=xb[:, 2:PSZ + 2],
              scalar=cw[:, w0 + 2:w0 + 3], in1=yb_[:], op0=ALU.mult, op1=ALU.add)
        pr.op("act", "activation", reads=(ykey,), writes=(ykey,), out=yb_[:], in_=yb_[:], func=AF.Silu)

    ebc = [0]
    for pc in range(NPC):
        p0 = pc * PSZ
        conv_piece(mq_d, xq, "xq", yq, "yq", 0, pc)
        conv_piece(mk_d, xk, "xk", yk, "yk", 3, pc)
        pr.op("act", "copy", reads=("yk",), writes=("kT",), out=kT[:, p0:p0 + PSZ], in_=yk[:])
        for sp_ in range(PSZ // 512):
            for di in range(2):
                tri = le if di == 0 else ge
                trik = "le" if di == 0 else "ge"
                bank = pA[ebc[0] % 2]
                bkey = "pA%d" % (ebc[0] % 2)
                es = ebc[0] % 2
                ebc[0] += 1
                for c4 in range(4):
                    n = pc * (PSZ // P) + sp_ * 4 + c4
                    pr.op("dve", "tensor_scalar", reads=("onesf", ("lf", di)), writes=(("lfbc", c4),), out=lfbc[:, c4, :],
                          in0=onesf[:], scalar1=lf[:, di, n:n + 1], scalar2=None, op0=ALU.mult)
                    pr.op("pe", "matmul", reads=(("lfbc", c4), trik), writes=(bkey,), out=bank[:, c4 * P:(c4 + 1) * P],
                          lhsT=lfbc[:, c4, :], rhs=tri[:], start=True, stop=True)
                pr.op("act", "activation", reads=(bkey,), writes=(("eb", es),), out=eb[:, es, :], in_=bank[:], func=AF.Exp)
                t0 = p0 + sp_ * 512
                pr.op("dve", "scalar_tensor_tensor", reads=("yq", ("eb", es)), writes=(("qs", di),),
                      out=qs[di][:, t0:t0 + 512], in0=yq[:, sp_ * 512:(sp_ + 1) * 512], scalar=QK_SCALE, in1=eb[:, es, :],
                      op0=ALU.mult, op1=ALU.mult)
        for c16 in range(PSZ // P):
            n = pc * (PSZ // P) + c16
            pr.op("pe", "transpose", reads=("yk", "identf"), writes=("pO",), out=pO[:, 0:P],
                  in_=yk[:, c16 * P:(c16 + 1) * P], identity=identf[:])
            for di in range(2):
                pr.op("act", "activation", reads=("pO", ("wend", di)), writes=(("kw", di),), out=kw[di][:, n, :],
                      in_=pO[:, 0:P], func=AF.Copy, scale=wend[:, di, n:n + 1])

    outs = []

    def mlstm_common(di, n):
        tok = slice(n * P, (n + 1) * P)
        msk = le if di == 0 else ge
        mskk = "le" if di == 0 else "ge"
        bank = pH[n % 2]
        bkey = "pH%d" % (n % 2)
        pr.op("pe", "matmul", reads=("kT", ("qs", di)), writes=("pS",), out=pS[:, 0:P], lhsT=kT[:, tok],
              rhs=qs[di][:, tok], start=True, stop=True)
        pr.op("dve", "scalar_tensor_tensor", reads=("pS", ("ecol", di), mskk), writes=(("ws", di),), out=ws[:, di, :],
              in0=pS[:, 0:P], scalar=ecol[:, di, n:n + 1], in1=msk[:], op0=ALU.mult, op1=ALU.mult)
        pr.op("pe", "matmul", reads=(("ws", di), "vext", "vext1"), writes=(bkey,), out=bank[:, 0:257], lhsT=ws[:, di, :],
              rhs=vext[:, n, 0:257], start=True, stop=False)
        pr.op("pe", "matmul", reads=(("qs", di), ("Cbf", di)), writes=(bkey,), out=bank[:, 0:257], lhsT=qs[di][:, tok],
              rhs=cbf[di][:], start=False, stop=True)
        pr.op("pe", "matmul", reads=(("kw", di), "vext", "vext1"), writes=("pC",), out=pC[:, 0:257], lhsT=kw[di][:, n, :],
              rhs=vext[:, n, 0:257], start=True, stop=True)
        pr.op("dve", "scalar_tensor_tensor", reads=("pC", ("eg", di), ("C", di)), writes=(("C", di),), out=cst[di][:],
              in0=cst[di][:], scalar=eg[:, di, n:n + 1], in1=pC[:, 0:257], op0=ALU.mult, op1=ALU.add)
        pr.op("act", "copy", reads=(("C", di),), writes=(("Cbf", di),), out=cbf[di][:], in_=cst[di][:])
        rc = rr[:, di:di + 1]
        pr.op("act", "activation", reads=(bkey,), writes=(("rr", di),), out=rc, in_=bank[:, 256:257], func=AF.Abs)
        pr.op("dve", "tensor_scalar", reads=(("rr", di),), writes=(("rr", di),), out=rc, in0=rc, scalar1=1.0,
              scalar2=None, op0=ALU.max)
        pr.op("dve", "reciprocal", reads=(("rr", di),), writes=(("rr", di),), out=rc, in_=rc)
        return bank, bkey, rc

    for n in range(NCH - 1, -1, -1):
        bank, bkey, rc = mlstm_common(1, n)
        s2 = n % 2
        pr.op("dve", "tensor_scalar", reads=(bkey, ("rr", 1)), writes=(("hbt", s2),), out=hbt[:, s2, :],
              in0=bank[:, 0:256], scalar1=rc, scalar2=None, op0=ALU.mult)
        pr.dma("sp", hb_d[n], hbt[:, s2, :], reads=(("hbt", s2),), writes=(("hbd", n),))

    aq4 = v["aq4"]
    smo3_r = v["smo3"]
    attT_r = v["attT_r"]
    memT_r = v["memT_r"]
    for n in range(NCH):
        s2 = n % 2
        tok = slice(n * P, (n + 1) * P)
        pr.dma("sp", qblk[:, s2, :].rearrange("p (h t) -> p h t", h=2), aq4[:, n, :, :], writes=(("qblk", s2),))
        kbs = [kb for kb in (n - 1, n, n + 1) if 0 <= kb < NCH]
        for i, kb in enumerate(kbs):
            bank = pA[i // 2]
            bkey = "pA%d" % (i // 2)
            reg = bank[:, (i % 2) * 256:(i % 2) * 256 + 256]
            masked = kb != n
            pr.op("pe", "matmul", reads=("ak", ("qblk", s2)), writes=(bkey,), out=reg, lhsT=ak[:, kb * P:(kb + 1) * P],
                  rhs=qblk[:, s2, :], start=True, stop=not masked)
            if masked:
                nm = nmp if kb < n else nmn
                nmk = "nmp" if kb < n else "nmn"
                pr.op("pe", "matmul", reads=("ident", nmk), writes=(bkey,), out=reg, lhsT=ident_bf[:], rhs=nm[:],
                      start=False, stop=True)
            pr.op("act", "activation", reads=(bkey,), writes=(("pT_sb", s2, i),), out=pT_sb[:, s2, i, :], in_=reg,
                  func=AF.Exp, scale=ATT_SCALE)
        for i, kb in enumerate(kbs):
            pr.op("pe", "matmul", reads=("av3", ("pT_sb", s2, i)), writes=("pO",), out=pO[:, 0:256], lhsT=av3[:, kb, :],
                  rhs=pT_sb[:, s2, i, :], start=(i == 0), stop=(i == len(kbs) - 1))
        for i, kb in enumerate(kbs):
            pr.op("pe", "matmul", reads=("ones_bf", ("pT_sb", s2, i)), writes=("pO",), out=pO[:, 256:512], lhsT=ones_bf[:],
                  rhs=pT_sb[:, s2, i, :], start=(i == 0), stop=(i == len(kbs) - 1))
        pr.op("dve", "tensor_tensor", reads=("pO", "esink"), writes=(("dent", s2),), out=dent[:, s2, :], in0=pO[:, 256:512],
              in1=esink[:], op=ALU.add)
        pr.op("dve", "reciprocal", reads=(("dent", s2),), writes=(("dent", s2),), out=dent[:, s2, :], in_=dent[:, s2, :])
        pr.op("dve", "tensor_tensor", reads=("pO", ("dent", s2)), writes=(("ast", s2),), out=ast[:, s2, :], in0=pO[:, 0:256],
              in1=dent[:, s2, :], op=ALU.mult)
        outs.append(pr.dma("sp", attT_r[:, :, tok], ast[:, s2, :].rearrange("d (h t) -> d h t", h=2),
                           reads=(("ast", s2),), writes=(("out", len(outs)),)))

        pr.dma("sp", hbt[:, s2, :], hb_d[n], reads=(("hbd", n),), writes=(("hbt", s2),))
        pr.dma("sp", smo_t[:, s2, :], smo3_r[:, n, :], writes=(("smo", s2),))
        bank, bkey, rc = mlstm_common(0, n)
        pr.op("dve", "scalar_tensor_tensor", reads=(bkey, ("rr", 0), ("hbt", s2)), writes=(("hsum", s2),),
              out=hsum[:, s2, :], in0=bank[:, 0:256], scalar=rc, in1=hbt[:, s2, :], op0=ALU.mult, op1=ALU.add)
        ss_c = ssn[:, s2:s2 + 1]
        rs_c = rsn[:, s2:s2 + 1]
        pr.op("act", "activation", reads=(("hsum", s2),), writes=("junk", ("ssn", s2)), out=junk[:], in_=hsum[:, s2, :],
              func=AF.Square, accum_out=ss_c)
        rstd_chain(pr, ss_c, ("ssn", s2), rs_c, ("rsn", s2), 256)
        pr.op("dve", "tensor_tensor", reads=(("smo", s2), "gn"), writes=(("smo", s2),), out=smo_t[:, s2, :],
              in0=smo_t[:, s2, :], in1=gn[:], op=ALU.mult)
        pr.op("dve", "scalar_tensor_tensor", reads=(("hsum", s2), ("rsn", s2), ("smo", s2)), writes=(("obf", s2),),
              out=obf[:, s2, :], in0=hsum[:, s2, :], scalar=rs_c, in1=smo_t[:, s2, :], op0=ALU.mult, op1=ALU.mult)
        for h2 in range(2):
            pr.op("pe", "transpose", reads=(("obf", s2), "ident"), writes=("pT",), out=pT[:, h2 * P:(h2 + 1) * P],
                  in_=obf[:, s2, h2 * P:(h2 + 1) * P], identity=ident_bf[:])
        pr.op("act", "copy", reads=("pT",), writes=(("mst", s2),), out=mst[:, s2, :], in_=pT[:, 0:256])
        outs.append(pr.dma("sp", memT_r[:, :, tok], mst[:, s2, :].rearrange("e (h t) -> e h t", h=2),
                           reads=(("mst", s2),), writes=(("out", len(outs)),)))

    pr.flush()


def emit_k2(pr, v, gate_j):
    for dst, src, k in v.get("extra_dmas", []):
        pr.dma("pool", dst, src, writes=(k,))
    ak_d, mq_d, mk_d = v["ak"], v["mq"], v["mk"]
    gb4_d, cw_d, gn_d, sink_d = v["gb4"], v["cw"], v["gn"], v["sink2"]
    identf_d, le_d, ge_d, nmp_d, nmn_d = v["identf"], v["le"], v["ge"], v["nmp"], v["nmn"]

    def sb(name, shape, dt):
        return pr.sb(name + "_s", shape, dt)
    nc = pr.nc
    identf = sb("identf", [P, P], F32)
    ident_bf = sb("ident_bf", [P, P], BF16)
    onesf = sb("onesf", [P, P], F32)
    ones_bf = sb("ones_bf", [P, P], BF16)
    le = sb("le", [P, P], F32)
    ge = sb("ge", [P, P], F32)
    nmp = sb("nmp", [P, 256], BF16)
    nmn = sb("nmn", [P, 256], BF16)
    gb4 = sb("gb4", [P, 4], F32)
    negb = sb("negb", [P, 4], F32)
    cw = sb("cw", [P, 6], F32)
    gn = sb("gn", [P, 256], F32)
    esink = sb("esink", [P, 256], F32)
    g4 = sb("g4", [P, 4, NCH], F32)
    li = sb("li", [P, 2, NCH], F32)
    lf = sb("lf", [P, 2, NCH], F32)
    cg = sb("cg", [P, 4, NCH], F32)
    ecol = sb("ecol", [P, 2, NCH], F32)
    wend = sb("wend", [P, 2, NCH], F32)
    eg = sb("eg", [P, 2, NCH], F32)
    gtmp = sb("gtmp", [P, 2, NCH], F32)
    ak = sb("ak", [P, S], BF16)
    av3 = sb("av3", [P, NCH, P], BF16)
    kT = sb("kT", [P, S], BF16)
    qs = [sb("qs_f", [P, S], BF16), sb("qs_b", [P, S], BF16)]
    kw = [sb("kw_f", [P, NCH, P], BF16), sb("kw_b", [P, NCH, P], BF16)]
    vext = sb("vext", [P, NCH, 258], BF16)
    PSZ = 512
    NPC = S // PSZ
    xq = sb("xq", [P, PSZ + 2], F32)
    yq = sb("yq", [P, PSZ], F32)
    xk = sb("xk", [P, PSZ + 2], F32)
    yk = sb("yk", [P, PSZ], F32)
    lfbc = sb("lfbc", [P, 2, 4, P], F32)
    eb = sb("eb", [P, 2, 512], F32)
    cst = [sb("C_f", [P, 257], F32), sb("C_b", [P, 257], F32)]
    cbf = [sb("Cbf_f", [P, 257], BF16), sb("Cbf_b", [P, 257], BF16)]
    ws = sb("ws", [P, 2, P], BF16)
    rr = sb("rr", [P, 4], F32)
    hbt = sb("hbt", [P, 2, 2, 258], F32)
    NCS = 3
    hcm = sb("hcm", [P, NCS, 2, 258], F32)
    rr2 = sb("rr2", [P, NCS, 2], F32)
    hsum = sb("hsum", [P, NCS, 256], F32)
    smo_t = sb("smo_t", [P, NCS, 256], F32)
    obf = sb("obf", [P, NCS, 256], BF16)
    junk = sb("junk", [P, NCS, 256], F32)
    ssn = sb("ssn", [P, NCS], F32)
    rsn = sb("rsn", [P, NCS], F32)
    mst = sb("mst", [P, NCS, 256], BF16)
    qblk = sb("qblk", [P, 2, 256], BF16)
    pT_sb = sb("pT_sb", [P, 2, 3, 256], BF16)
    dent = sb("dent", [P, 2, 256], F32)
    ast = sb("ast", [P, 2, 256], BF16)

    pX = [pr.ps("pXf", [P, 512], F32), pr.ps("pXb", [P, 512], F32)]
    pH = [pr.ps("pHf", [P, 512], F32), pr.ps("pHb", [P, 512], F32)]
    pT = pr.ps("pT", [P, 1024], BF16)
    hfb_d = v["hfb"]
    pA = [pr.ps("pA0", [P, 512], F32), pr.ps("pA1", [P, 512], F32)]
    pO = pr.ps("pO", [P, 512], F32)

    pr.dma("sp", identf[:], identf_d, writes=("identf",))
    pr.dma("pool", ident_bf[:], identf_d, writes=("ident",))
    pr.dma("sp", le[:], le_d, writes=("le",))
    pr.dma("sp", ge[:], ge_d, writes=("ge",))
    pr.dma("pool", nmp[:], nmp_d, writes=("nmp",))
    pr.dma("pool", nmn[:], nmn_d, writes=("nmn",))
    pr.dma("sp", gb4[:], gb4_d, writes=("gb4",))
    pr.dma("sp", cw[:], cw_d, writes=("cw",))
    pr.dma("sp", gn[:], gn_d, writes=("gn",))
    pr.dma("sp", esink[:], sink_d, writes=("esink",))
    if gate_j is None:
        pr.dma("sp", g4[:], v["g4"], writes=("g4",))
    else:
        gall = sb("gall", [P, NCH, 16], F32)
        pr.dma("sp", gall[:], v["gall"], writes=("gall",))
        for g in range(4):
            pr.op("dve", "tensor_copy", reads=("gall",), writes=("g4",), out=g4[:, g, :],
                  in_=gall[:, :, g * 4 + gate_j])
    pr.dma("sp", ak[:], ak_d, writes=("ak",))
    pr.dma("sp", av3[:], v["av3"], writes=("av3",))
    pr.dma("sp", vext[:, :, 0:256], v["mv3"], writes=("vext",))
    pr.op("dve", "memset", writes=("vext1",), ap=vext[:, :, 256:257], constant=1.0)
    pr.op("dve", "memset", writes=("onesf",), ap=onesf[:], constant=1.0)
    pr.op("dve", "memset", writes=("ones_bf",), ap=ones_bf[:], constant=1.0)
    for di in range(2):
        pr.op("dve", "memset", writes=(("C", di),), ap=cst[di][:], constant=0.0)
        pr.op("dve", "memset", writes=(("Cbf", di),), ap=cbf[di][:], constant=0.0)
    pr.op("act", "activation", reads=("esink",), writes=("esink",), out=esink[:], in_=esink[:], func=AF.Exp)

    def conv_piece(src_d, xb, xkey, yb_, ykey, w0, pc):
        p0 = pc * PSZ
        lo = p0 - 1 if pc > 0 else 0
        hi = p0 + PSZ + 1 if pc < NPC - 1 else S
        c_lo = 0 if pc > 0 else 1
        if pc == 0:
            pr.op("dve", "memset", writes=(xkey,), ap=xb[:, 0:1], constant=0.0)
        if pc == NPC - 1:
            pr.op("dve", "memset", writes=(xkey,), ap=xb[:, PSZ + 1:PSZ + 2], constant=0.0)
        pr.dma("sp", xb[:, c_lo:c_lo + (hi - lo)], src_d[:, lo:hi], writes=(xkey,))
        pr.op("dve", "tensor_scalar", reads=(xkey, "cw"), writes=(ykey,), out=yb_[:], in0=xb[:, 1:PSZ + 1],
              scalar1=cw[:, w0 + 1:w0 + 2], scalar2=None, op0=ALU.mult)
        pr.op("dve", "scalar_tensor_tensor", reads=(xkey, "cw", ykey), writes=(ykey,), out=yb_[:], in0=xb[:, 0:PSZ],
              scalar=cw[:, w0:w0 + 1], in1=yb_[:], op0=ALU.mult, op1=ALU.add)
        pr.op("dve", "scalar_tensor_tensor", reads=(xkey, "cw", ykey), writes=(ykey,), out=yb_[:], in0=xb[:, 2:PSZ + 2],
              scalar=cw[:, w0 + 2:w0 + 3], in1=yb_[:], op0=ALU.mult, op1=ALU.add)
        pr.op("act", "activation", reads=(ykey,), writes=(ykey,), out=yb_[:], in_=yb_[:], func=AF.Silu)


    def preproc():
        pr.op("dve", "tensor_scalar", reads=("gb4",), writes=("negb",), out=negb[:], in0=gb4[:], scalar1=-1.0,
              scalar2=None, op0=ALU.mult)
        for di in range(2):
            pr.op("dve", "tensor_scalar", reads=("g4", "gb4"), writes=(("li", di),), out=li[:, di, :],
                  in0=g4[:, 2 * di, :], scalar1=gb4[:, 2 * di:2 * di + 1], scalar2=None, op0=ALU.add)
            pr.op("act", "activation", reads=("g4", "negb"), writes=(("gtmp", di),), out=gtmp[:, di, :],
                  in_=g4[:, 2 * di + 1, :], func=AF.Exp, scale=-1.0, bias=negb[:, 2 * di + 1:2 * di + 2])
            pr.op("act", "activation", reads=(("gtmp", di),), writes=(("gtmp", di),), out=gtmp[:, di, :],
                  in_=gtmp[:, di, :], func=AF.Ln, bias=1.0)
            pr.op("dve", "tensor_scalar", reads=(("gtmp", di),), writes=(("lf", di),), out=lf[:, di, :],
                  in0=gtmp[:, di, :], scalar1=-1.0, scalar2=None, op0=ALU.mult)
        for di in range(2):
            tri = le if di == 0 else ge
            trik = "le" if di == 0 else "ge"
            pr.op("pe", "matmul", reads=(("lf", di), trik), writes=(("pH", 0),), out=pH[0][:, di * 128:di * 128 + 64],
                  lhsT=tri[:], rhs=lf[:, di, :], start=True, stop=True)
            pr.op("pe", "matmul", reads=(("lf", di), "onesf"), writes=(("pH", 0),), out=pH[0][:, di * 128 + 64:di * 128 + 128],
                  lhsT=onesf[:], rhs=lf[:, di, :], start=True, stop=True)
        pr.op("act", "copy", reads=(("pH", 0),), writes=("cg",), out=cg[:].rearrange("p g n -> p (g n)"), in_=pH[0][:, 0:256])
        for di in range(2):
            bc = cg[:, 2 * di, :]
            gt_ = cg[:, 2 * di + 1, :]
            pr.op("dve", "tensor_tensor", reads=(("li", di), "cg"), writes=(("gtmp", di),), out=gtmp[:, di, :],
                  in0=li[:, di, :], in1=bc, op=ALU.subtract)
            pr.op("act", "activation", reads=(("gtmp", di),), writes=(("ecol", di),), out=ecol[:, di, :],
                  in_=gtmp[:, di, :], func=AF.Exp)
            pr.op("dve", "tensor_tensor", reads=(("gtmp", di), "cg"), writes=(("gtmp", di),), out=gtmp[:, di, :],
                  in0=gtmp[:, di, :], in1=gt_, op=ALU.add)
            pr.op("act", "activation", reads=(("gtmp", di),), writes=(("wend", di),), out=wend[:, di, :],
                  in_=gtmp[:, di, :], func=AF.Exp)
            pr.op("act", "activation", reads=("cg",), writes=(("eg", di),), out=eg[:, di, :], in_=gt_, func=AF.Exp)

        yield

    def q_stream():
        ebc = [0]
        for pc in range(NPC):
            p0 = pc * PSZ
            conv_piece(mq_d, xq, "xq", yq, "yq", 0, pc)
            yield
            for sp_ in range(PSZ // 512):
                for di in range(2):
                    tri = le if di == 0 else ge
                    trik = "le" if di == 0 else "ge"
                    bank = pX[ebc[0] % 2]
                    bkeys = (("pX", ebc[0] % 2),)
                    es = ebc[0] % 2
                    ebc[0] += 1
                    for c4 in range(4):
                        n = pc * (PSZ // P) + sp_ * 4 + c4
                        pr.op("dve", "tensor_scalar", reads=("onesf", ("lf", di)), writes=(("lfbc", di, c4),),
                              out=lfbc[:, di, c4, :], in0=onesf[:], scalar1=lf[:, di, n:n + 1], scalar2=None, op0=ALU.mult)
                        pr.op("pe", "matmul", reads=(("lfbc", di, c4), trik), writes=bkeys, out=bank[:, c4 * P:(c4 + 1) * P],
                              lhsT=lfbc[:, di, c4, :], rhs=tri[:], start=True, stop=True)
                    yield
                    pr.op("act", "activation", reads=bkeys, writes=(("eb", es),), out=eb[:, es, :], in_=bank[:], func=AF.Exp)
                    yield
                    t0 = p0 + sp_ * 512
                    pr.op("dve", "scalar_tensor_tensor", reads=("yq", ("eb", es)), writes=(("qs", di),),
                          out=qs[di][:, t0:t0 + 512], in0=yq[:, sp_ * 512:(sp_ + 1) * 512], scalar=QK_SCALE, in1=eb[:, es, :],
                          op0=ALU.mult, op1=ALU.mult)
                    yield

    def k_stream():
        for pc in range(NPC):
            p0 = pc * PSZ
            conv_piece(mk_d, xk, "xk", yk, "yk", 3, pc)
            yield
            pr.op("act", "copy", reads=("yk",), writes=("kT",), out=kT[:, p0:p0 + PSZ], in_=yk[:])
            for c16 in range(PSZ // P):
                n = pc * (PSZ // P) + c16
                pr.op("pe", "transpose", reads=("yk", "identf"), writes=(("pH", 1),), out=pH[1][:, 0:P],
                      in_=yk[:, c16 * P:(c16 + 1) * P], identity=identf[:])
                yield
                for di in range(2):
                    pr.op("act", "activation", reads=(("pH", 1), ("wend", di)), writes=(("kw", di),), out=kw[di][:, n, :],
                          in_=pH[1][:, 0:P], func=AF.Copy, scale=wend[:, di, n:n + 1])
                yield

    outs = []
    smo3_r = v["smo3"]
    aq4 = v["aq4"]
    att_dst = v["att_dst"]
    mem_dst = v["mem_dst"]

    progress = [0, 0]
    LAG = 3

    def scan_stream(di):
        order = range(NCH) if di == 0 else range(NCH - 1, -1, -1)
        msk = le if di == 0 else ge
        mskk = "le" if di == 0 else "ge"
        X = pX[di]
        H = pH[di]
        for n in order:
            tok = slice(n * P, (n + 1) * P)
            s2 = n % 2
            pr.op("pe", "matmul", reads=("kT", ("qs", di)), writes=(("pX", di),), out=X[:, 0:P], lhsT=kT[:, tok],
                  rhs=qs[di][:, tok], start=True, stop=True)
            yield
            pr.op("dve", "scalar_tensor_tensor", reads=(("pX", di), ("ecol", di), mskk), writes=(("ws", di),),
                  out=ws[:, di, :], in0=X[:, 0:P], scalar=ecol[:, di, n:n + 1], in1=msk[:], op0=ALU.mult, op1=ALU.mult)
            yield
            pr.op("pe", "matmul", reads=(("ws", di), "vext", "vext1"), writes=(("pH", di),), out=H[:, 0:257],
                  lhsT=ws[:, di, :], rhs=vext[:, n, 0:257], start=True, stop=False)
            pr.op("pe", "matmul", reads=(("qs", di), ("Cbf", di)), writes=(("pH", di),), out=H[:, 0:257],
                  lhsT=qs[di][:, tok], rhs=cbf[di][:], start=False, stop=True)
            pr.op("pe", "matmul", reads=(("kw", di), "vext", "vext1"), writes=(("pX", di),), out=X[:, 128:385],
                  lhsT=kw[di][:, n, :], rhs=vext[:, n, 0:257], start=True, stop=True)
            yield
            pr.op("dve", "scalar_tensor_tensor", reads=(("pX", di), ("eg", di), ("C", di)), writes=(("C", di),),
                  out=cst[di][:], in0=cst[di][:], scalar=eg[:, di, n:n + 1], in1=X[:, 128:385], op0=ALU.mult, op1=ALU.add)
            yield
            pr.op("pool", "tensor_copy", reads=(("C", di),), writes=(("Cbf", di),), out=cbf[di][:], in_=cst[di][:])
            yield
            pr.op("act", "copy", reads=(("pH", di),), writes=(("hbt", di, s2),), out=hbt[:, di, s2, 0:257], in_=H[:, 0:257])
            pr.dma("pool", hfb_d[di, n, :, 0:257], hbt[:, di, s2, 0:257], reads=(("hbt", di, s2),), writes=(("hfd", di, n),))
            progress[di] += 1
            yield

    def attention_stream():
        for n in range(NCH):
            s2 = n % 2
            tok = slice(n * P, (n + 1) * P)
            pr.dma("sp", qblk[:, s2, :].rearrange("p (h t) -> p h t", h=2), aq4[:, n, :, :], writes=(("qblk", s2),))
            kbs = [kb for kb in (n - 1, n, n + 1) if 0 <= kb < NCH]
            for i, kb in enumerate(kbs):
                bank = pA[i // 2]
                bkey = "pA%d" % (i // 2)
                reg = bank[:, (i % 2) * 256:(i % 2) * 256 + 256]
                masked = kb != n
                pr.op("pe", "matmul", reads=("ak", ("qblk", s2)), writes=(bkey,), out=reg, lhsT=ak[:, kb * P:(kb + 1) * P],
                      rhs=qblk[:, s2, :], start=True, stop=not masked)
                if masked:
                    nm = nmp if kb < n else nmn
                    nmk = "nmp" if kb < n else "nmn"
                    pr.op("pe", "matmul", reads=("ident", nmk), writes=(bkey,), out=reg, lhsT=ident_bf[:], rhs=nm[:],
                          start=False, stop=True)
                yield
                pr.op("act", "activation", reads=(bkey,), writes=(("pT_sb", s2, i),), out=pT_sb[:, s2, i, :], in_=reg,
                      func=AF.Exp, scale=ATT_SCALE)
                yield
            for i, kb in enumerate(kbs):
                pr.op("pe", "matmul", reads=("av3", ("pT_sb", s2, i)), writes=("pO",), out=pO[:, 0:256], lhsT=av3[:, kb, :],
                      rhs=pT_sb[:, s2, i, :], start=(i == 0), stop=(i == len(kbs) - 1))
            for i, kb in enumerate(kbs):
                pr.op("pe", "matmul", reads=("ones_bf", ("pT_sb", s2, i)), writes=("pO",), out=pO[:, 256:512],
                      lhsT=ones_bf[:], rhs=pT_sb[:, s2, i, :], start=(i == 0), stop=(i == len(kbs) - 1))
            yield
            pr.op("dve", "tensor_tensor", reads=("pO", "esink"), writes=(("dent", s2),), out=dent[:, s2, :],
                  in0=pO[:, 256:512], in1=esink[:], op=ALU.add)
            pr.op("dve", "reciprocal", reads=(("dent", s2),), writes=(("dent", s2),), out=dent[:, s2, :], in_=dent[:, s2, :])
            yield
            pr.op("dve", "tensor_tensor", reads=("pO", ("dent", s2)), writes=(("ast", s2),), out=ast[:, s2, :],
                  in0=pO[:, 0:256], in1=dent[:, s2, :], op=ALU.mult)
            outs.append(pr.dma("sp", att_dst(n), ast[:, s2, :].rearrange("d (h t) -> d h t", h=2),
                               reads=(("ast", s2),), writes=(("out", len(outs)),)))
            yield

    comb_order = sorted(range(NCH), key=lambda n: (max(n, NCH - 1 - n), n))

    def combine_stream(r):
        for n in comb_order[r::NCS]:
            need = min(NCH, max(n, NCH - 1 - n) + 1 + LAG)
            while min(progress) < need:
                yield
            tok = slice(n * P, (n + 1) * P)
            for di in range(2):
                pr.dma("pool", hcm[:, r, di, 0:257], hfb_d[di, n, :, 0:257], reads=(("hfd", di, n),), writes=(("hcm", r, di),))
            pr.dma("sp", smo_t[:, r, :], smo3_r[:, n, :], writes=(("smo", r),))
            yield
            pr.op("act", "activation", reads=(("hcm", r, 0), ("hcm", r, 1)), writes=(("rr2", r),), out=rr2[:, r, :],
                  in_=hcm[:, r, :, 256], func=AF.Abs)
            yield
            pr.op("dve", "tensor_scalar", reads=(("rr2", r),), writes=(("rr2", r),), out=rr2[:, r, :], in0=rr2[:, r, :],
                  scalar1=1.0, scalar2=None, op0=ALU.max)
            pr.op("dve", "reciprocal", reads=(("rr2", r),), writes=(("rr2", r),), out=rr2[:, r, :], in_=rr2[:, r, :])
            yield
            pr.op("dve", "tensor_scalar", reads=(("hcm", r, 0), ("rr2", r)), writes=(("hsum", r),), out=hsum[:, r, :],
                  in0=hcm[:, r, 0, 0:256], scalar1=rr2[:, r, 0:1], scalar2=None, op0=ALU.mult)
            pr.op("dve", "scalar_tensor_tensor", reads=(("hcm", r, 1), ("rr2", r), ("hsum", r)), writes=(("hsum", r),),
                  out=hsum[:, r, :], in0=hcm[:, r, 1, 0:256], scalar=rr2[:, r, 1:2], in1=hsum[:, r, :],
                  op0=ALU.mult, op1=ALU.add)
            yield
            ss_c = ssn[:, r:r + 1]
            rs_c = rsn[:, r:r + 1]
            pr.op("act", "activation", reads=(("hsum", r),), writes=(("junk", r), ("ssn", r)), out=junk[:, r, :],
                  in_=hsum[:, r, :], func=AF.Square, accum_out=ss_c)
            pr.op("dve", "tensor_tensor", reads=(("smo", r), "gn"), writes=(("smo", r),), out=smo_t[:, r, :],
                  in0=smo_t[:, r, :], in1=gn[:], op=ALU.mult)
            yield
            pr.op("dve", "tensor_scalar", reads=(("ssn", r),), writes=(("rsn", r),), out=rs_c, in0=ss_c,
                  scalar1=1.0 / 256, scalar2=EPS, op0=ALU.mult, op1=ALU.add)
            yield
            pr.op("act", "activation", reads=(("rsn", r),), writes=(("rsn", r),), out=rs_c, in_=rs_c, func=AF.Sqrt)
            yield
            pr.op("dve", "reciprocal", reads=(("rsn", r),), writes=(("rsn", r),), out=rs_c, in_=rs_c)
            yield
            pr.op("dve", "scalar_tensor_tensor", reads=(("hsum", r), ("rsn", r), ("smo", r)), writes=(("obf", r),),
                  out=obf[:, r, :], in0=hsum[:, r, :], scalar=rs_c, in1=smo_t[:, r, :], op0=ALU.mult, op1=ALU.mult)
            yield
            for h2 in range(2):
                pr.op("pe", "transpose", reads=(("obf", r), "ident"), writes=("pT",),
                      out=pT[:, r * 256 + h2 * P:r * 256 + (h2 + 1) * P], in_=obf[:, r, h2 * P:(h2 + 1) * P],
                      identity=ident_bf[:])
            yield
            pr.op("act", "copy", reads=("pT",), writes=(("mst", r),), out=mst[:, r, :], in_=pT[:, r * 256:(r + 1) * 256])
            outs.append(pr.dma("sp", mem_dst(n), mst[:, r, :].rearrange("e (h t) -> e h t", h=2),
                               reads=(("mst", r),), writes=(("out", len(outs)),)))
            yield

    pre = preproc()
    active = [pre, attention_stream()]
    while active:
        for g in list(active):
            try:
                next(g)
            except StopIteration:
                active.remove(g)
                if g is pre:
                    qg, kg = q_stream(), k_stream()
                    pend = {id(qg), id(kg)}
                    active.extend([qg, kg])
                elif "pend" in dir() and id(g) in pend:
                    pend.discard(id(g))
                    if not pend:
                        active.extend([scan_stream(0), scan_stream(1)] + [combine_stream(r) for r in range(NCS)])
    pr.flush()


def _wx_index():
    cols = [np.arange(0, 1280), np.arange(1536, 2560), np.arange(1280, 1536), np.arange(4608, 4624),
            np.arange(2560, 4608)]
    return np.concatenate(cols)


def _rope_tables():
    half = 64
    inv_freq = (np.float32(10000.0) ** (-np.arange(half, dtype=np.float32) / np.float32(half))).astype(np.float32)
    pos = np.arange(S, dtype=np.float32)
    ang = (pos[:, None] * inv_freq[None, :]).astype(np.float32)
    cos = np.cos(ang).astype(np.float32)
    sin = np.sin(ang).astype(np.float32)
    cosT = np.concatenate([cos, cos], axis=1).T
    sinT = np.concatenate([-sin, sin], axis=1).T
    return np.ascontiguousarray(cosT), np.ascontiguousarray(sinT)


_CACHE = {}
_RUNKW = {}
_LAST = {}


def _get(name, builder):
    if name not in _CACHE:
        _CACHE[name] = builder()
    return _CACHE[name]


def run_k1(x_flat, w_in_l, g_pre_l):
    nc = _get("k1", build_k1)
    wx = np.ascontiguousarray(w_in_l[:, _get("wxi", _wx_index)])
    cosT, sinT = _get("rope", _rope_tables)
    gcol = np.ascontiguousarray(g_pre_l.reshape(KC, P).T)
    ident = np.eye(P, dtype=np.float32)
    in_maps = []
    for c in range(NCORE):
        p0 = (c % 4) * TOK
        in_maps.append({
            "x": np.ascontiguousarray(x_flat[c * TOK:(c + 1) * TOK]),
            "w_in": wx, "gcol": gcol,
            "cosT": np.ascontiguousarray(cosT[:, p0:p0 + TOK]),
            "sinT": np.ascontiguousarray(sinT[:, p0:p0 + TOK]),
            "ident": ident,
        })
    res = run_bass_kernel_spmd(nc, in_maps, core_ids=list(range(NCORE)), **_RUNKW)
    _LAST["t"] = res.exec_time_ns
    return res.results


def _k2_consts():
    r = np.arange(P)
    le = (r[:, None] <= r[None, :]).astype(np.float32)
    ge = (r[:, None] >= r[None, :]).astype(np.float32)
    nmp1 = np.where(r[:, None] < r[None, :], np.float32(-30000.0), np.float32(0.0)).astype(np.float32)
    nmn1 = np.where(r[:, None] > r[None, :], np.float32(-30000.0), np.float32(0.0)).astype(np.float32)
    return {
        "identf": np.eye(P, dtype=np.float32), "le": le, "ge": ge,
        "nmp": np.ascontiguousarray(np.concatenate([nmp1, nmp1], axis=1)),
        "nmn": np.ascontiguousarray(np.concatenate([nmn1, nmn1], axis=1)),
    }


def run_k2(k1, conv_w_l, gate_bias_l, ml_norm_g_l, attn_sink_l):
    nc = _get("k2", build_k2)
    consts = _get("k2c", _k2_consts)
    ca = np.ascontiguousarray
    in_maps = []
    for c in range(NCORE):
        b, j = c // 4, c % 4
        kv = j // 2
        cores = k1[4 * b:4 * b + 4]
        AQ = np.concatenate([r["aq"][2 * j * P:(2 * j + 2) * P] for r in cores], axis=1)
        AK = np.concatenate([r["ak"][kv * P:(kv + 1) * P] for r in cores], axis=1)
        AV = np.concatenate([r["av"][:, kv * P:(kv + 1) * P] for r in cores], axis=0)
        MQ = np.concatenate([r["mqk"][j * P:(j + 1) * P] for r in cores], axis=1)
        MK = np.concatenate([r["mqk"][512 + j * P:512 + (j + 1) * P] for r in cores], axis=1)
        MV = np.concatenate([r["mv"][:, j * 256:(j + 1) * 256] for r in cores], axis=0)
        SMO = np.concatenate([r["smo"][:, j * 256:(j + 1) * 256] for r in cores], axis=0)
        GT = np.concatenate([r["gt"][:, [j, 4 + j, 8 + j, 12 + j]] for r in cores], axis=0)
        m = dict(consts)
        m["aq2"] = ca(AQ.reshape(2, P, NCH, P).transpose(1, 2, 0, 3).reshape(P, NCH * 256))
        m["ak"] = ca(AK)
        m["av3"] = ca(AV.reshape(NCH, P, P).transpose(1, 0, 2).reshape(P, NCH * P))
        m["mq"] = ca(MQ)
        m["mk"] = ca(MK)
        m["mv3"] = ca(MV.reshape(NCH, P, 256).transpose(1, 0, 2).reshape(P, NCH * 256))
        m["smo3"] = ca(SMO.reshape(NCH, P, 256).transpose(1, 0, 2).reshape(P, NCH * 256))
        m["g4"] = ca(GT.reshape(NCH, P, 4).transpose(1, 2, 0).reshape(P, 4 * NCH))
        m["gb4"] = ca(np.broadcast_to(gate_bias_l[[j, 4 + j, 8 + j, 12 + j]][None, :], (P, 4)))
        cwq = conv_w_l[:, j * P:(j + 1) * P].T
        cwk = conv_w_l[:, 512 + j * P:512 + (j + 1) * P].T
        m["cw"] = ca(np.concatenate([cwq, cwk], axis=1))
        m["gn"] = ca(np.broadcast_to(ml_norm_g_l[j * 256:(j + 1) * 256][None, :], (P, 256)))
        m["sink2"] = ca(np.broadcast_to(np.repeat(attn_sink_l[2 * j:2 * j + 2], P)[None, :], (P, 256)))
        in_maps.append(m)
    res = run_bass_kernel_spmd(nc, in_maps, core_ids=list(range(NCORE)), **_RUNKW)
    _LAST["t"] = res.exec_time_ns
    out = res.results
    mixT = []
    for c in range(NCORE):
        b, part = c // 4, c % 4
        sl = slice(part * TOK, (part + 1) * TOK)
        att = [out[4 * b + j]["attT"][:, sl] for j in range(4)]
        mem = [out[4 * b + j]["memT"][:, sl] for j in range(4)]
        mixT.append(ca(np.concatenate(att + mem, axis=0)))
    return mixT


def kernel_unfused(x, w_in, conv_w, gate_bias, ml_norm_g, attn_sink, w_out,
                   g_pre_mix, g_post_mix, g_pre_mlp, g_post_mlp, w_up, w_down):
    f = lambda a: np.ascontiguousarray(np.asarray(a, dtype=np.float32))
    x = f(x)
    xf = x.reshape(B * S, D)
    depth = w_in.shape[0]
    for l in range(depth):
        k1 = run_k1(xf, f(w_in[l]), f(g_pre_mix[l]))
        mixT = run_k2(k1, f(conv_w[l]), f(gate_bias[l]), f(ml_norm_g[l]), f(attn_sink[l]))
        del k1
        ys = run_k3(mixT, xf, f(w_out[l]), f(w_up[l]), f(w_down[l]), f(g_post_mix[l]), f(g_pre_mlp[l]),
                    f(g_post_mlp[l]))
        xf = np.concatenate(ys, axis=0)
    return xf.reshape(B, S, D).astype(np.float32)


DEPTH = 2


def build_fused():
    nc = bass.Bass("TRN2", target_bir_lowering=False)

    def din(name, shape, dt=F32):
        return nc.dram_tensor(name, list(shape), dt, kind="ExternalInput").ap()

    def dint(name, shape, dt):
        return nc.dram_tensor(name, list(shape), dt).ap()
    x_in = din("x", [S, D])
    w_in = din("w_in", [DEPTH, D, WX_COLS])
    w_out = din("w_out", [DEPTH, D, D])
    w_up = din("w_up", [DEPTH, D, DFF])
    w_down = din("w_down", [DEPTH, DFF, D])
    gcol1 = din("gcol1", [DEPTH, P, KC])
    gcol2 = din("gcol2", [DEPTH, P, KC])
    g_pm = din("g_pm", [DEPTH, P, D])
    g_pl = din("g_pl", [DEPTH, P, D])
    cosT = din("cosT", [P, S])
    sinT = din("sinT", [P, S])
    ident = din("ident", [P, P])
    gb4 = din("gb4", [DEPTH, 4, P, 4])
    cw = din("cw", [DEPTH, 4, P, 6])
    gn = din("gn", [DEPTH, 4, P, 256])
    sink2 = din("sink2", [DEPTH, 4, P, 256])
    le = din("le", [P, P])
    ge = din("ge", [P, P])
    nmp = din("nmp", [P, 256])
    nmn = din("nmn", [P, 256])
    y_out = nc.dram_tensor("y", [TOK, D], F32, kind="ExternalOutput").ap()
    xidx_d = din("xidx", [P, TOK // P], mybir.dt.int32)
    midx_d = din("midx", [P, (TOK // 512) * KC], mybir.dt.int32)
    aq = dint("aq_s", [1024, S], BF16)
    ak = dint("ak_s", [256, S], BF16)
    mqk = dint("mqk_s", [1024, S], F32)
    av = dint("av_s", [S, 256], BF16)
    mv = dint("mv_s", [S, 1024], BF16)
    smo = dint("smo_s", [S, 1024], F32)
    gt = dint("gt_s", [P, NCH, 16], F32)
    NBLK = S // 512
    mixB = dint("mixB_s", [NBLK, D, 512], BF16)
    x1 = dint("x1_s", [S, D], F32)
    hb = dint("hb_s", [NCH, P, 256], F32)
    hfb = dint("hfb_s", [2, NCH, P, 258], F32)

    wb = []
    casts = []
    for l in range(DEPTH):
        wl = {"w_in": dint("wb_in%d" % l, [D, WX_COLS], BF16), "w_out": dint("wb_out%d" % l, [D, D], BF16),
              "w_up": dint("wb_up%d" % l, [D, DFF], BF16), "w_down": dint("wb_down%d" % l, [DFF, D], BF16)}
        wb.append(wl)
        cl = {}
        for nm, src in (("w_in", w_in[l]), ("w_out", w_out[l]), ("w_up", w_up[l]), ("w_down", w_down[l])):
            rows = src.shape[0]
            cl[nm] = [(wl[nm][r0:r0 + P, :], src[r0:r0 + P, :], ("wcast", l, nm, r0)) for r0 in range(0, rows, P)]
        casts.append(cl)

    pr = Prog(nc)
    for dst, src, k in casts[0]["w_in"]:
        pr.dma("pool", dst, src, writes=(k,))
    pr.flush()
    rest0 = casts[0]["w_out"] + casts[0]["w_up"] + casts[0]["w_down"]
    all1 = casts[1]["w_in"] + casts[1]["w_out"] + casts[1]["w_up"] + casts[1]["w_down"]
    q1 = (len(all1) + 3) // 4
    for l in range(DEPTH):
        xsrc = x_in if l == 0 else x1
        ydst = x1 if l == 0 else y_out
        emit_k1(pr, {"x": xsrc, "w": wb[l]["w_in"], "gcol": gcol1[l], "cos": cosT, "sin": sinT, "ident": ident,
                     "aq": aq, "ak": ak, "mqk": mqk, "av": av, "mv": mv, "smo": smo, "gt": gt}, S, True,
                extra_dmas=(rest0 if l == 0 else None))
        for j in range(4):
            kv = j // 2
            v = {
                "aq4": aq[2 * j * P:(2 * j + 2) * P, :].rearrange("(h d) (n t) -> d n h t", h=2, t=P),
                "ak": ak[kv * P:(kv + 1) * P, :],
                "av3": av[:, kv * P:(kv + 1) * P].rearrange("(n t) d -> t n d", t=P),
                "mq": mqk[j * P:(j + 1) * P, :],
                "mk": mqk[512 + j * P:512 + (j + 1) * P, :],
                "mv3": mv[:, j * 256:(j + 1) * 256].rearrange("(n t) e -> t n e", t=P),
                "smo3": smo[:, j * 256:(j + 1) * 256].rearrange("(n t) e -> t n e", t=P),
                "gall": gt,
                "gb4": gb4[l, j], "cw": cw[l, j], "gn": gn[l, j], "sink2": sink2[l, j],
                "identf": ident, "le": le, "ge": ge, "nmp": nmp, "nmn": nmn,
                "att_dst": (lambda n, j=j: mixB[n // 4, 2 * j * P:(2 * j + 2) * P, (n % 4) * P:(n % 4 + 1) * P]
                            .rearrange("(h d) t -> d h t", h=2)),
                "mem_dst": (lambda n, j=j: mixB[n // 4, 1024 + j * 256:1024 + (j + 1) * 256, (n % 4) * P:(n % 4 + 1) * P]
                            .rearrange("(h e) t -> e h t", h=2)),
                "hb": hb, "hfb": hfb,
                "extra_dmas": (all1[j * q1:(j + 1) * q1] if l == 0 else []),
            }
            emit_k2(pr, v, j)
        k3d = {"x": xsrc, "w_out": wb[l]["w_out"], "w_up": wb[l]["w_up"], "w_down": wb[l]["w_down"],
               "g_pm": g_pm[l], "g_pl": g_pl[l], "gcol": gcol2[l], "ident": ident, "y": ydst,
               "mix_blk": lambda tb: mixB[tb].rearrange("(kc p) t -> p kc t", p=P)}
        if l == DEPTH - 1:
            k3d["gather"] = {"xidx": xidx_d, "midx": midx_d, "mix_flat": mixB.rearrange("b f t -> (b f) t")}
            emit_k3(pr, k3d, TOK)
        else:
            emit_k3(pr, k3d, S)
    pr.emit()
    return nc


def kernel(x, w_in, conv_w, gate_bias, ml_norm_g, attn_sink, w_out,
           g_pre_mix, g_post_mix, g_pre_mlp, g_post_mlp, w_up, w_down):
    f = lambda a: np.ascontiguousarray(np.asarray(a, dtype=np.float32))
    ca = np.ascontiguousarray
    nc = _get("fused", build_fused)
    x = f(x)
    w_in, conv_w, gate_bias, ml_norm_g, attn_sink = f(w_in), f(conv_w), f(gate_bias), f(ml_norm_g), f(attn_sink)
    g_pre_mix, g_post_mix, g_pre_mlp, g_post_mlp = f(g_pre_mix), f(g_post_mix), f(g_pre_mlp), f(g_post_mlp)
    wxi = _get("wxi", _wx_index)
    cosT, sinT = _get("rope", _rope_tables)
    consts = _get("k2c", _k2_consts)
    shared = {
        "w_in": ca(w_in[:, :, wxi]), "w_out": f(w_out), "w_up": f(w_up), "w_down": f(w_down),
        "gcol1": ca(g_pre_mix.reshape(DEPTH, KC, P).transpose(0, 2, 1)),
        "gcol2": ca(g_pre_mlp.reshape(DEPTH, KC, P).transpose(0, 2, 1)),
        "g_pm": ca(np.broadcast_to(g_post_mix[:, None, :], (DEPTH, P, D))),
        "g_pl": ca(np.broadcast_to(g_post_mlp[:, None, :], (DEPTH, P, D))),
        "cosT": cosT, "sinT": sinT, "ident": consts["identf"],
        "le": consts["le"], "ge": consts["ge"], "nmp": consts["nmp"], "nmn": consts["nmn"],
    }
    gb4 = np.zeros((DEPTH, 4, P, 4), np.float32)
    cw = np.zeros((DEPTH, 4, P, 6), np.float32)
    gn = np.zeros((DEPTH, 4, P, 256), np.float32)
    sink2 = np.zeros((DEPTH, 4, P, 256), np.float32)
    for l in range(DEPTH):
        for j in range(4):
            gb4[l, j] = gate_bias[l][[j, 4 + j, 8 + j, 12 + j]][None, :]
            cw[l, j, :, 0:3] = conv_w[l][:, j * P:(j + 1) * P].T
            cw[l, j, :, 3:6] = conv_w[l][:, 512 + j * P:512 + (j + 1) * P].T
            gn[l, j] = ml_norm_g[l][j * 256:(j + 1) * 256][None, :]
            sink2[l, j] = np.repeat(attn_sink[l][2 * j:2 * j + 2], P)[None, :]
    shared.update({"gb4": gb4, "cw": cw, "gn": gn, "sink2": sink2})
    in_maps = []
    pp = np.arange(P, dtype=np.int32)[:, None]
    for c in range(NCORE):
        part = c % 4
        m = dict(shared)
        m["x"] = ca(x[c // 4])
        m["xidx"] = ca((part * TOK + np.arange(TOK // P, dtype=np.int32)[None, :] * P + pp).astype(np.int32))
        tb = np.arange(TOK // 512, dtype=np.int32)[:, None]
        kc = np.arange(KC, dtype=np.int32)[None, :]
        rows = ((part * (TOK // 512) + tb) * D + kc * P).reshape(1, -1)
        m["midx"] = ca((rows + pp).astype(np.int32))
        in_maps.append(m)
    res = run_bass_kernel_spmd(nc, in_maps, core_ids=list(range(NCORE)), **_RUNKW)
    _LAST["t"] = res.exec_time_ns
    out = np.empty((B, S, D), np.float32)
    for c in range(NCORE):
        out[c // 4, (c % 4) * TOK:(c % 4 + 1) * TOK] = res.results[c]["y"]
    return out
```

```python
import contextlib
import numpy as np
import ml_dtypes
import concourse.bass as bass
import concourse.mybir as mybir
from concourse.bass_utils import run_bass_kernel_spmd

F32 = mybir.dt.float32
BF16 = mybir.dt.bfloat16
AF = mybir.ActivationFunctionType
ALU = mybir.AluOpType
AX = mybir.AxisListType

D = 2048
S = 8192
B = 2
NCORE = 8
TOK = 2048
P = 128
KC = D // P
IN_COLS = 4624
DFF = 8192
EPS = 1e-6
WX_COLS = 4624

ENGS = ("sp", "act", "dve", "pool", "pe")
NSLOT = 8


class Op:
    __slots__ = ("eng", "fn", "deps", "sig", "val", "dma", "slot", "slotval", "nm")

    def __init__(self, eng, fn, dma):
        self.eng = eng
        self.fn = fn
        self.deps = []
        self.sig = False
        self.val = 0
        self.dma = dma
        self.slot = 0
        self.slotval = 0


class Prog:
    def __init__(self, nc):
        self.nc = nc
        self.ops = {e: [] for e in ENGS}
        self.res = {}
        self.stack = contextlib.ExitStack()
        self.gstack = contextlib.ExitStack()
        self.sem_eng = {e: self.gstack.enter_context(nc.semaphore("sem_" + e)) for e in ENGS}
        self.sem_dma = {e: [self.gstack.enter_context(nc.semaphore("dq_%s_%d" % (e, i))) for i in range(NSLOT)]
                        for e in ENGS if e != "pe"}
        self.cnt = {e: 0 for e in ENGS}
        self.dk = {e: 0 for e in ENGS}
        self.seen = {e: {} for e in ENGS}
        self.batch = 0
        self.uid = 0

    def sb(self, name, shape, dt):
        self.uid += 1
        return self.stack.enter_context(self.nc.sbuf_tensor("%s_%d" % (name, self.uid), list(shape), dt))

    def ps(self, name, shape, dt):
        self.uid += 1
        return self.stack.enter_context(self.nc.psum_tensor("%s_%d" % (name, self.uid), list(shape), dt))

    def add(self, eng, fn, reads=(), writes=(), dma=False):
        op = Op(eng, fn, dma)
        op.nm = self.batch
        deps = set()
        for r in reads:
            st = self.res.get(r)
            if st is not None and st[0] is not None:
                deps.add(st[0])
        for w in writes:
            st = self.res.get(w)
            if st is not None:
                if st[0] is not None:
                    deps.add(st[0])
                deps.update(st[1].values())
                deps.update(st[2])
        for r in reads:
            st = self.res.get(r)
            if st is None:
                st = [None, {}, []]
                self.res[r] = st
            if dma:
                st[2].append(op)
            else:
                st[1][eng] = op
        for w in writes:
            self.res[w] = [op, {}, []]
        deps.discard(op)
        for d in deps:
            if d.nm != self.batch:
                continue
            if d.eng == "pe" and eng == "pe" and not d.dma and not dma:
                continue
            d.sig = True
            op.deps.append(d)
        self.ops[eng].append(op)
        return op

    def op(self, eng, method, reads=(), writes=(), **kw):
        return self.add(eng, lambda e: getattr(e, method)(**kw), reads, writes)

    def dma(self, eng, out, in_, reads=(), writes=()):
        return self.add(eng, lambda e: e.dma_start(out=out, in_=in_), reads, writes, dma=True)

    def fence(self, eng, reads):
        return self.add(eng, None, reads=reads, writes=())

    def flush(self):
        nc = self.nc
        sem_eng, sem_dma = self.sem_eng, self.sem_dma
        for e in ENGS:
            last = None
            for op in self.ops[e]:
                if not op.dma and op.fn is not None:
                    last = op
            if last is not None:
                last.sig = True
            for op in self.ops[e]:
                if op.dma:
                    k = self.dk[e]
                    op.slot = k % NSLOT
                    op.slotval = 16 * (k // NSLOT + 1)
                    self.dk[e] = k + 1
                elif op.sig and op.fn is not None:
                    self.cnt[e] += 1
                    op.val = self.cnt[e]
        targets = []
        for e in ENGS:
            if self.cnt[e] > 0:
                targets.append((e, sem_eng[e], self.cnt[e]))
            if e != "pe":
                k = self.dk[e]
                for sl in range(NSLOT):
                    n_used = (k - sl + NSLOT - 1) // NSLOT if k > sl else 0
                    if n_used > 0:
                        targets.append((None, sem_dma[e][sl], 16 * n_used))

        def emit_engine(ename, eng):
            seen = self.seen[ename]
            for op in self.ops[ename]:
                waits = {}
                for d in op.deps:
                    if d.dma:
                        key = sem_dma[d.eng][d.slot]
                        v = d.slotval
                    else:
                        key = sem_eng[d.eng]
                        v = d.val
                    if waits.get(key, 0) < v:
                        waits[key] = v
                if op.dma and op.slotval > 16:
                    key = sem_dma[ename][op.slot]
                    v = op.slotval - 16
                    if waits.get(key, 0) < v:
                        waits[key] = v
                for key, v in waits.items():
                    if seen.get(key, 0) >= v:
                        continue
                    seen[key] = v
                    eng.wait_ge(key, v)
                if op.fn is None:
                    continue
                ins = op.fn(eng)
                if op.dma:
                    ins.then_inc(sem_dma[ename][op.slot], 16)
                elif op.sig:
                    ins.then_inc(sem_eng[ename], 1)
            for (te, key, v) in targets:
                if te == ename:
                    continue
                if seen.get(key, 0) >= v:
                    continue
                seen[key] = v
                eng.wait_ge(key, v)

        with nc.Block() as block:
            @block.sync
            def _(eng):
                emit_engine("sp", eng)

            @block.scalar
            def _(eng):
                emit_engine("act", eng)

            @block.vector
            def _(eng):
                emit_engine("dve", eng)

            @block.gpsimd
            def _(eng):
                emit_engine("pool", eng)

            @block.tensor
            def _(eng):
                emit_engine("pe", eng)
        self.ops = {e: [] for e in ENGS}
        self.stack.close()
        self.stack = contextlib.ExitStack()
        self.batch += 1

    def emit(self):
        self.flush()
        self.gstack.close()


class Banks:
    def __init__(self, prog, names):
        self.tiles = [(n, prog.ps(n, [P, 512], F32)) for n in names]
        self.i = 0

    def next(self):
        t = self.tiles[self.i % len(self.tiles)]
        self.i += 1
        return t


def mm_group(prog, out_ap, bank_key, pairs, extra_reads=()):
    n = len(pairs)
    last = None
    for i, (l, r, rk) in enumerate(pairs):
        def fn(e, l=l, r=r, i=i):
            return e.matmul(out_ap, l, r, start=(i == 0), stop=(i == n - 1))
        last = prog.add("pe", fn, reads=tuple(rk) + tuple(extra_reads), writes=(bank_key,))
    return last


def rmsnorm_to_featmajor(prog, pe_banks_t, x_tile, xkey, g_col, hT, hT_key, col0, ident_bf, scr, ti):
    ss, rstd, xs = scr["ss"], scr["rstd"], scr["xs"]
    sfx = ti % 2
    ssk, rsk, xsk = ("ss", sfx), ("rstd", sfx), ("xs", sfx)
    ss_c = ss[:, sfx:sfx + 1]
    rs_c = rstd[:, sfx:sfx + 1]
    xs_t = xs[:, sfx, :]
    prog.add("act", lambda e: e.activation(out=xs_t, in_=x_tile, func=AF.Square, accum_out=ss_c),
             reads=(xkey,), writes=(xsk, ssk))
    prog.add("dve", lambda e: e.tensor_scalar(out=rs_c, in0=ss_c, scalar1=1.0 / D, scalar2=EPS,
                                              op0=ALU.mult, op1=ALU.add),
             reads=(ssk,), writes=(rsk,))
    prog.add("act", lambda e: e.activation(out=rs_c, in_=rs_c, func=AF.Sqrt), reads=(rsk,), writes=(rsk,))
    prog.add("dve", lambda e: e.reciprocal(out=rs_c, in_=rs_c), reads=(rsk,), writes=(rsk,))
    prog.add("act", lambda e: e.activation(out=xs_t, in_=x_tile, func=AF.Copy, scale=rs_c),
             reads=(xkey, rsk), writes=(xsk,))
    for q in range(4):
        bname, bt = pe_banks_t.next()
        for j in range(4):
            kc = q * 4 + j
            prog.add("pe", lambda e, kc=kc, j=j, bt=bt: e.transpose(
                out=bt[:, j * P:(j + 1) * P], in_=xs_t[:, kc * P:(kc + 1) * P], identity=ident_bf[:]),
                reads=(xsk, "ident"), writes=(bname,))
        for j in range(4):
            kc = q * 4 + j
            prog.add("dve", lambda e, kc=kc, j=j, bt=bt: e.tensor_scalar(
                out=hT[:, kc, col0:col0 + P], in0=bt[:, j * P:(j + 1) * P],
                scalar1=g_col[:, kc:kc + 1], scalar2=None, op0=ALU.mult),
                reads=(bname, "gcol"), writes=(hT_key,))


def build_k1():
    nc = bass.Bass("TRN2", target_bir_lowering=False)
    d = {
        "x": nc.dram_tensor("x", [TOK, D], F32, kind="ExternalInput").ap(),
        "w": nc.dram_tensor("w_in", [D, WX_COLS], F32, kind="ExternalInput").ap(),
        "gcol": nc.dram_tensor("gcol", [P, KC], F32, kind="ExternalInput").ap(),
        "cos": nc.dram_tensor("cosT", [P, TOK], F32, kind="ExternalInput").ap(),
        "sin": nc.dram_tensor("sinT", [P, TOK], F32, kind="ExternalInput").ap(),
        "ident": nc.dram_tensor("ident", [P, P], F32, kind="ExternalInput").ap(),
        "aq": nc.dram_tensor("aq", [1024, TOK], BF16, kind="ExternalOutput").ap(),
        "ak": nc.dram_tensor("ak", [256, TOK], BF16, kind="ExternalOutput").ap(),
        "mqk": nc.dram_tensor("mqk", [1024, TOK], F32, kind="ExternalOutput").ap(),
        "av": nc.dram_tensor("av", [TOK, 256], BF16, kind="ExternalOutput").ap(),
        "mv": nc.dram_tensor("mv", [TOK, 1024], BF16, kind="ExternalOutput").ap(),
        "smo": nc.dram_tensor("smo", [TOK, 1024], F32, kind="ExternalOutput").ap(),
        "gt": nc.dram_tensor("gt", [TOK, 16], F32, kind="ExternalOutput").ap(),
    }
    pr = Prog(nc)
    emit_k1(pr, d, TOK, False)
    pr.emit()
    return nc


def emit_k1(pr, d, ntok, gt_tiled, extra_dmas=None):
    extra_dmas = list(extra_dmas or [])
    x, w, gcol_d, cos_d, sin_d, ident_d = d["x"], d["w"], d["gcol"], d["cos"], d["sin"], d["ident"]
    aq_o, ak_o, mqk_o, av_o, mv_o, smo_o, gt_o = d["aq"], d["ak"], d["mqk"], d["av"], d["mv"], d["smo"], d["gt"]
    ident_bf = pr.sb("ident_bf", [P, P], BF16)
    gcol = pr.sb("gcol_sb", [P, KC], F32)
    cst = pr.sb("cs_sb", [P, 2, 2, 512], F32)
    xt = pr.sb("xt", [P, 2, D], F32)
    scr = {
        "ss": pr.sb("ss", [P, 2], F32),
        "rstd": pr.sb("rstd", [P, 2], F32),
        "xs": pr.sb("xs", [P, 2, D], BF16),
    }
    hT = pr.sb("hT", [P, KC, 512], BF16)
    NW = 3
    wt = pr.sb("wt", [P, NW, KC, 512], BF16)
    ev32 = pr.sb("ev32", [P, 4, 512], F32)
    evbf = pr.sb("evbf", [P, 4, 512], BF16)
    t1 = pr.sb("t1", [P, 2, 512], F32)
    banks = Banks(pr, ["pb%d" % i for i in range(6)])
    tb_t = [("pt%d" % i, pr.ps("pt%d" % i, [P, 1024], BF16)) for i in range(2)]

    class TB:
        i = 0

        def next(self):
            t = tb_t[self.i % 2]
            self.i += 1
            return t
    tbanks = TB()

    pr.dma("pool", ident_bf[:], ident_d, writes=("ident",))
    pr.dma("sp", gcol[:], gcol_d, writes=("gcol",))

    w_r = w.rearrange("(kc p) c -> p kc c", p=P)
    wcount = [0]
    evc = [0]
    outs = []

    def load_w(c0, n):
        slot = wcount[0] % NW
        wcount[0] += 1
        key = ("wt", slot)
        pr.dma("pool", wt[:, slot, :, 0:n], w_r[:, :, c0:c0 + n], writes=(key,))
        if extra_dmas:
            dst, src, k = extra_dmas.pop(0)
            pr.dma("pool", dst, src, writes=(k,))
        return slot, key

    def ev_slot():
        s = evc[0] % 4
        evc[0] += 1
        return s

    for tb in range(ntok // 512):
        t0 = tb * 512
        csl = tb % 2
        pr.dma("sp", cst[:, csl, 0, :], cos_d[:, t0:t0 + 512], writes=(("cos", csl),))
        pr.dma("sp", cst[:, csl, 1, :], sin_d[:, t0:t0 + 512], writes=(("sin", csl),))
        for ti in range(4):
            g_ti = tb * 4 + ti
            xs_ = g_ti % 2
            xkey = ("xt", xs_)
            pr.dma("sp", xt[:, xs_, :], x[g_ti * P:(g_ti + 1) * P, :], writes=(xkey,))
            rmsnorm_to_featmajor(pr, tbanks, xt[:, xs_, :], xkey, gcol, hT, ("hT", ti), ti * P,
                                 ident_bf, scr, g_ti)
        hkeys = tuple(("hT", ti) for ti in range(4))

        for grp, (c0g, nh) in enumerate(((0, 4), (512, 4), (1024, 2))):
            slot, wkey = load_w(c0g, nh * P)
            for hl in range(nh):
                hh = grp * 4 + hl
                na, ba = banks.next()
                mm_group(pr, ba[:], na, [(wt[:, slot, kc, hl * P:(hl + 1) * P], hT[:, kc, :], hkeys + (wkey,))
                                         for kc in range(KC)])
                es = ev_slot()
                cs_ap = cst[:, csl, 0, :]
                sn_ap = cst[:, csl, 1, :]
                pr.op("dve", "tensor_tensor", reads=(na, ("cos", csl)), writes=("t1a",), out=t1[:, 0, :], in0=ba[:],
                      in1=cs_ap, op=ALU.mult)
                pr.op("dve", "tensor_tensor", reads=(na, ("sin", csl)), writes=("t1b",), out=t1[0:64, 1, :],
                      in0=ba[64:128, :], in1=sn_ap[0:64, :], op=ALU.mult)
                pr.op("dve", "tensor_tensor", reads=(na, ("sin", csl), "t1b"), writes=("t1b",), out=t1[64:128, 1, :],
                      in0=ba[0:64, :], in1=sn_ap[64:128, :], op=ALU.mult)
                pr.op("dve", "tensor_tensor", reads=("t1a", "t1b"), writes=(("evbf", es),), out=evbf[:, es, :],
                      in0=t1[:, 0, :], in1=t1[:, 1, :], op=ALU.add)
                dst = aq_o[hh * P:(hh + 1) * P, t0:t0 + 512] if hh < 8 else ak_o[(hh - 8) * P:(hh - 7) * P, t0:t0 + 512]
                outs.append(pr.dma("sp", dst, evbf[:, es, :], reads=(("evbf", es),), writes=(("out", len(outs)),)))

        for grp in range(2):
            slot, wkey = load_w(1280 + grp * 512, 512)
            for j in range(4):
                nb_, bt = banks.next()
                mm_group(pr, bt[:], nb_, [(wt[:, slot, kc, j * P:(j + 1) * P], hT[:, kc, :], hkeys + (wkey,))
                                          for kc in range(KC)])
                es = ev_slot()
                pr.add("act", lambda e, bt=bt, es=es: e.copy(out=ev32[:, es, :], in_=bt[:]),
                       reads=(nb_,), writes=(("ev32", es),))
                r0 = grp * 512 + j * P
                outs.append(pr.dma("sp", mqk_o[r0:r0 + P, t0:t0 + 512], ev32[:, es, :],
                                   reads=(("ev32", es),), writes=(("out", len(outs)),)))

        tm_groups = [
            (2304, 272, "avg"),
            (2576, 512, "mv0"), (3088, 512, "mv1"),
            (3600, 512, "mo0"), (4112, 512, "mo1"),
        ]
        for c0, ncols, kind in tm_groups:
            slot, wkey = load_w(c0, ncols)
            for ti in range(4):
                r0 = t0 + ti * P
                nb_, bt = banks.next()
                mm_group(pr, bt[:, 0:ncols], nb_,
                         [(hT[:, kc, ti * P:(ti + 1) * P], wt[:, slot, kc, 0:ncols], (("hT", ti), wkey))
                          for kc in range(KC)])
                es = ev_slot()
                if kind == "avg":
                    pr.add("act", lambda e, bt=bt, es=es: e.copy(out=evbf[:, es, 0:256], in_=bt[:, 0:256]),
                           reads=(nb_,), writes=(("evbf", es),))
                    pr.add("act", lambda e, bt=bt, es=es: e.copy(out=ev32[:, es, 0:16], in_=bt[:, 256:272]),
                           reads=(nb_,), writes=(("ev32", es),))
                    outs.append(pr.dma("sp", av_o[r0:r0 + P, :], evbf[:, es, 0:256],
                                       reads=(("evbf", es),), writes=(("out", len(outs)),)))
                    gt_dst = gt_o[:, r0 // P, :] if gt_tiled else gt_o[r0:r0 + P, :]
                    outs.append(pr.dma("sp", gt_dst, ev32[:, es, 0:16],
                                       reads=(("ev32", es),), writes=(("out", len(outs)),)))
                elif kind.startswith("mv"):
                    c = int(kind[2]) * 512
                    pr.add("act", lambda e, bt=bt, es=es: e.copy(out=evbf[:, es, :], in_=bt[:]),
                           reads=(nb_,), writes=(("evbf", es),))
                    outs.append(pr.dma("sp", mv_o[r0:r0 + P, c:c + 512], evbf[:, es, :],
                                       reads=(("evbf", es),), writes=(("out", len(outs)),)))
                else:
                    c = int(kind[2]) * 512
                    pr.add("act", lambda e, bt=bt, es=es: e.activation(out=ev32[:, es, :], in_=bt[:], func=AF.Sigmoid),
                           reads=(nb_,), writes=(("ev32", es),))
                    outs.append(pr.dma("sp", smo_o[r0:r0 + P, c:c + 512], ev32[:, es, :],
                                       reads=(("ev32", es),), writes=(("out", len(outs)),)))

    for dst, src, k in extra_dmas:
        pr.dma("pool", dst, src, writes=(k,))
    pr.flush()


def rstd_chain(pr, ss_c, ssk, rs_c, rsk, n):
    pr.add("dve", lambda e: e.tensor_scalar(out=rs_c, in0=ss_c, scalar1=1.0 / n, scalar2=EPS,
                                            op0=ALU.mult, op1=ALU.add), reads=(ssk,), writes=(rsk,))
    pr.add("act", lambda e: e.activation(out=rs_c, in_=rs_c, func=AF.Sqrt), reads=(rsk,), writes=(rsk,))
    pr.add("dve", lambda e: e.reciprocal(out=rs_c, in_=rs_c), reads=(rsk,), writes=(rsk,))


def build_k3():
    nc = bass.Bass("TRN2", target_bir_lowering=False)
    d = {
        "mixT": nc.dram_tensor("mixT", [D, TOK], BF16, kind="ExternalInput").ap(),
        "x": nc.dram_tensor("x", [TOK, D], F32, kind="ExternalInput").ap(),
        "w_out": nc.dram_tensor("w_out", [D, D], F32, kind="ExternalInput").ap(),
        "w_up": nc.dram_tensor("w_up", [D, DFF], F32, kind="ExternalInput").ap(),
        "w_down": nc.dram_tensor("w_down", [DFF, D], F32, kind="ExternalInput").ap(),
        "g_pm": nc.dram_tensor("g_pm", [P, D], F32, kind="ExternalInput").ap(),
        "g_pl": nc.dram_tensor("g_pl", [P, D], F32, kind="ExternalInput").ap(),
        "gcol": nc.dram_tensor("gcol", [P, KC], F32, kind="ExternalInput").ap(),
        "ident": nc.dram_tensor("ident", [P, P], F32, kind="ExternalInput").ap(),
        "y": nc.dram_tensor("y", [TOK, D], F32, kind="ExternalOutput").ap(),
    }
    mixT_r = d["mixT"].rearrange("(kc p) t -> p kc t", p=P)
    d["mix_blk"] = lambda tb: mixT_r[:, :, tb * 512:(tb + 1) * 512]
    pr = Prog(nc)
    emit_k3(pr, d, TOK)
    pr.emit()
    return nc


def emit_k3(pr, d, ntok):
    x, w_out, w_up, w_down = d["x"], d["w_out"], d["w_up"], d["w_down"]
    gat = d.get("gather")
    extra_dmas = list(d.get("extra_dmas") or [])
    extra_every = 4
    gpm_d, gpl_d, gcol_d, ident_d, y_o = d["g_pm"], d["g_pl"], d["gcol"], d["ident"], d["y"]
    ident_bf = pr.sb("ident_bf", [P, P], BF16)
    gcol = pr.sb("gcol_sb", [P, KC], F32)
    gpm = pr.sb("gpm_sb", [P, D], F32)
    gpl = pr.sb("gpl_sb", [P, D], F32)
    scr = {
        "ss": pr.sb("ss", [P, 2], F32),
        "rstd": pr.sb("rstd", [P, 2], F32),
        "xs": pr.sb("xs", [P, 2, D], BF16),
    }
    ss4 = pr.sb("ss4", [P, 4, 4], F32)
    ssr = pr.sb("ssr", [P, 4], F32)
    rs2 = pr.sb("rs2", [P, 4], F32)
    mixb = pr.sb("mixb", [P, KC, 512], BF16)
    h2T = pr.sb("h2T", [P, KC, 512], BF16)
    NW = 3
    wt = pr.sb("wt", [P, NW, KC, 512], BF16)
    x1b = pr.sb("x1b", [P, 4, D], F32)
    yb = pr.sb("yb", [P, 4, D], F32)
    uT = pr.sb("uT", [P, 32, 512], BF16)
    r32 = pr.sb("r32", [P, 2, 512], F32)
    pa = Banks(pr, ["pa0", "pa1"])
    pd = [("pd%d" % i, pr.ps("pd%d" % i, [P, 512], F32)) for i in range(4)]
    tb_t = [("pt%d" % i, pr.ps("pt%d" % i, [P, 1024], BF16)) for i in range(2)]

    class TB:
        i = 0

        def next(self):
            t = tb_t[self.i % 2]
            self.i += 1
            return t
    tbanks = TB()

    pr.dma("pool", ident_bf[:], ident_d, writes=("ident",))
    pr.dma("sp", gcol[:], gcol_d, writes=("gcol",))
    pr.dma("sp", gpm[:], gpm_d, writes=("gpm",))
    pr.dma("sp", gpl[:], gpl_d, writes=("gpl",))

    w_out_r = w_out.rearrange("(kc p) c -> p kc c", p=P)
    w_up_r = w_up.rearrange("(kc p) c -> p kc c", p=P)
    w_down_r = w_down.rearrange("(fc p) c -> p fc c", p=P)
    mixkeys = tuple(("mixb", kc) for kc in range(KC))
    if gat is not None:
        I32 = mybir.dt.int32
        xidx = pr.sb("xidx", [P, ntok // P], I32)
        midx = pr.sb("midx", [P, (ntok // 512) * KC], I32)
        pr.dma("sp", xidx[:], gat["xidx"], writes=("xidx",))
        pr.dma("sp", midx[:], gat["midx"], writes=("midx",))
    wcount = [0]
    rc = [0]
    outs = []

    def load_w(src):
        slot = wcount[0] % NW
        wcount[0] += 1
        key = ("wt", slot)
        pr.dma("pool", wt[:, slot, :, :], src, writes=(key,))
        if extra_dmas and wcount[0] % extra_every == 0:
            dst_, src_, k_ = extra_dmas.pop(0)
            pr.dma("pool", dst_, src_, writes=(k_,))
        return slot, key

    def r32_slot():
        s = rc[0] % 2
        rc[0] += 1
        return s

    def sumsq(src_ap, src_key, acc_ap, acc_key):
        rs = r32_slot()
        pr.add("act", lambda e: e.activation(out=r32[:, rs, :], in_=src_ap, func=AF.Square, accum_out=acc_ap),
               reads=(src_key,), writes=(("r32", rs), acc_key))

    def norm_residual(ti, g_sb, gkey, base_ap, base_key, dst_ap, dst_key):
        ss_c = ssr[:, ti:ti + 1]
        rs_c = rs2[:, ti:ti + 1]
        ybk = [("yb", ti, cb) for cb in range(4)]
        pr.add("dve", lambda e: e.reduce_sum(out=ss_c, in_=ss4[:, ti, :], axis=AX.X),
               reads=tuple(("ss4", ti, cb) for cb in range(4)), writes=(("ssr", ti),))
        rstd_chain(pr, ss_c, ("ssr", ti), rs_c, ("rs2", ti), D)
        pr.add("dve", lambda e: e.scalar_tensor_tensor(out=yb[:, ti, :], in0=yb[:, ti, :], scalar=rs_c, in1=g_sb[:],
                                                       op0=ALU.mult, op1=ALU.mult),
               reads=tuple(ybk) + (("rs2", ti), gkey), writes=tuple(ybk))
        pr.add("dve", lambda e: e.tensor_tensor(out=dst_ap, in0=base_ap, in1=yb[:, ti, :], op=ALU.add),
               reads=tuple(ybk) + (base_key,), writes=(dst_key,))

    for tb in range(ntok // 512):
        t0 = tb * 512
        if gat is None:
            pr.dma("sp", mixb[:], d["mix_blk"](tb), writes=mixkeys)
            for ti in range(4):
                pr.dma("sp", x1b[:, ti, :], x[t0 + ti * P:t0 + (ti + 1) * P, :], writes=(("x1b", ti),))
        else:
            for kc in range(KC):
                pr.add("pool", lambda e, kc=kc, col=tb * KC + kc: e.indirect_dma_start(
                    out=mixb[:, kc, :], out_offset=None, in_=gat["mix_flat"],
                    in_offset=bass.IndirectOffsetOnAxis(midx[:, col:col + 1], 0)),
                    reads=("midx",), writes=(("mixb", kc),), dma=True)
            for ti in range(4):
                pr.add("pool", lambda e, ti=ti, col=tb * 4 + ti: e.indirect_dma_start(
                    out=x1b[:, ti, :], out_offset=None, in_=x,
                    in_offset=bass.IndirectOffsetOnAxis(xidx[:, col:col + 1], 0)),
                    reads=("xidx",), writes=(("x1b", ti),), dma=True)
        for cb in range(4):
            slot, wkey = load_w(w_out_r[:, :, cb * 512:(cb + 1) * 512])
            for ti in range(4):
                nb_, bt = pa.next()
                mm_group(pr, bt[:], nb_, [(mixb[:, kc, ti * P:(ti + 1) * P], wt[:, slot, kc, :], (("mixb", kc), wkey))
                                          for kc in range(KC)])
                ypiece = yb[:, ti, cb * 512:(cb + 1) * 512]
                pr.add("act", lambda e, bt=bt, ypiece=ypiece: e.copy(out=ypiece, in_=bt[:]),
                       reads=(nb_,), writes=(("yb", ti, cb),))
                sumsq(ypiece, ("yb", ti, cb), ss4[:, ti, cb:cb + 1], ("ss4", ti, cb))
        for ti in range(4):
            norm_residual(ti, gpm, "gpm", x1b[:, ti, :], ("x1b", ti), x1b[:, ti, :], ("x1b", ti))
            rmsnorm_to_featmajor(pr, tbanks, x1b[:, ti, :], ("x1b", ti), gcol, h2T, ("h2T", ti), ti * P,
                                 ident_bf, scr, ti)
        hkeys = tuple(("h2T", ti) for ti in range(4))

        for hf in range(2):
            for fgl in range(8):
                fg = hf * 8 + fgl
                slot, wkey = load_w(w_up_r[:, :, fg * 512:(fg + 1) * 512])
                for j in range(4):
                    fcl = fgl * 4 + j
                    nb_, bt = pa.next()
                    mm_group(pr, bt[:], nb_, [(wt[:, slot, kc, j * P:(j + 1) * P], h2T[:, kc, :], hkeys + (wkey,))
                                              for kc in range(KC)])
                    rs = r32_slot()
                    pr.add("act", lambda e, bt=bt, rs=rs: e.activation(out=r32[:, rs, :], in_=bt[:], func=AF.Relu),
                           reads=(nb_,), writes=(("r32", rs),))
                    sq_eng = "dve"
                    pr.add(sq_eng, lambda e, rs=rs, fcl=fcl: e.tensor_tensor(out=uT[:, fcl, :], in0=r32[:, rs, :],
                                                                            in1=r32[:, rs, :], op=ALU.mult),
                           reads=(("r32", rs),), writes=(("uT", fcl),))
            for cb in range(4):
                for qi in range(2):
                    q = hf * 2 + qi
                    slot, wkey = load_w(w_down_r[:, q * 16:(q + 1) * 16, cb * 512:(cb + 1) * 512])
                    for ti in range(4):
                        pname, pt = pd[ti]
                        for f16 in range(16):
                            fcl = qi * 16 + f16
                            pr.add("pe", lambda e, pt=pt, fcl=fcl, f16=f16, ti=ti, slot=slot, qi=qi: e.matmul(
                                pt[:], uT[:, fcl, ti * P:(ti + 1) * P], wt[:, slot, f16, :],
                                start=(qi == 0 and f16 == 0), stop=(qi == 1 and f16 == 15)),
                                reads=(("uT", fcl), wkey), writes=(pname,))
                for ti in range(4):
                    pname, pt = pd[ti]
                    ypiece = yb[:, ti, cb * 512:(cb + 1) * 512]
                    if hf == 0:
                        pr.add("act", lambda e, pt=pt, ypiece=ypiece: e.copy(out=ypiece, in_=pt[:]),
                               reads=(pname,), writes=(("yb", ti, cb),))
                    else:
                        pr.add("dve", lambda e, pt=pt, ypiece=ypiece: e.tensor_tensor(out=ypiece, in0=ypiece, in1=pt[:],
                                                                                      op=ALU.add),
                               reads=(pname, ("yb", ti, cb)), writes=(("yb", ti, cb),))
                        sumsq(ypiece, ("yb", ti, cb), ss4[:, ti, cb:cb + 1], ("ss4", ti, cb))
        for ti in range(4):
            norm_residual(ti, gpl, "gpl", x1b[:, ti, :], ("x1b", ti), yb[:, ti, :], ("ybo", ti))
            outs.append(pr.dma("sp", y_o[t0 + ti * P:t0 + (ti + 1) * P, :], yb[:, ti, :],
                               reads=(("ybo", ti),) + tuple(("yb", ti, cb) for cb in range(4)),
                               writes=(("out", len(outs)),)))

    for dst_, src_, k_ in extra_dmas:
        pr.dma("pool", dst_, src_, writes=(k_,))
    pr.flush()


def run_k3(mixT_cores, x_flat, w_out_l, w_up_l, w_down_l, g_pm, g_pre_mlp, g_pl):
    nc = _get("k3", build_k3)
    gcol = np.ascontiguousarray(g_pre_mlp.reshape(KC, P).T)
    gpm = np.ascontiguousarray(np.broadcast_to(g_pm[None, :], (P, D)))
    gpl = np.ascontiguousarray(np.broadcast_to(g_pl[None, :], (P, D)))
    ident = np.eye(P, dtype=np.float32)
    in_maps = []
    for c in range(NCORE):
        in_maps.append({
            "mixT": mixT_cores[c],
            "x": np.ascontiguousarray(x_flat[c * TOK:(c + 1) * TOK]),
            "w_out": w_out_l, "w_up": w_up_l, "w_down": w_down_l,
            "g_pm": gpm, "g_pl": gpl, "gcol": gcol, "ident": ident,
        })
    res = run_bass_kernel_spmd(nc, in_maps, core_ids=list(range(NCORE)), **_RUNKW)
    _LAST["t"] = res.exec_time_ns
    return [r["y"] for r in res.results]


NCH = S // P
ATT_SCALE = float(128 ** -0.5)
QK_SCALE = float(128 ** -0.5)


def build_k2():
    nc = bass.Bass("TRN2", target_bir_lowering=False)

    def din(name, shape, dt):
        return nc.dram_tensor(name, list(shape), dt, kind="ExternalInput").ap()
    v = {
        "aq4": din("aq2", [P, NCH * 256], BF16).rearrange("p (n h t) -> p n h t", h=2, t=P),
        "ak": din("ak", [P, S], BF16),
        "av3": din("av3", [P, NCH * P], BF16).rearrange("p (n d) -> p n d", d=P),
        "mq": din("mq", [P, S], F32),
        "mk": din("mk", [P, S], F32),
        "mv3": din("mv3", [P, NCH * 256], BF16).rearrange("p (n e) -> p n e", e=256),
        "smo3": din("smo3", [P, NCH * 256], F32).rearrange("p (n e) -> p n e", e=256),
        "g4": din("g4", [P, 4 * NCH], F32).rearrange("p (g n) -> p g n", g=4),
        "gb4": din("gb4", [P, 4], F32),
        "cw": din("cw", [P, 6], F32),
        "gn": din("gn", [P, 256], F32),
        "sink2": din("sink2", [P, 256], F32),
        "identf": din("identf", [P, P], F32),
        "le": din("le", [P, P], F32),
        "ge": din("ge", [P, P], F32),
        "nmp": din("nmp", [P, 256], F32),
        "nmn": din("nmn", [P, 256], F32),
        "attT_r": nc.dram_tensor("attT", [256, S], BF16, kind="ExternalOutput").ap().rearrange("(h d) t -> d h t", h=2),
        "memT_r": nc.dram_tensor("memT", [256, S], BF16, kind="ExternalOutput").ap().rearrange("(h e) t -> e h t", h=2),
        "hb": nc.dram_tensor("hb_scratch", [NCH, P, 256], F32).ap(),
        "hfb": nc.dram_tensor("hfb_scratch", [2, NCH, P, 258], F32).ap(),
    }
    v["att_dst"] = lambda n: v["attT_r"][:, :, n * P:(n + 1) * P]
    v["mem_dst"] = lambda n: v["memT_r"][:, :, n * P:(n + 1) * P]
    pr = Prog(nc)
    emit_k2(pr, v, None)
    pr.emit()
    return nc


def emit_k2_v1(pr, v, gate_j):
    ak_d, mq_d, mk_d = v["ak"], v["mq"], v["mk"]
    gb4_d, cw_d, gn_d, sink_d = v["gb4"], v["cw"], v["gn"], v["sink2"]
    identf_d, le_d, ge_d, nmp_d, nmn_d = v["identf"], v["le"], v["ge"], v["nmp"], v["nmn"]
    hb_d = v["hb"]

    def sb(name, shape, dt):
        return pr.sb(name + "_s", shape, dt)
    nc = pr.nc
    identf = sb("identf", [P, P], F32)
    ident_bf = sb("ident_bf", [P, P], BF16)
    onesf = sb("onesf", [P, P], F32)
    ones_bf = sb("ones_bf", [P, P], BF16)
    le = sb("le", [P, P], F32)
    ge = sb("ge", [P, P], F32)
    nmp = sb("nmp", [P, 256], BF16)
    nmn = sb("nmn", [P, 256], BF16)
    gb4 = sb("gb4", [P, 4], F32)
    negb = sb("negb", [P, 4], F32)
    cw = sb("cw", [P, 6], F32)
    gn = sb("gn", [P, 256], F32)
    esink = sb("esink", [P, 256], F32)
    g4 = sb("g4", [P, 4, NCH], F32)
    li = sb("li", [P, 2, NCH], F32)
    lf = sb("lf", [P, 2, NCH], F32)
    cg = sb("cg", [P, 4, NCH], F32)
    ecol = sb("ecol", [P, 2, NCH], F32)
    wend = sb("wend", [P, 2, NCH], F32)
    eg = sb("eg", [P, 2, NCH], F32)
    gtmp = sb("gtmp", [P, 2, NCH], F32)
    ak = sb("ak", [P, S], BF16)
    av3 = sb("av3", [P, NCH, P], BF16)
    kT = sb("kT", [P, S], BF16)
    qs = [sb("qs_f", [P, S], BF16), sb("qs_b", [P, S], BF16)]
    kw = [sb("kw_f", [P, NCH, P], BF16), sb("kw_b", [P, NCH, P], BF16)]
    vext = sb("vext", [P, NCH, 258], BF16)
    PSZ = 1024
    NPC = S // PSZ
    xq = sb("xq", [P, PSZ + 2], F32)
    yq = sb("yq", [P, PSZ], F32)
    xk = sb("xk", [P, PSZ + 2], F32)
    yk = sb("yk", [P, PSZ], F32)
    lfbc = sb("lfbc", [P, 4, P], F32)
    eb = sb("eb", [P, 2, 512], F32)
    cst = [sb("C_f", [P, 257], F32), sb("C_b", [P, 257], F32)]
    cbf = [sb("Cbf_f", [P, 257], BF16), sb("Cbf_b", [P, 257], BF16)]
    ws = sb("ws", [P, 2, P], BF16)
    rr = sb("rr", [P, 4], F32)
    hbt = sb("hbt", [P, 2, 256], F32)
    hsum = sb("hsum", [P, 2, 256], F32)
    smo_t = sb("smo_t", [P, 2, 256], F32)
    obf = sb("obf", [P, 2, 256], BF16)
    junk = sb("junk", [P, 256], F32)
    ssn = sb("ssn", [P, 2], F32)
    rsn = sb("rsn", [P, 2], F32)
    mst = sb("mst", [P, 2, 256], BF16)
    qblk = sb("qblk", [P, 2, 256], BF16)
    pT_sb = sb("pT_sb", [P, 2, 3, 256], BF16)
    dent = sb("dent", [P, 2, 256], F32)
    ast = sb("ast", [P, 2, 256], BF16)

    pS = pr.ps("pS", [P, 512], F32)
    pH = [pr.ps("pH0", [P, 512], F32), pr.ps("pH1", [P, 512], F32)]
    pC = pr.ps("pC", [P, 512], F32)
    pT = pr.ps("pT", [P, 1024], BF16)
    pA = [pr.ps("pA0", [P, 512], F32), pr.ps("pA1", [P, 512], F32)]
    pO = pr.ps("pO", [P, 512], F32)

    pr.dma("sp", identf[:], identf_d, writes=("identf",))
    pr.dma("pool", ident_bf[:], identf_d, writes=("ident",))
    pr.dma("sp", le[:], le_d, writes=("le",))
    pr.dma("sp", ge[:], ge_d, writes=("ge",))
    pr.dma("pool", nmp[:], nmp_d, writes=("nmp",))
    pr.dma("pool", nmn[:], nmn_d, writes=("nmn",))
    pr.dma("sp", gb4[:], gb4_d, writes=("gb4",))
    pr.dma("sp", cw[:], cw_d, writes=("cw",))
    pr.dma("sp", gn[:], gn_d, writes=("gn",))
    pr.dma("sp", esink[:], sink_d, writes=("esink",))
    if gate_j is None:
        pr.dma("sp", g4[:], v["g4"], writes=("g4",))
    else:
        gall = sb("gall", [P, NCH, 16], F32)
        pr.dma("sp", gall[:], v["gall"], writes=("gall",))
        for g in range(4):
            pr.op("dve", "tensor_copy", reads=("gall",), writes=("g4",), out=g4[:, g, :],
                  in_=gall[:, :, g * 4 + gate_j])
    pr.dma("sp", ak[:], ak_d, writes=("ak",))
    pr.dma("sp", av3[:], v["av3"], writes=("av3",))
    pr.dma("sp", vext[:, :, 0:256], v["mv3"], writes=("vext",))
    pr.op("dve", "memset", writes=("vext1",), ap=vext[:, :, 256:257], constant=1.0)
    pr.op("dve", "memset", writes=("onesf",), ap=onesf[:], constant=1.0)
    pr.op("dve", "memset", writes=("ones_bf",), ap=ones_bf[:], constant=1.0)
    for di in range(2):
        pr.op("dve", "memset", writes=(("C", di),), ap=cst[di][:], constant=0.0)
        pr.op("dve", "memset", writes=(("Cbf", di),), ap=cbf[di][:], constant=0.0)
    pr.op("act", "activation", reads=("esink",), writes=("esink",), out=esink[:], in_=esink[:], func=AF.Exp)

    pr.op("dve", "tensor_scalar", reads=("gb4",), writes=("negb",), out=negb[:], in0=gb4[:], scalar1=-1.0,
          scalar2=None, op0=ALU.mult)
    for di in range(2):
        pr.op("dve", "tensor_scalar", reads=("g4", "gb4"), writes=(("li", di),), out=li[:, di, :],
              in0=g4[:, 2 * di, :], scalar1=gb4[:, 2 * di:2 * di + 1], scalar2=None, op0=ALU.add)
        pr.op("act", "activation", reads=("g4", "negb"), writes=(("gtmp", di),), out=gtmp[:, di, :],
              in_=g4[:, 2 * di + 1, :], func=AF.Exp, scale=-1.0, bias=negb[:, 2 * di + 1:2 * di + 2])
        pr.op("act", "activation", reads=(("gtmp", di),), writes=(("gtmp", di),), out=gtmp[:, di, :],
              in_=gtmp[:, di, :], func=AF.Ln, bias=1.0)
        pr.op("dve", "tensor_scalar", reads=(("gtmp", di),), writes=(("lf", di),), out=lf[:, di, :],
              in0=gtmp[:, di, :], scalar1=-1.0, scalar2=None, op0=ALU.mult)
    for di in range(2):
        tri = le if di == 0 else ge
        trik = "le" if di == 0 else "ge"
        pr.op("pe", "matmul", reads=(("lf", di), trik), writes=("pA0",), out=pA[0][:, di * 128:di * 128 + 64],
              lhsT=tri[:], rhs=lf[:, di, :], start=True, stop=True)
        pr.op("pe", "matmul", reads=(("lf", di), "onesf"), writes=("pA0",), out=pA[0][:, di * 128 + 64:di * 128 + 128],
              lhsT=onesf[:], rhs=lf[:, di, :], start=True, stop=True)
    pr.op("act", "copy", reads=("pA0",), writes=("cg",), out=cg[:].rearrange("p g n -> p (g n)"), in_=pA[0][:, 0:256])
    for di in range(2):
        bc = cg[:, 2 * di, :]
        gt_ = cg[:, 2 * di + 1, :]
        pr.op("dve", "tensor_tensor", reads=(("li", di), "cg"), writes=(("gtmp", di),), out=gtmp[:, di, :],
              in0=li[:, di, :], in1=bc, op=ALU.subtract)
        pr.op("act", "activation", reads=(("gtmp", di),), writes=(("ecol", di),), out=ecol[:, di, :],
              in_=gtmp[:, di, :], func=AF.Exp)
        pr.op("dve", "tensor_tensor", reads=(("gtmp", di), "cg"), writes=(("gtmp", di),), out=gtmp[:, di, :],
              in0=gtmp[:, di, :], in1=gt_, op=ALU.add)
        pr.op("act", "activation", reads=(("gtmp", di),), writes=(("wend", di),), out=wend[:, di, :],
              in_=gtmp[:, di, :], func=AF.Exp)
        pr.op("act", "activation", reads=("cg",), writes=(("eg", di),), out=eg[:, di, :], in_=gt_, func=AF.Exp)

    def conv_piece(src_d, xb, xkey, yb_, ykey, w0, pc):
        p0 = pc * PSZ
        lo = p0 - 1 if pc > 0 else 0
        hi = p0 + PSZ + 1 if pc < NPC - 1 else S
        c_lo = 0 if pc > 0 else 1
        if pc == 0:
            pr.op("dve", "memset", writes=(xkey,), ap=xb[:, 0:1], constant=0.0)
        if pc == NPC - 1:
            pr.op("dve", "memset", writes=(xkey,), ap=xb[:, PSZ + 1:PSZ + 2], constant=0.0)
        pr.dma("sp", xb[:, c_lo:c_lo + (hi - lo)], src_d[:, lo:hi], writes=(xkey,))
        pr.op("dve", "tensor_scalar", reads=(xkey, "cw"), writes=(ykey,), out=yb_[:], in0=xb[:, 1:PSZ + 1],
              scalar1=cw[:, w0 + 1:w0 + 2], scalar2=None, op0=ALU.mult)
        pr.op("dve", "scalar_tensor_tensor", reads=(xkey, "cw", ykey), writes=(ykey,), out=yb_[:], in0=xb[:, 0:PSZ],
              scalar=cw[:, w0:w0 + 1], in1=yb_[:], op0=ALU.mult, op1=ALU.add)
        pr.op("dve", "scalar_tensor_tensor", reads=(xkey, "cw", ykey), writes=(ykey,), out=yb_[:], in0=xb[:, 2:PSZ + 2],
              scalar=cw[:, w0 + 2:w0 + 3], in1=yb_[:], op0=ALU.mult, op1=ALU.add)
        pr.op("act", "activation", reads=(ykey,), writes=(ykey,), out=yb_[:], in_=yb_[:], func=AF.Silu)

    ebc = [0]
    for pc in range(NPC):
        p0 = pc * PSZ
        conv_piece(mq_d, xq, "xq", yq, "yq", 0, pc)
        conv_piece(mk_d, xk, "xk", yk, "yk", 3, pc)
        pr.op("act", "copy", reads=("yk",), writes=("kT",), out=kT[:, p0:p0 + PSZ], in_=yk[:])
        for sp_ in range(PSZ // 512):
            for di in range(2):
                tri = le if di == 0 else ge
                trik = "le" if di == 0 else "ge"
                bank = pA[ebc[0] % 2]
                bkey = "pA%d" % (ebc[0] % 2)
                es = ebc[0] % 2
                ebc[0] += 1
                for c4 in range(4):
                    n = pc * (PSZ // P) + sp_ * 4 + c4
                    pr.op("dve", "tensor_scalar", reads=("onesf", ("lf", di)), writes=(("lfbc", c4),), out=lfbc[:, c4, :],
                          in0=onesf[:], scalar1=lf[:, di, n:n + 1], scalar2=None, op0=ALU.mult)
                    pr.op("pe", "matmul", reads=(("lfbc", c4), trik), writes=(bkey,), out=bank[:, c4 * P:(c4 + 1) * P],
                          lhsT=lfbc[:, c4, :], rhs=tri[:], start=True, stop=True)
                pr.op("act", "activation", reads=(bkey,), writes=(("eb", es),), out=eb[:, es, :], in_=bank[:], func=AF.Exp)
                t0 = p0 + sp_ * 512
                pr.op("dve", "scalar_tensor_tensor", reads=("yq", ("eb", es)), writes=(("qs", di),),
                      out=qs[di][:, t0:t0 + 512], in0=yq[:, sp_ * 512:(sp_ + 1) * 512], scalar=QK_SCALE, in1=eb[:, es, :],
                      op0=ALU.mult, op1=ALU.mult)
        for c16 in range(PSZ // P):
            n = pc * (PSZ // P) + c16
            pr.op("pe", "transpose", reads=("yk", "identf"), writes=("pO",), out=pO[:, 0:P],
                  in_=yk[:, c16 * P:(c16 + 1) * P], identity=identf[:])
            for di in range(2):
                pr.op("act", "activation", reads=("pO", ("wend", di)), writes=(("kw", di),), out=kw[di][:, n, :],
                      in_=pO[:, 0:P], func=AF.Copy, scale=wend[:, di, n:n + 1])

    outs = []

    def mlstm_common(di, n):
        tok = slice(n * P, (n + 1) * P)
        msk = le if di == 0 else ge
        mskk = "le" if di == 0 else "ge"
        bank = pH[n % 2]
        bkey = "pH%d" % (n % 2)
        pr.op("pe", "matmul", reads=("kT", ("qs", di)), writes=("pS",), out=pS[:, 0:P], lhsT=kT[:, tok],
              rhs=qs[di][:, tok], start=True, stop=True)
        pr.op("dve", "scalar_tensor_tensor", reads=("pS", ("ecol", di), mskk), writes=(("ws", di),), out=ws[:, di, :],
              in0=pS[:, 0:P], scalar=ecol[:, di, n:n + 1], in1=msk[:], op0=ALU.mult, op1=ALU.mult)
        pr.op("pe", "matmul", reads=(("ws", di), "vext", "vext1"), writes=(bkey,), out=bank[:, 0:257], lhsT=ws[:, di, :],
              rhs=vext[:, n, 0:257], start=True, stop=False)
        pr.op("pe", "matmul", reads=(("qs", di), ("Cbf", di)), writes=(bkey,), out=bank[:, 0:257], lhsT=qs[di][:, tok],
              rhs=cbf[di][:], start=False, stop=True)
        pr.op("pe", "matmul", reads=(("kw", di), "vext", "vext1"), writes=("pC",), out=pC[:, 0:257], lhsT=kw[di][:, n, :],
              rhs=vext[:, n, 0:257], start=True, stop=True)
        pr.op("dve", "scalar_tensor_tensor", reads=("pC", ("eg", di), ("C", di)), writes=(("C", di),), out=cst[di][:],
              in0=cst[di][:], scalar=eg[:, di, n:n + 1], in1=pC[:, 0:257], op0=ALU.mult, op1=ALU.add)
        pr.op("act", "copy", reads=(("C", di),), writes=(("Cbf", di),), out=cbf[di][:], in_=cst[di][:])
        rc = rr[:, di:di + 1]
        pr.op("act", "activation", reads=(bkey,), writes=(("rr", di),), out=rc, in_=bank[:, 256:257], func=AF.Abs)
        pr.op("dve", "tensor_scalar", reads=(("rr", di),), writes=(("rr", di),), out=rc, in0=rc, scalar1=1.0,
              scalar2=None, op0=ALU.max)
        pr.op("dve", "reciprocal", reads=(("rr", di),), writes=(("rr", di),), out=rc, in_=rc)
        return bank, bkey, rc

    for n in range(NCH - 1, -1, -1):
        bank, bkey, rc = mlstm_common(1, n)
        s2 = n % 2
        pr.op("dve", "tensor_scalar", reads=(bkey, ("rr", 1)), writes=(("hbt", s2),), out=hbt[:, s2, :],
              in0=bank[:, 0:256], scalar1=rc, scalar2=None, op0=ALU.mult)
        pr.dma("sp", hb_d[n], hbt[:, s2, :], reads=(("hbt", s2),), writes=(("hbd", n),))

    aq4 = v["aq4"]
    smo3_r = v["smo3"]
    attT_r = v["attT_r"]
    memT_r = v["memT_r"]
    for n in range(NCH):
        s2 = n % 2
        tok = slice(n * P, (n + 1) * P)
        pr.dma("sp", qblk[:, s2, :].rearrange("p (h t) -> p h t", h=2), aq4[:, n, :, :], writes=(("qblk", s2),))
        kbs = [kb for kb in (n - 1, n, n + 1) if 0 <= kb < NCH]
        for i, kb in enumerate(kbs):
            bank = pA[i // 2]
            bkey = "pA%d" % (i // 2)
            reg = bank[:, (i % 2) * 256:(i % 2) * 256 + 256]
            masked = kb != n
            pr.op("pe", "matmul", reads=("ak", ("qblk", s2)), writes=(bkey,), out=reg, lhsT=ak[:, kb * P:(kb + 1) * P],
                  rhs=qblk[:, s2, :], start=True, stop=not masked)
            if masked:
                nm = nmp if kb < n else nmn
                nmk = "nmp" if kb < n else "nmn"
                pr.op("pe", "matmul", reads=("ident", nmk), writes=(bkey,), out=reg, lhsT=ident_bf[:], rhs=nm[:],
                      start=False, stop=True)
            pr.op("act", "activation", reads=(bkey,), writes=(("pT_sb", s2, i),), out=pT_sb[:, s2, i, :], in_=reg,
                  func=AF.Exp, scale=ATT_SCALE)
        for i, kb in enumerate(kbs):
            pr.op("pe", "matmul", reads=("av3", ("pT_sb", s2, i)), writes=("pO",), out=pO[:, 0:256], lhsT=av3[:, kb, :],
                  rhs=pT_sb[:, s2, i, :], start=(i == 0), stop=(i == len(kbs) - 1))
        for i, kb in enumerate(kbs):
            pr.op("pe", "matmul", reads=("ones_bf", ("pT_sb", s2, i)), writes=("pO",), out=pO[:, 256:512], lhsT=ones_bf[:],
                  rhs=pT_sb[:, s2, i, :], start=(i == 0), stop=(i == len(kbs) - 1))
        pr.op("dve", "tensor_tensor", reads=("pO", "esink"), writes=(("dent", s2),), out=dent[:, s2, :], in0=pO[:, 256:512],
              in1=esink[:], op=ALU.add)
        pr.op("dve", "reciprocal", reads=(("dent", s2),), writes=(("dent", s2),), out=dent[:, s2, :], in_=dent[:, s2, :])
        pr.op("dve", "tensor_tensor", reads=("pO", ("dent", s2)), writes=(("ast", s2),), out=ast[:, s2, :], in0=pO[:, 0:256],
              in1=dent[:, s2, :], op=ALU.mult)
        outs.append(pr.dma("sp", attT_r[:, :, tok], ast[:, s2, :].rearrange("d (h t) -> d h t", h=2),
                           reads=(("ast", s2),), writes=(("out", len(outs)),)))

        pr.dma("sp", hbt[:, s2, :], hb_d[n], reads=(("hbd", n),), writes=(("hbt", s2),))
        pr.dma("sp", smo_t[:, s2, :], smo3_r[:, n, :], writes=(("smo", s2),))
        bank, bkey, rc = mlstm_common(0, n)
        pr.op("dve", "scalar_tensor_tensor", reads=(bkey, ("rr", 0), ("hbt", s2)), writes=(("hsum", s2),),
              out=hsum[:, s2, :], in0=bank[:, 0:256], scalar=rc, in1=hbt[:, s2, :], op0=ALU.mult, op1=ALU.add)
        ss_c = ssn[:, s2:s2 + 1]
        rs_c = rsn[:, s2:s2 + 1]
        pr.op("act", "activation", reads=(("hsum", s2),), writes=("junk", ("ssn", s2)), out=junk[:], in_=hsum[:, s2, :],
              func=AF.Square, accum_out=ss_c)
        rstd_chain(pr, ss_c, ("ssn", s2), rs_c, ("rsn", s2), 256)
        pr.op("dve", "tensor_tensor", reads=(("smo", s2), "gn"), writes=(("smo", s2),), out=smo_t[:, s2, :],
              in0=smo_t[:, s2, :], in1=gn[:], op=ALU.mult)
        pr.op("dve", "scalar_tensor_tensor", reads=(("hsum", s2), ("rsn", s2), ("smo", s2)), writes=(("obf", s2),),
              out=obf[:, s2, :], in0=hsum[:, s2, :], scalar=rs_c, in1=smo_t[:, s2, :], op0=ALU.mult, op1=ALU.mult)
        for h2 in range(2):
            pr.op("pe", "transpose", reads=(("obf", s2), "ident"), writes=("pT",), out=pT[:, h2 * P:(h2 + 1) * P],
                  in_=obf[:, s2, h2 * P:(h2 + 1) * P], identity=ident_bf[:])
        pr.op("act", "copy", reads=("pT",), writes=(("mst", s2),), out=mst[:, s2, :], in_=pT[:, 0:256])
        outs.append(pr.dma("sp", memT_r[:, :, tok], mst[:, s2, :].rearrange("e (h t) -> e h t", h=2),
                           reads=(("mst", s2),), writes=(("out", len(outs)),)))

    pr.flush()


def emit_k2(pr, v, gate_j):
    for dst, src, k in v.get("extra_dmas", []):
        pr.dma("pool", dst, src, writes=(k,))
    ak_d, mq_d, mk_d = v["ak"], v["mq"], v["mk"]
    gb4_d, cw_d, gn_d, sink_d = v["gb4"], v["cw"], v["gn"], v["sink2"]
    identf_d, le_d, ge_d, nmp_d, nmn_d = v["identf"], v["le"], v["ge"], v["nmp"], v["nmn"]

    def sb(name, shape, dt):
        return pr.sb(name + "_s", shape, dt)
    nc = pr.nc
    identf = sb("identf", [P, P], F32)
    ident_bf = sb("ident_bf", [P, P], BF16)
    onesf = sb("onesf", [P, P], F32)
    ones_bf = sb("ones_bf", [P, P], BF16)
    le = sb("le", [P, P], F32)
    ge = sb("ge", [P, P], F32)
    nmp = sb("nmp", [P, 256], BF16)
    nmn = sb("nmn", [P, 256], BF16)
    gb4 = sb("gb4", [P, 4], F32)
    negb = sb("negb", [P, 4], F32)
    cw = sb("cw", [P, 6], F32)
    gn = sb("gn", [P, 256], F32)
    esink = sb("esink", [P, 256], F32)
    g4 = sb("g4", [P, 4, NCH], F32)
    li = sb("li", [P, 2, NCH], F32)
    lf = sb("lf", [P, 2, NCH], F32)
    cg = sb("cg", [P, 4, NCH], F32)
    ecol = sb("ecol", [P, 2, NCH], F32)
    wend = sb("wend", [P, 2, NCH], F32)
    eg = sb("eg", [P, 2, NCH], F32)
    gtmp = sb("gtmp", [P, 2, NCH], F32)
    ak = sb("ak", [P, S], BF16)
    av3 = sb("av3", [P, NCH, P], BF16)
    kT = sb("kT", [P, S], BF16)
    qs = [sb("qs_f", [P, S], BF16), sb("qs_b", [P, S], BF16)]
    kw = [sb("kw_f", [P, NCH, P], BF16), sb("kw_b", [P, NCH, P], BF16)]
    vext = sb("vext", [P, NCH, 258], BF16)
    PSZ = 512
    NPC = S // PSZ
    xq = sb("xq", [P, PSZ + 2], F32)
    yq = sb("yq", [P, PSZ], F32)
    xk = sb("xk", [P, PSZ + 2], F32)
    yk = sb("yk", [P, PSZ], F32)
    lfbc = sb("lfbc", [P, 2, 4, P], F32)
    eb = sb("eb", [P, 2, 512], F32)
    cst = [sb("C_f", [P, 257], F32), sb("C_b", [P, 257], F32)]
    cbf = [sb("Cbf_f", [P, 257], BF16), sb("Cbf_b", [P, 257], BF16)]
    ws = sb("ws", [P, 2, P], BF16)
    rr = sb("rr", [P, 4], F32)
    hbt = sb("hbt", [P, 2, 2, 258], F32)
    NCS = 3
    hcm = sb("hcm", [P, NCS, 2, 258], F32)
    rr2 = sb("rr2", [P, NCS, 2], F32)
    hsum = sb("hsum", [P, NCS, 256], F32)
    smo_t = sb("smo_t", [P, NCS, 256], F32)
    obf = sb("obf", [P, NCS, 256], BF16)
    junk = sb("junk", [P, NCS, 256], F32)
    ssn = sb("ssn", [P, NCS], F32)
    rsn = sb("rsn", [P, NCS], F32)
    mst = sb("mst", [P, NCS, 256], BF16)
    qblk = sb("qblk", [P, 2, 256], BF16)
    pT_sb = sb("pT_sb", [P, 2, 3, 256], BF16)
    dent = sb("dent", [P, 2, 256], F32)
    ast = sb("ast", [P, 2, 256], BF16)

    pX = [pr.ps("pXf", [P, 512], F32), pr.ps("pXb", [P, 512], F32)]
    pH = [pr.ps("pHf", [P, 512], F32), pr.ps("pHb", [P, 512], F32)]
    pT = pr.ps("pT", [P, 1024], BF16)
    hfb_d = v["hfb"]
    pA = [pr.ps("pA0", [P, 512], F32), pr.ps("pA1", [P, 512], F32)]
    pO = pr.ps("pO", [P, 512], F32)

    pr.dma("sp", identf[:], identf_d, writes=("identf",))
    pr.dma("pool", ident_bf[:], identf_d, writes=("ident",))
    pr.dma("sp", le[:], le_d, writes=("le",))
    pr.dma("sp", ge[:], ge_d, writes=("ge",))
    pr.dma("pool", nmp[:], nmp_d, writes=("nmp",))
    pr.dma("pool", nmn[:], nmn_d, writes=("nmn",))
    pr.dma("sp", gb4[:], gb4_d, writes=("gb4",))
    pr.dma("sp", cw[:], cw_d, writes=("cw",))
    pr.dma("sp", gn[:], gn_d, writes=("gn",))
    pr.dma("sp", esink[:], sink_d, writes=("esink",))
    if gate_j is None:
        pr.dma("sp", g4[:], v["g4"], writes=("g4",))
    else:
        gall = sb("gall", [P, NCH, 16], F32)
        pr.dma("sp", gall[:], v["gall"], writes=("gall",))
        for g in range(4):
            pr.op("dve", "tensor_copy", reads=("gall",), writes=("g4",), out=g4[:, g, :],
                  in_=gall[:, :, g * 4 + gate_j])
    pr.dma("sp", ak[:], ak_d, writes=("ak",))
    pr.dma("sp", av3[:], v["av3"], writes=("av3",))
    pr.dma("sp", vext[:, :, 0:256], v["mv3"], writes=("vext",))
    pr.op("dve", "memset", writes=("vext1",), ap=vext[:, :, 256:257], constant=1.0)
    pr.op("dve", "memset", writes=("onesf",), ap=onesf[:], constant=1.0)
    pr.op("dve", "memset", writes=("ones_bf",), ap=ones_bf[:], constant=1.0)
    for di in range(2):
        pr.op("dve", "memset", writes=(("C", di),), ap=cst[di][:], constant=0.0)
        pr.op("dve", "memset", writes=(("Cbf", di),), ap=cbf[di][:], constant=0.0)
    pr.op("act", "activation", reads=("esink",), writes=("esink",), out=esink[:], in_=esink[:], func=AF.Exp)

    def conv_piece(src_d, xb, xkey, yb_, ykey, w0, pc):
        p0 = pc * PSZ
        lo = p0 - 1 if pc > 0 else 0
        hi = p0 + PSZ + 1 if pc < NPC - 1 else S
        c_lo = 0 if pc > 0 else 1
        if pc == 0:
            pr.op("dve", "memset", writes=(xkey,), ap=xb[:, 0:1], constant=0.0)
        if pc == NPC - 1:
            pr.op("dve", "memset", writes=(xkey,), ap=xb[:, PSZ + 1:PSZ + 2], constant=0.0)
        pr.dma("sp", xb[:, c_lo:c_lo + (hi - lo)], src_d[:, lo:hi], writes=(xkey,))
        pr.op("dve", "tensor_scalar", reads=(xkey, "cw"), writes=(ykey,), out=yb_[:], in0=xb[:, 1:PSZ + 1],
              scalar1=cw[:, w0 + 1:w0 + 2], scalar2=None, op0=ALU.mult)
        pr.op("dve", "scalar_tensor_tensor", reads=(xkey, "cw", ykey), writes=(ykey,), out=yb_[:], in0=xb[:, 0:PSZ],
              scalar=cw[:, w0:w0 + 1], in1=yb_[:], op0=ALU.mult, op1=ALU.add)
        pr.op("dve", "scalar_tensor_tensor", reads=(xkey, "cw", ykey), writes=(ykey,), out=yb_[:], in0=xb[:, 2:PSZ + 2],
              scalar=cw[:, w0 + 2:w0 + 3], in1=yb_[:], op0=ALU.mult, op1=ALU.add)
        pr.op("act", "activation", reads=(ykey,), writes=(ykey,), out=yb_[:], in_=yb_[:], func=AF.Silu)


    def preproc():
        pr.op("dve", "tensor_scalar", reads=("gb4",), writes=("negb",), out=negb[:], in0=gb4[:], scalar1=-1.0,
              scalar2=None, op0=ALU.mult)
        for di in range(2):
            pr.op("dve", "tensor_scalar", reads=("g4", "gb4"), writes=(("li", di),), out=li[:, di, :],
                  in0=g4[:, 2 * di, :], scalar1=gb4[:, 2 * di:2 * di + 1], scalar2=None, op0=ALU.add)
            pr.op("act", "activation", reads=("g4", "negb"), writes=(("gtmp", di),), out=gtmp[:, di, :],
                  in_=g4[:, 2 * di + 1, :], func=AF.Exp, scale=-1.0, bias=negb[:, 2 * di + 1:2 * di + 2])
            pr.op("act", "activation", reads=(("gtmp", di),), writes=(("gtmp", di),), out=gtmp[:, di, :],
                  in_=gtmp[:, di, :], func=AF.Ln, bias=1.0)
            pr.op("dve", "tensor_scalar", reads=(("gtmp", di),), writes=(("lf", di),), out=lf[:, di, :],
                  in0=gtmp[:, di, :], scalar1=-1.0, scalar2=None, op0=ALU.mult)
        for di in range(2):
            tri = le if di == 0 else ge
            trik = "le" if di == 0 else "ge"
            pr.op("pe", "matmul", reads=(("lf", di), trik), writes=(("pH", 0),), out=pH[0][:, di * 128:di * 128 + 64],
                  lhsT=tri[:], rhs=lf[:, di, :], start=True, stop=True)
            pr.op("pe", "matmul", reads=(("lf", di), "onesf"), writes=(("pH", 0),), out=pH[0][:, di * 128 + 64:di * 128 + 128],
                  lhsT=onesf[:], rhs=lf[:, di, :], start=True, stop=True)
        pr.op("act", "copy", reads=(("pH", 0),), writes=("cg",), out=cg[:].rearrange("p g n -> p (g n)"), in_=pH[0][:, 0:256])
        for di in range(2):
            bc = cg[:, 2 * di, :]
            gt_ = cg[:, 2 * di + 1, :]
            pr.op("dve", "tensor_tensor", reads=(("li", di), "cg"), writes=(("gtmp", di),), out=gtmp[:, di, :],
                  in0=li[:, di, :], in1=bc, op=ALU.subtract)
            pr.op("act", "activation", reads=(("gtmp", di),), writes=(("ecol", di),), out=ecol[:, di, :],
                  in_=gtmp[:, di, :], func=AF.Exp)
            pr.op("dve", "tensor_tensor", reads=(("gtmp", di), "cg"), writes=(("gtmp", di),), out=gtmp[:, di, :],
                  in0=gtmp[:, di, :], in1=gt_, op=ALU.add)
            pr.op("act", "activation", reads=(("gtmp", di),), writes=(("wend", di),), out=wend[:, di, :],
                  in_=gtmp[:, di, :], func=AF.Exp)
            pr.op("act", "activation", reads=("cg",), writes=(("eg", di),), out=eg[:, di, :], in_=gt_, func=AF.Exp)

        yield

    def q_stream():
        ebc = [0]
        for pc in range(NPC):
            p0 = pc * PSZ
            conv_piece(mq_d, xq, "xq", yq, "yq", 0, pc)
            yield
            for sp_ in range(PSZ // 512):
                for di in range(2):
                    tri = le if di == 0 else ge
                    trik = "le" if di == 0 else "ge"
                    bank = pX[ebc[0] % 2]
                    bkeys = (("pX", ebc[0] % 2),)
                    es = ebc[0] % 2
                    ebc[0] += 1
                    for c4 in range(4):
                        n = pc * (PSZ // P) + sp_ * 4 + c4
                        pr.op("dve", "tensor_scalar", reads=("onesf", ("lf", di)), writes=(("lfbc", di, c4),),
                              out=lfbc[:, di, c4, :], in0=onesf[:], scalar1=lf[:, di, n:n + 1], scalar2=None, op0=ALU.mult)
                        pr.op("pe", "matmul", reads=(("lfbc", di, c4), trik), writes=bkeys, out=bank[:, c4 * P:(c4 + 1) * P],
                              lhsT=lfbc[:, di, c4, :], rhs=tri[:], start=True, stop=True)
                    yield
                    pr.op("act", "activation", reads=bkeys, writes=(("eb", es),), out=eb[:, es, :], in_=bank[:], func=AF.Exp)
                    yield
                    t0 = p0 + sp_ * 512
                    pr.op("dve", "scalar_tensor_tensor", reads=("yq", ("eb", es)), writes=(("qs", di),),
                          out=qs[di][:, t0:t0 + 512], in0=yq[:, sp_ * 512:(sp_ + 1) * 512], scalar=QK_SCALE, in1=eb[:, es, :],
                          op0=ALU.mult, op1=ALU.mult)
                    yield

    def k_stream():
        for pc in range(NPC):
            p0 = pc * PSZ
            conv_piece(mk_d, xk, "xk", yk, "yk", 3, pc)
            yield
            pr.op("act", "copy", reads=("yk",), writes=("kT",), out=kT[:, p0:p0 + PSZ], in_=yk[:])
            for c16 in range(PSZ // P):
                n = pc * (PSZ // P) + c16
                pr.op("pe", "transpose", reads=("yk", "identf"), writes=(("pH", 1),), out=pH[1][:, 0:P],
                      in_=yk[:, c16 * P:(c16 + 1) * P], identity=identf[:])
                yield
                for di in range(2):
                    pr.op("act", "activation", reads=(("pH", 1), ("wend", di)), writes=(("kw", di),), out=kw[di][:, n, :],
                          in_=pH[1][:, 0:P], func=AF.Copy, scale=wend[:, di, n:n + 1])
                yield

    outs = []
    smo3_r = v["smo3"]
    aq4 = v["aq4"]
    att_dst = v["att_dst"]
    mem_dst = v["mem_dst"]

    progress = [0, 0]
    LAG = 3

    def scan_stream(di):
        order = range(NCH) if di == 0 else range(NCH - 1, -1, -1)
        msk = le if di == 0 else ge
        mskk = "le" if di == 0 else "ge"
        X = pX[di]
        H = pH[di]
        for n in order:
            tok = slice(n * P, (n + 1) * P)
            s2 = n % 2
            pr.op("pe", "matmul", reads=("kT", ("qs", di)), writes=(("pX", di),), out=X[:, 0:P], lhsT=kT[:, tok],
                  rhs=qs[di][:, tok], start=True, stop=True)
            yield
            pr.op("dve", "scalar_tensor_tensor", reads=(("pX", di), ("ecol", di), mskk), writes=(("ws", di),),
                  out=ws[:, di, :], in0=X[:, 0:P], scalar=ecol[:, di, n:n + 1], in1=msk[:], op0=ALU.mult, op1=ALU.mult)
            yield
            pr.op("pe", "matmul", reads=(("ws", di), "vext", "vext1"), writes=(("pH", di),), out=H[:, 0:257],
                  lhsT=ws[:, di, :], rhs=vext[:, n, 0:257], start=True, stop=False)
            pr.op("pe", "matmul", reads=(("qs", di), ("Cbf", di)), writes=(("pH", di),), out=H[:, 0:257],
                  lhsT=qs[di][:, tok], rhs=cbf[di][:], start=False, stop=True)
            pr.op("pe", "matmul", reads=(("kw", di), "vext", "vext1"), writes=(("pX", di),), out=X[:, 128:385],
                  lhsT=kw[di][:, n, :], rhs=vext[:, n, 0:257], start=True, stop=True)
            yield
            pr.op("dve", "scalar_tensor_tensor", reads=(("pX", di), ("eg", di), ("C", di)), writes=(("C", di),),
                  out=cst[di][:], in0=cst[di][:], scalar=eg[:, di, n:n + 1], in1=X[:, 128:385], op0=ALU.mult, op1=ALU.add)
            yield
            pr.op("pool", "tensor_copy", reads=(("C", di),), writes=(("Cbf", di),), out=cbf[di][:], in_=cst[di][:])
            yield
            pr.op("act", "copy", reads=(("pH", di),), writes=(("hbt", di, s2),), out=hbt[:, di, s2, 0:257], in_=H[:, 0:257])
            pr.dma("pool", hfb_d[di, n, :, 0:257], hbt[:, di, s2, 0:257], reads=(("hbt", di, s2),), writes=(("hfd", di, n),))
            progress[di] += 1
            yield

    def attention_stream():
        for n in range(NCH):
            s2 = n % 2
            tok = slice(n * P, (n + 1) * P)
            pr.dma("sp", qblk[:, s2, :].rearrange("p (h t) -> p h t", h=2), aq4[:, n, :, :], writes=(("qblk", s2),))
            kbs = [kb for kb in (n - 1, n, n + 1) if 0 <= kb < NCH]
            for i, kb in enumerate(kbs):
                bank = pA[i // 2]
                bkey = "pA%d" % (i // 2)
                reg = bank[:, (i % 2) * 256:(i % 2) * 256 + 256]
                masked = kb != n
                pr.op("pe", "matmul", reads=("ak", ("qblk", s2)), writes=(bkey,), out=reg, lhsT=ak[:, kb * P:(kb + 1) * P],
                      rhs=qblk[:, s2, :], start=True, stop=not masked)
                if masked:
                    nm = nmp if kb < n else nmn
                    nmk = "nmp" if kb < n else "nmn"
                    pr.op("pe", "matmul", reads=("ident", nmk), writes=(bkey,), out=reg, lhsT=ident_bf[:], rhs=nm[:],
                          start=False, stop=True)
                yield
                pr.op("act", "activation", reads=(bkey,), writes=(("pT_sb", s2, i),), out=pT_sb[:, s2, i, :], in_=reg,
                      func=AF.Exp, scale=ATT_SCALE)
                yield
            for i, kb in enumerate(kbs):
                pr.op("pe", "matmul", reads=("av3", ("pT_sb", s2, i)), writes=("pO",), out=pO[:, 0:256], lhsT=av3[:, kb, :],
                      rhs=pT_sb[:, s2, i, :], start=(i == 0), stop=(i == len(kbs) - 1))
            for i, kb in enumerate(kbs):
                pr.op("pe", "matmul", reads=("ones_bf", ("pT_sb", s2, i)), writes=("pO",), out=pO[:, 256:512],
                      lhsT=ones_bf[:], rhs=pT_sb[:, s2, i, :], start=(i == 0), stop=(i == len(kbs) - 1))
            yield
            pr.op("dve", "tensor_tensor", reads=("pO", "esink"), writes=(("dent", s2),), out=dent[:, s2, :],
                  in0=pO[:, 256:512], in1=esink[:], op=ALU.add)
            pr.op("dve", "reciprocal", reads=(("dent", s2),), writes=(("dent", s2),), out=dent[:, s2, :], in_=dent[:, s2, :])
            yield
            pr.op("dve", "tensor_tensor", reads=("pO", ("dent", s2)), writes=(("ast", s2),), out=ast[:, s2, :],
                  in0=pO[:, 0:256], in1=dent[:, s2, :], op=ALU.mult)
            outs.append(pr.dma("sp", att_dst(n), ast[:, s2, :].rearrange("d (h t) -> d h t", h=2),
                               reads=(("ast", s2),), writes=(("out", len(outs)),)))
            yield

    comb_order = sorted(range(NCH), key=lambda n: (max(n, NCH - 1 - n), n))

    def combine_stream(r):
        for n in comb_order[r::NCS]:
            need = min(NCH, max(n, NCH - 1 - n) + 1 + LAG)
            while min(progress) < need:
                yield
            tok = slice(n * P, (n + 1) * P)
            for di in range(2):
                pr.dma("pool", hcm[:, r, di, 0:257], hfb_d[di, n, :, 0:257], reads=(("hfd", di, n),), writes=(("hcm", r, di),))
            pr.dma("sp", smo_t[:, r, :], smo3_r[:, n, :], writes=(("smo", r),))
            yield
            pr.op("act", "activation", reads=(("hcm", r, 0), ("hcm", r, 1)), writes=(("rr2", r),), out=rr2[:, r, :],
                  in_=hcm[:, r, :, 256], func=AF.Abs)
            yield
            pr.op("dve", "tensor_scalar", reads=(("rr2", r),), writes=(("rr2", r),), out=rr2[:, r, :], in0=rr2[:, r, :],
                  scalar1=1.0, scalar2=None, op0=ALU.max)
            pr.op("dve", "reciprocal", reads=(("rr2", r),), writes=(("rr2", r),), out=rr2[:, r, :], in_=rr2[:, r, :])
            yield
            pr.op("dve", "tensor_scalar", reads=(("hcm", r, 0), ("rr2", r)), writes=(("hsum", r),), out=hsum[:, r, :],
                  in0=hcm[:, r, 0, 0:256], scalar1=rr2[:, r, 0:1], scalar2=None, op0=ALU.mult)
            pr.op("dve", "scalar_tensor_tensor", reads=(("hcm", r, 1), ("rr2", r), ("hsum", r)), writes=(("hsum", r),),
                  out=hsum[:, r, :], in0=hcm[:, r, 1, 0:256], scalar=rr2[:, r, 1:2], in1=hsum[:, r, :],
                  op0=ALU.mult, op1=ALU.add)
            yield
            ss_c = ssn[:, r:r + 1]
            rs_c = rsn[:, r:r + 1]
            pr.op("act", "activation", reads=(("hsum", r),), writes=(("junk", r), ("ssn", r)), out=junk[:, r, :],
                  in_=hsum[:, r, :], func=AF.Square, accum_out=ss_c)
            pr.op("dve", "tensor_tensor", reads=(("smo", r), "gn"), writes=(("smo", r),), out=smo_t[:, r, :],
                  in0=smo_t[:, r, :], in1=gn[:], op=ALU.mult)
            yield
            pr.op("dve", "tensor_scalar", reads=(("ssn", r),), writes=(("rsn", r),), out=rs_c, in0=ss_c,
                  scalar1=1.0 / 256, scalar2=EPS, op0=ALU.mult, op1=ALU.add)
            yield
            pr.op("act", "activation", reads=(("rsn", r),), writes=(("rsn", r),), out=rs_c, in_=rs_c, func=AF.Sqrt)
            yield
            pr.op("dve", "reciprocal", reads=(("rsn", r),), writes=(("rsn", r),), out=rs_c, in_=rs_c)
            yield
            pr.op("dve", "scalar_tensor_tensor", reads=(("hsum", r), ("rsn", r), ("smo", r)), writes=(("obf", r),),
                  out=obf[:, r, :], in0=hsum[:, r, :], scalar=rs_c, in1=smo_t[:, r, :], op0=ALU.mult, op1=ALU.mult)
            yield
            for h2 in range(2):
                pr.op("pe", "transpose", reads=(("obf", r), "ident"), writes=("pT",),
                      out=pT[:, r * 256 + h2 * P:r * 256 + (h2 + 1) * P], in_=obf[:, r, h2 * P:(h2 + 1) * P],
                      identity=ident_bf[:])
            yield
            pr.op("act", "copy", reads=("pT",), writes=(("mst", r),), out=mst[:, r, :], in_=pT[:, r * 256:(r + 1) * 256])
            outs.append(pr.dma("sp", mem_dst(n), mst[:, r, :].rearrange("e (h t) -> e h t", h=2),
                               reads=(("mst", r),), writes=(("out", len(outs)),)))
            yield

    pre = preproc()
    active = [pre, attention_stream()]
    while active:
        for g in list(active):
            try:
                next(g)
            except StopIteration:
                active.remove(g)
                if g is pre:
                    qg, kg = q_stream(), k_stream()
                    pend = {id(qg), id(kg)}
                    active.extend([qg, kg])
                elif "pend" in dir() and id(g) in pend:
                    pend.discard(id(g))
                    if not pend:
                        active.extend([scan_stream(0), scan_stream(1)] + [combine_stream(r) for r in range(NCS)])
    pr.flush()


def _wx_index():
    cols = [np.arange(0, 1280), np.arange(1536, 2560), np.arange(1280, 1536), np.arange(4608, 4624),
            np.arange(2560, 4608)]
    return np.concatenate(cols)


def _rope_tables():
    half = 64
    inv_freq = (np.float32(10000.0) ** (-np.arange(half, dtype=np.float32) / np.float32(half))).astype(np.float32)
    pos = np.arange(S, dtype=np.float32)
    ang = (pos[:, None] * inv_freq[None, :]).astype(np.float32)
    cos = np.cos(ang).astype(np.float32)
    sin = np.sin(ang).astype(np.float32)
    cosT = np.concatenate([cos, cos], axis=1).T
    sinT = np.concatenate([-sin, sin], axis=1).T
    return np.ascontiguousarray(cosT), np.ascontiguousarray(sinT)


_CACHE = {}
_RUNKW = {}
_LAST = {}


def _get(name, builder):
    if name not in _CACHE:
        _CACHE[name] = builder()
    return _CACHE[name]


def run_k1(x_flat, w_in_l, g_pre_l):
    nc = _get("k1", build_k1)
    wx = np.ascontiguousarray(w_in_l[:, _get("wxi", _wx_index)])
    cosT, sinT = _get("rope", _rope_tables)
    gcol = np.ascontiguousarray(g_pre_l.reshape(KC, P).T)
    ident = np.eye(P, dtype=np.float32)
    in_maps = []
    for c in range(NCORE):
        p0 = (c % 4) * TOK
        in_maps.append({
            "x": np.ascontiguousarray(x_flat[c * TOK:(c + 1) * TOK]),
            "w_in": wx, "gcol": gcol,
            "cosT": np.ascontiguousarray(cosT[:, p0:p0 + TOK]),
            "sinT": np.ascontiguousarray(sinT[:, p0:p0 + TOK]),
            "ident": ident,
        })
    res = run_bass_kernel_spmd(nc, in_maps, core_ids=list(range(NCORE)), **_RUNKW)
    _LAST["t"] = res.exec_time_ns
    return res.results


def _k2_consts():
    r = np.arange(P)
    le = (r[:, None] <= r[None, :]).astype(np.float32)
    ge = (r[:, None] >= r[None, :]).astype(np.float32)
    nmp1 = np.where(r[:, None] < r[None, :], np.float32(-30000.0), np.float32(0.0)).astype(np.float32)
    nmn1 = np.where(r[:, None] > r[None, :], np.float32(-30000.0), np.float32(0.0)).astype(np.float32)
    return {
        "identf": np.eye(P, dtype=np.float32), "le": le, "ge": ge,
        "nmp": np.ascontiguousarray(np.concatenate([nmp1, nmp1], axis=1)),
        "nmn": np.ascontiguousarray(np.concatenate([nmn1, nmn1], axis=1)),
    }


def run_k2(k1, conv_w_l, gate_bias_l, ml_norm_g_l, attn_sink_l):
    nc = _get("k2", build_k2)
    consts = _get("k2c", _k2_consts)
    ca = np.ascontiguousarray
    in_maps = []
    for c in range(NCORE):
        b, j = c // 4, c % 4
        kv = j // 2
        cores = k1[4 * b:4 * b + 4]
        AQ = np.concatenate([r["aq"][2 * j * P:(2 * j + 2) * P] for r in cores], axis=1)
        AK = np.concatenate([r["ak"][kv * P:(kv + 1) * P] for r in cores], axis=1)
        AV = np.concatenate([r["av"][:, kv * P:(kv + 1) * P] for r in cores], axis=0)
        MQ = np.concatenate([r["mqk"][j * P:(j + 1) * P] for r in cores], axis=1)
        MK = np.concatenate([r["mqk"][512 + j * P:512 + (j + 1) * P] for r in cores], axis=1)
        MV = np.concatenate([r["mv"][:, j * 256:(j + 1) * 256] for r in cores], axis=0)
        SMO = np.concatenate([r["smo"][:, j * 256:(j + 1) * 256] for r in cores], axis=0)
        GT = np.concatenate([r["gt"][:, [j, 4 + j, 8 + j, 12 + j]] for r in cores], axis=0)
        m = dict(consts)
        m["aq2"] = ca(AQ.reshape(2, P, NCH, P).transpose(1, 2, 0, 3).reshape(P, NCH * 256))
        m["ak"] = ca(AK)
        m["av3"] = ca(AV.reshape(NCH, P, P).transpose(1, 0, 2).reshape(P, NCH * P))
        m["mq"] = ca(MQ)
        m["mk"] = ca(MK)
        m["mv3"] = ca(MV.reshape(NCH, P, 256).transpose(1, 0, 2).reshape(P, NCH * 256))
        m["smo3"] = ca(SMO.reshape(NCH, P, 256).transpose(1, 0, 2).reshape(P, NCH * 256))
        m["g4"] = ca(GT.reshape(NCH, P, 4).transpose(1, 2, 0).reshape(P, 4 * NCH))
        m["gb4"] = ca(np.broadcast_to(gate_bias_l[[j, 4 + j, 8 + j, 12 + j]][None, :], (P, 4)))
        cwq = conv_w_l[:, j * P:(j + 1) * P].T
        cwk = conv_w_l[:, 512 + j * P:512 + (j + 1) * P].T
        m["cw"] = ca(np.concatenate([cwq, cwk], axis=1))
        m["gn"] = ca(np.broadcast_to(ml_norm_g_l[j * 256:(j + 1) * 256][None, :], (P, 256)))
        m["sink2"] = ca(np.broadcast_to(np.repeat(attn_sink_l[2 * j:2 * j + 2], P)[None, :], (P, 256)))
        in_maps.append(m)
    res = run_bass_kernel_spmd(nc, in_maps, core_ids=list(range(NCORE)), **_RUNKW)
    _LAST["t"] = res.exec_time_ns
    out = res.results
    mixT = []
    for c in range(NCORE):
        b, part = c // 4, c % 4
        sl = slice(part * TOK, (part + 1) * TOK)
        att = [out[4 * b + j]["attT"][:, sl] for j in range(4)]
        mem = [out[4 * b + j]["memT"][:, sl] for j in range(4)]
        mixT.append(ca(np.concatenate(att + mem, axis=0)))
    return mixT


def kernel_unfused(x, w_in, conv_w, gate_bias, ml_norm_g, attn_sink, w_out,
                   g_pre_mix, g_post_mix, g_pre_mlp, g_post_mlp, w_up, w_down):
    f = lambda a: np.ascontiguousarray(np.asarray(a, dtype=np.float32))
    x = f(x)
    xf = x.reshape(B * S, D)
    depth = w_in.shape[0]
    for l in range(depth):
        k1 = run_k1(xf, f(w_in[l]), f(g_pre_mix[l]))
        mixT = run_k2(k1, f(conv_w[l]), f(gate_bias[l]), f(ml_norm_g[l]), f(attn_sink[l]))
        del k1
        ys = run_k3(mixT, xf, f(w_out[l]), f(w_up[l]), f(w_down[l]), f(g_post_mix[l]), f(g_pre_mlp[l]),
                    f(g_post_mlp[l]))
        xf = np.concatenate(ys, axis=0)
    return xf.reshape(B, S, D).astype(np.float32)


DEPTH = 2


def build_fused():
    nc = bass.Bass("TRN2", target_bir_lowering=False)

    def din(name, shape, dt=F32):
        return nc.dram_tensor(name, list(shape), dt, kind="ExternalInput").ap()

    def dint(name, shape, dt):
        return nc.dram_tensor(name, list(shape), dt).ap()
    x_in = din("x", [S, D])
    w_in = din("w_in", [DEPTH, D, WX_COLS])
    w_out = din("w_out", [DEPTH, D, D])
    w_up = din("w_up", [DEPTH, D, DFF])
    w_down = din("w_down", [DEPTH, DFF, D])
    gcol1 = din("gcol1", [DEPTH, P, KC])
    gcol2 = din("gcol2", [DEPTH, P, KC])
    g_pm = din("g_pm", [DEPTH, P, D])
    g_pl = din("g_pl", [DEPTH, P, D])
    cosT = din("cosT", [P, S])
    sinT = din("sinT", [P, S])
    ident = din("ident", [P, P])
    gb4 = din("gb4", [DEPTH, 4, P, 4])
    cw = din("cw", [DEPTH, 4, P, 6])
    gn = din("gn", [DEPTH, 4, P, 256])
    sink2 = din("sink2", [DEPTH, 4, P, 256])
    le = din("le", [P, P])
    ge = din("ge", [P, P])
    nmp = din("nmp", [P, 256])
    nmn = din("nmn", [P, 256])
    y_out = nc.dram_tensor("y", [TOK, D], F32, kind="ExternalOutput").ap()
    xidx_d = din("xidx", [P, TOK // P], mybir.dt.int32)
    midx_d = din("midx", [P, (TOK // 512) * KC], mybir.dt.int32)
    aq = dint("aq_s", [1024, S], BF16)
    ak = dint("ak_s", [256, S], BF16)
    mqk = dint("mqk_s", [1024, S], F32)
    av = dint("av_s", [S, 256], BF16)
    mv = dint("mv_s", [S, 1024], BF16)
    smo = dint("smo_s", [S, 1024], F32)
    gt = dint("gt_s", [P, NCH, 16], F32)
    NBLK = S // 512
    mixB = dint("mixB_s", [NBLK, D, 512], BF16)
    x1 = dint("x1_s", [S, D], F32)
    hb = dint("hb_s", [NCH, P, 256], F32)
    hfb = dint("hfb_s", [2, NCH, P, 258], F32)

    wb = []
    casts = []
    for l in range(DEPTH):
        wl = {"w_in": dint("wb_in%d" % l, [D, WX_COLS], BF16), "w_out": dint("wb_out%d" % l, [D, D], BF16),
              "w_up": dint("wb_up%d" % l, [D, DFF], BF16), "w_down": dint("wb_down%d" % l, [DFF, D], BF16)}
        wb.append(wl)
        cl = {}
        for nm, src in (("w_in", w_in[l]), ("w_out", w_out[l]), ("w_up", w_up[l]), ("w_down", w_down[l])):
            rows = src.shape[0]
            cl[nm] = [(wl[nm][r0:r0 + P, :], src[r0:r0 + P, :], ("wcast", l, nm, r0)) for r0 in range(0, rows, P)]
        casts.append(cl)

    pr = Prog(nc)
    for dst, src, k in casts[0]["w_in"]:
        pr.dma("pool", dst, src, writes=(k,))
    pr.flush()
    rest0 = casts[0]["w_out"] + casts[0]["w_up"] + casts[0]["w_down"]
    all1 = casts[1]["w_in"] + casts[1]["w_out"] + casts[1]["w_up"] + casts[1]["w_down"]
    q1 = (len(all1) + 3) // 4
    for l in range(DEPTH):
        xsrc = x_in if l == 0 else x1
        ydst = x1 if l == 0 else y_out
        emit_k1(pr, {"x": xsrc, "w": wb[l]["w_in"], "gcol": gcol1[l], "cos": cosT, "sin": sinT, "ident": ident,
                     "aq": aq, "ak": ak, "mqk": mqk, "av": av, "mv": mv, "smo": smo, "gt": gt}, S, True,
                extra_dmas=(rest0 if l == 0 else None))
        for j in range(4):
            kv = j // 2
            v = {
                "aq4": aq[2 * j * P:(2 * j + 2) * P, :].rearrange("(h d) (n t) -> d n h t", h=2, t=P),
                "ak": ak[kv * P:(kv + 1) * P, :],
                "av3": av[:, kv * P:(kv + 1) * P].rearrange("(n t) d -> t n d", t=P),
                "mq": mqk[j * P:(j + 1) * P, :],
                "mk": mqk[512 + j * P:512 + (j + 1) * P, :],
                "mv3": mv[:, j * 256:(j + 1) * 256].rearrange("(n t) e -> t n e", t=P),
                "smo3": smo[:, j * 256:(j + 1) * 256].rearrange("(n t) e -> t n e", t=P),
                "gall": gt,
                "gb4": gb4[l, j], "cw": cw[l, j], "gn": gn[l, j], "sink2": sink2[l, j],
                "identf": ident, "le": le, "ge": ge, "nmp": nmp, "nmn": nmn,
                "att_dst": (lambda n, j=j: mixB[n // 4, 2 * j * P:(2 * j + 2) * P, (n % 4) * P:(n % 4 + 1) * P]
                            .rearrange("(h d) t -> d h t", h=2)),
                "mem_dst": (lambda n, j=j: mixB[n // 4, 1024 + j * 256:1024 + (j + 1) * 256, (n % 4) * P:(n % 4 + 1) * P]
                            .rearrange("(h e) t -> e h t", h=2)),
                "hb": hb, "hfb": hfb,
                "extra_dmas": [],
            }
            emit_k2(pr, v, j)
        k3d = {"x": xsrc, "w_out": wb[l]["w_out"], "w_up": wb[l]["w_up"], "w_down": wb[l]["w_down"],
               "g_pm": g_pm[l], "g_pl": g_pl[l], "gcol": gcol2[l], "ident": ident, "y": ydst,
               "mix_blk": lambda tb: mixB[tb].rearrange("(kc p) t -> p kc t", p=P),
               "extra_dmas": (all1 if l == 0 else None)}
        if l == DEPTH - 1:
            k3d["gather"] = {"xidx": xidx_d, "midx": midx_d, "mix_flat": mixB.rearrange("b f t -> (b f) t")}
            emit_k3(pr, k3d, TOK)
        else:
            emit_k3(pr, k3d, S)
    pr.emit()
    return nc


def kernel(x, w_in, conv_w, gate_bias, ml_norm_g, attn_sink, w_out,
           g_pre_mix, g_post_mix, g_pre_mlp, g_post_mlp, w_up, w_down):
    f = lambda a: np.ascontiguousarray(np.asarray(a, dtype=np.float32))
    ca = np.ascontiguousarray
    nc = _get("fused", build_fused)
    x = f(x)
    w_in, conv_w, gate_bias, ml_norm_g, attn_sink = f(w_in), f(conv_w), f(gate_bias), f(ml_norm_g), f(attn_sink)
    g_pre_mix, g_post_mix, g_pre_mlp, g_post_mlp = f(g_pre_mix), f(g_post_mix), f(g_pre_mlp), f(g_post_mlp)
    wxi = _get("wxi", _wx_index)
    cosT, sinT = _get("rope", _rope_tables)
    consts = _get("k2c", _k2_consts)
    shared = {
        "w_in": ca(w_in[:, :, wxi]), "w_out": f(w_out), "w_up": f(w_up), "w_down": f(w_down),
        "gcol1": ca(g_pre_mix.reshape(DEPTH, KC, P).transpose(0, 2, 1)),
        "gcol2": ca(g_pre_mlp.reshape(DEPTH, KC, P).transpose(0, 2, 1)),
        "g_pm": ca(np.broadcast_to(g_post_mix[:, None, :], (DEPTH, P, D))),
        "g_pl": ca(np.broadcast_to(g_post_mlp[:, None, :], (DEPTH, P, D))),
        "cosT": cosT, "sinT": sinT, "ident": consts["identf"],
        "le": consts["le"], "ge": consts["ge"], "nmp": consts["nmp"], "nmn": consts["nmn"],
    }
    gb4 = np.zeros((DEPTH, 4, P, 4), np.float32)
    cw = np.zeros((DEPTH, 4, P, 6), np.float32)
    gn = np.zeros((DEPTH, 4, P, 256), np.float32)
    sink2 = np.zeros((DEPTH, 4, P, 256), np.float32)
    for l in range(DEPTH):
        for j in range(4):
            gb4[l, j] = gate_bias[l][[j, 4 + j, 8 + j, 12 + j]][None, :]
            cw[l, j, :, 0:3] = conv_w[l][:, j * P:(j + 1) * P].T
            cw[l, j, :, 3:6] = conv_w[l][:, 512 + j * P:512 + (j + 1) * P].T
            gn[l, j] = ml_norm_g[l][j * 256:(j + 1) * 256][None, :]
            sink2[l, j] = np.repeat(attn_sink[l][2 * j:2 * j + 2], P)[None, :]
    shared.update({"gb4": gb4, "cw": cw, "gn": gn, "sink2": sink2})
    in_maps = []
    pp = np.arange(P, dtype=np.int32)[:, None]
    for c in range(NCORE):
        part = c % 4
        m = dict(shared)
        m["x"] = ca(x[c // 4])
        m["xidx"] = ca((part * TOK + np.arange(TOK // P, dtype=np.int32)[None, :] * P + pp).astype(np.int32))
        tb = np.arange(TOK // 512, dtype=np.int32)[:, None]
        kc = np.arange(KC, dtype=np.int32)[None, :]
        rows = ((part * (TOK // 512) + tb) * D + kc * P).reshape(1, -1)
        m["midx"] = ca((rows + pp).astype(np.int32))
        in_maps.append(m)
    res = run_bass_kernel_spmd(nc, in_maps, core_ids=list(range(NCORE)), **_RUNKW)
    _LAST["t"] = res.exec_time_ns
    out = np.empty((B, S, D), np.float32)
    for c in range(NCORE):
        out[c // 4, (c % 4) * TOK:(c % 4 + 1) * TOK] = res.results[c]["y"]
    return out
```

```python
import contextlib
import numpy as np
import ml_dtypes
import concourse.bass as bass
import concourse.mybir as mybir
from concourse.bass_utils import run_bass_kernel_spmd

F32 = mybir.dt.float32
BF16 = mybir.dt.bfloat16
AF = mybir.ActivationFunctionType
ALU = mybir.AluOpType
AX = mybir.AxisListType

D = 2048
S = 8192
B = 2
NCORE = 8
TOK = 2048
P = 128
KC = D // P
IN_COLS = 4624
DFF = 8192
EPS = 1e-6
WX_COLS = 4624

ENGS = ("sp", "act", "dve", "pool", "pe")
NSLOT = 8


class Op:
    __slots__ = ("eng", "fn", "deps", "sig", "val", "dma", "slot", "slotval", "nm")

    def __init__(self, eng, fn, dma):
        self.eng = eng
        self.fn = fn
        self.deps = []
        self.sig = False
        self.val = 0
        self.dma = dma
        self.slot = 0
        self.slotval = 0


class Prog:
    def __init__(self, nc):
        self.nc = nc
        self.ops = {e: [] for e in ENGS}
        self.res = {}
        self.stack = contextlib.ExitStack()
        self.gstack = contextlib.ExitStack()
        self.sem_eng = {e: self.gstack.enter_context(nc.semaphore("sem_" + e)) for e in ENGS}
        self.sem_dma = {e: [self.gstack.enter_context(nc.semaphore("dq_%s_%d" % (e, i))) for i in range(NSLOT)]
                        for e in ENGS if e != "pe"}
        self.cnt = {e: 0 for e in ENGS}
        self.dk = {e: 0 for e in ENGS}
        self.seen = {e: {} for e in ENGS}
        self.batch = 0
        self.uid = 0

    def sb(self, name, shape, dt):
        self.uid += 1
        return self.stack.enter_context(self.nc.sbuf_tensor("%s_%d" % (name, self.uid), list(shape), dt))

    def ps(self, name, shape, dt):
        self.uid += 1
        return self.stack.enter_context(self.nc.psum_tensor("%s_%d" % (name, self.uid), list(shape), dt))

    def add(self, eng, fn, reads=(), writes=(), dma=False):
        op = Op(eng, fn, dma)
        op.nm = self.batch
        deps = set()
        for r in reads:
            st = self.res.get(r)
            if st is not None and st[0] is not None:
                deps.add(st[0])
        for w in writes:
            st = self.res.get(w)
            if st is not None:
                if st[0] is not None:
                    deps.add(st[0])
                deps.update(st[1].values())
                deps.update(st[2])
        for r in reads:
            st = self.res.get(r)
            if st is None:
                st = [None, {}, []]
                self.res[r] = st
            if dma:
                st[2].append(op)
            else:
                st[1][eng] = op
        for w in writes:
            self.res[w] = [op, {}, []]
        deps.discard(op)
        for d in deps:
            if d.nm != self.batch:
                continue
            if d.eng == "pe" and eng == "pe" and not d.dma and not dma:
                continue
            d.sig = True
            op.deps.append(d)
        self.ops[eng].append(op)
        return op

    def op(self, eng, method, reads=(), writes=(), **kw):
        return self.add(eng, lambda e: getattr(e, method)(**kw), reads, writes)

    def dma(self, eng, out, in_, reads=(), writes=()):
        return self.add(eng, lambda e: e.dma_start(out=out, in_=in_), reads, writes, dma=True)

    def fence(self, eng, reads):
        return self.add(eng, None, reads=reads, writes=())

    def flush(self):
        nc = self.nc
        sem_eng, sem_dma = self.sem_eng, self.sem_dma
        for e in ENGS:
            last = None
            for op in self.ops[e]:
                if not op.dma and op.fn is not None:
                    last = op
            if last is not None:
                last.sig = True
            for op in self.ops[e]:
                if op.dma:
                    k = self.dk[e]
                    op.slot = k % NSLOT
                    op.slotval = 16 * (k // NSLOT + 1)
                    self.dk[e] = k + 1
                elif op.sig and op.fn is not None:
                    self.cnt[e] += 1
                    op.val = self.cnt[e]
        targets = []
        for e in ENGS:
            if self.cnt[e] > 0:
                targets.append((e, sem_eng[e], self.cnt[e]))
            if e != "pe":
                k = self.dk[e]
                for sl in range(NSLOT):
                    n_used = (k - sl + NSLOT - 1) // NSLOT if k > sl else 0
                    if n_used > 0:
                        targets.append((None, sem_dma[e][sl], 16 * n_used))

        def emit_engine(ename, eng):
            seen = self.seen[ename]
            for op in self.ops[ename]:
                waits = {}
                for d in op.deps:
                    if d.dma:
                        key = sem_dma[d.eng][d.slot]
                        v = d.slotval
                    else:
                        key = sem_eng[d.eng]
                        v = d.val
                    if waits.get(key, 0) < v:
                        waits[key] = v
                if op.dma and op.slotval > 16:
                    key = sem_dma[ename][op.slot]
                    v = op.slotval - 16
                    if waits.get(key, 0) < v:
                        waits[key] = v
                for key, v in waits.items():
                    if seen.get(key, 0) >= v:
                        continue
                    seen[key] = v
                    eng.wait_ge(key, v)
                if op.fn is None:
                    continue
                ins = op.fn(eng)
                if op.dma:
                    ins.then_inc(sem_dma[ename][op.slot], 16)
                elif op.sig:
                    ins.then_inc(sem_eng[ename], 1)
            for (te, key, v) in targets:
                if te == ename:
                    continue
                if seen.get(key, 0) >= v:
                    continue
                seen[key] = v
                eng.wait_ge(key, v)

        with nc.Block() as block:
            @block.sync
            def _(eng):
                emit_engine("sp", eng)

            @block.scalar
            def _(eng):
                emit_engine("act", eng)

            @block.vector
            def _(eng):
                emit_engine("dve", eng)

            @block.gpsimd
            def _(eng):
                emit_engine("pool", eng)

            @block.tensor
            def _(eng):
                emit_engine("pe", eng)
        self.ops = {e: [] for e in ENGS}
        self.stack.close()
        self.stack = contextlib.ExitStack()
        self.batch += 1

    def emit(self):
        self.flush()
        self.gstack.close()


class Banks:
    def __init__(self, prog, names):
        self.tiles = [(n, prog.ps(n, [P, 512], F32)) for n in names]
        self.i = 0

    def next(self):
        t = self.tiles[self.i % len(self.tiles)]
        self.i += 1
        return t


def mm_group(prog, out_ap, bank_key, pairs, extra_reads=()):
    n = len(pairs)
    last = None
    for i, (l, r, rk) in enumerate(pairs):
        def fn(e, l=l, r=r, i=i):
            return e.matmul(out_ap, l, r, start=(i == 0), stop=(i == n - 1))
        last = prog.add("pe", fn, reads=tuple(rk) + tuple(extra_reads), writes=(bank_key,))
    return last


def rmsnorm_to_featmajor(prog, pe_banks_t, x_tile, xkey, g_col, hT, hT_key, col0, ident_bf, scr, ti):
    ss, rstd, xs = scr["ss"], scr["rstd"], scr["xs"]
    sfx = ti % 2
    ssk, rsk, xsk = ("ss", sfx), ("rstd", sfx), ("xs", sfx)
    ss_c = ss[:, sfx:sfx + 1]
    rs_c = rstd[:, sfx:sfx + 1]
    xs_t = xs[:, sfx, :]
    prog.add("act", lambda e: e.activation(out=xs_t, in_=x_tile, func=AF.Square, accum_out=ss_c),
             reads=(xkey,), writes=(xsk, ssk))
    prog.add("dve", lambda e: e.tensor_scalar(out=rs_c, in0=ss_c, scalar1=1.0 / D, scalar2=EPS,
                                              op0=ALU.mult, op1=ALU.add),
             reads=(ssk,), writes=(rsk,))
    prog.add("act", lambda e: e.activation(out=rs_c, in_=rs_c, func=AF.Sqrt), reads=(rsk,), writes=(rsk,))
    prog.add("dve", lambda e: e.reciprocal(out=rs_c, in_=rs_c), reads=(rsk,), writes=(rsk,))
    prog.add("act", lambda e: e.activation(out=xs_t, in_=x_tile, func=AF.Copy, scale=rs_c),
             reads=(xkey, rsk), writes=(xsk,))
    for q in range(4):
        bname, bt = pe_banks_t.next()
        for j in range(4):
            kc = q * 4 + j
            prog.add("pe", lambda e, kc=kc, j=j, bt=bt: e.transpose(
                out=bt[:, j * P:(j + 1) * P], in_=xs_t[:, kc * P:(kc + 1) * P], identity=ident_bf[:]),
                reads=(xsk, "ident"), writes=(bname,))
        for j in range(4):
            kc = q * 4 + j
            prog.add("dve", lambda e, kc=kc, j=j, bt=bt: e.tensor_scalar(
                out=hT[:, kc, col0:col0 + P], in0=bt[:, j * P:(j + 1) * P],
                scalar1=g_col[:, kc:kc + 1], scalar2=None, op0=ALU.mult),
                reads=(bname, "gcol"), writes=(hT_key,))


def build_k1():
    nc = bass.Bass("TRN2", target_bir_lowering=False)
    d = {
        "x": nc.dram_tensor("x", [TOK, D], F32, kind="ExternalInput").ap(),
        "w": nc.dram_tensor("w_in", [D, WX_COLS], F32, kind="ExternalInput").ap(),
        "gcol": nc.dram_tensor("gcol", [P, KC], F32, kind="ExternalInput").ap(),
        "cos": nc.dram_tensor("cosT", [P, TOK], F32, kind="ExternalInput").ap(),
        "sin": nc.dram_tensor("sinT", [P, TOK], F32, kind="ExternalInput").ap(),
        "ident": nc.dram_tensor("ident", [P, P], F32, kind="ExternalInput").ap(),
        "aq": nc.dram_tensor("aq", [1024, TOK], BF16, kind="ExternalOutput").ap(),
        "ak": nc.dram_tensor("ak", [256, TOK], BF16, kind="ExternalOutput").ap(),
        "mqk": nc.dram_tensor("mqk", [1024, TOK], F32, kind="ExternalOutput").ap(),
        "av": nc.dram_tensor("av", [TOK, 256], BF16, kind="ExternalOutput").ap(),
        "mv": nc.dram_tensor("mv", [TOK, 1024], BF16, kind="ExternalOutput").ap(),
        "smo": nc.dram_tensor("smo", [TOK, 1024], F32, kind="ExternalOutput").ap(),
        "gt": nc.dram_tensor("gt", [TOK, 16], F32, kind="ExternalOutput").ap(),
    }
    pr = Prog(nc)
    emit_k1(pr, d, TOK, False)
    pr.emit()
    return nc


def emit_k1(pr, d, ntok, gt_tiled, extra_dmas=None):
    extra_dmas = list(extra_dmas or [])
    x, w, gcol_d, cos_d, sin_d, ident_d = d["x"], d["w"], d["gcol"], d["cos"], d["sin"], d["ident"]
    aq_o, ak_o, mqk_o, av_o, mv_o, smo_o, gt_o = d["aq"], d["ak"], d["mqk"], d["av"], d["mv"], d["smo"], d["gt"]
    ident_bf = pr.sb("ident_bf", [P, P], BF16)
    gcol = pr.sb("gcol_sb", [P, KC], F32)
    cst = pr.sb("cs_sb", [P, 2, 2, 512], F32)
    xt = pr.sb("xt", [P, 2, D], F32)
    scr = {
        "ss": pr.sb("ss", [P, 2], F32),
        "rstd": pr.sb("rstd", [P, 2], F32),
        "xs": pr.sb("xs", [P, 2, D], BF16),
    }
    hT = pr.sb("hT", [P, KC, 512], BF16)
    NW = 3
    wt = pr.sb("wt", [P, NW, KC, 512], BF16)
    ev32 = pr.sb("ev32", [P, 4, 512], F32)
    evbf = pr.sb("evbf", [P, 4, 512], BF16)
    t1 = pr.sb("t1", [P, 2, 512], F32)
    banks = Banks(pr, ["pb%d" % i for i in range(6)])
    tb_t = [("pt%d" % i, pr.ps("pt%d" % i, [P, 1024], BF16)) for i in range(2)]

    class TB:
        i = 0

        def next(self):
            t = tb_t[self.i % 2]
            self.i += 1
            return t
    tbanks = TB()

    pr.dma("pool", ident_bf[:], ident_d, writes=("ident",))
    pr.dma("sp", gcol[:], gcol_d, writes=("gcol",))

    w_r = w.rearrange("(kc p) c -> p kc c", p=P)
    wcount = [0]
    evc = [0]
    outs = []

    def load_w(c0, n):
        slot = wcount[0] % NW
        wcount[0] += 1
        key = ("wt", slot)
        pr.dma("pool", wt[:, slot, :, 0:n], w_r[:, :, c0:c0 + n], writes=(key,))
        if extra_dmas:
            dst, src, k = extra_dmas.pop(0)
            pr.dma("pool", dst, src, writes=(k,))
        return slot, key

    def ev_slot():
        s = evc[0] % 4
        evc[0] += 1
        return s

    for tb in range(ntok // 512):
        t0 = tb * 512
        csl = tb % 2
        pr.dma("sp", cst[:, csl, 0, :], cos_d[:, t0:t0 + 512], writes=(("cos", csl),))
        pr.dma("sp", cst[:, csl, 1, :], sin_d[:, t0:t0 + 512], writes=(("sin", csl),))
        for ti in range(4):
            g_ti = tb * 4 + ti
            xs_ = g_ti % 2
            xkey = ("xt", xs_)
            pr.dma("sp", xt[:, xs_, :], x[g_ti * P:(g_ti + 1) * P, :], writes=(xkey,))
            rmsnorm_to_featmajor(pr, tbanks, xt[:, xs_, :], xkey, gcol, hT, ("hT", ti), ti * P,
                                 ident_bf, scr, g_ti)
        hkeys = tuple(("hT", ti) for ti in range(4))

        for grp, (c0g, nh) in enumerate(((0, 4), (512, 4), (1024, 2))):
            slot, wkey = load_w(c0g, nh * P)
            for hl in range(nh):
                hh = grp * 4 + hl
                na, ba = banks.next()
                mm_group(pr, ba[:], na, [(wt[:, slot, kc, hl * P:(hl + 1) * P], hT[:, kc, :], hkeys + (wkey,))
                                         for kc in range(KC)])
                es = ev_slot()
                cs_ap = cst[:, csl, 0, :]
                sn_ap = cst[:, csl, 1, :]
                pr.op("dve", "tensor_tensor", reads=(na, ("cos", csl)), writes=("t1a",), out=t1[:, 0, :], in0=ba[:],
                      in1=cs_ap, op=ALU.mult)
                pr.op("dve", "tensor_tensor", reads=(na, ("sin", csl)), writes=("t1b",), out=t1[0:64, 1, :],
                      in0=ba[64:128, :], in1=sn_ap[0:64, :], op=ALU.mult)
                pr.op("dve", "tensor_tensor", reads=(na, ("sin", csl), "t1b"), writes=("t1b",), out=t1[64:128, 1, :],
                      in0=ba[0:64, :], in1=sn_ap[64:128, :], op=ALU.mult)
                pr.op("dve", "tensor_tensor", reads=("t1a", "t1b"), writes=(("evbf", es),), out=evbf[:, es, :],
                      in0=t1[:, 0, :], in1=t1[:, 1, :], op=ALU.add)
                dst = aq_o[hh * P:(hh + 1) * P, t0:t0 + 512] if hh < 8 else ak_o[(hh - 8) * P:(hh - 7) * P, t0:t0 + 512]
                outs.append(pr.dma("sp", dst, evbf[:, es, :], reads=(("evbf", es),), writes=(("out", len(outs)),)))

        for grp in range(2):
            slot, wkey = load_w(1280 + grp * 512, 512)
            for j in range(4):
                nb_, bt = banks.next()
                mm_group(pr, bt[:], nb_, [(wt[:, slot, kc, j * P:(j + 1) * P], hT[:, kc, :], hkeys + (wkey,))
                                          for kc in range(KC)])
                es = ev_slot()
                pr.add("act", lambda e, bt=bt, es=es: e.copy(out=ev32[:, es, :], in_=bt[:]),
                       reads=(nb_,), writes=(("ev32", es),))
                r0 = grp * 512 + j * P
                outs.append(pr.dma("sp", mqk_o[r0:r0 + P, t0:t0 + 512], ev32[:, es, :],
                                   reads=(("ev32", es),), writes=(("out", len(outs)),)))

        tm_groups = [
            (2304, 272, "avg"),
            (2576, 512, "mv0"), (3088, 512, "mv1"),
            (3600, 512, "mo0"), (4112, 512, "mo1"),
        ]
        for c0, ncols, kind in tm_groups:
            slot, wkey = load_w(c0, ncols)
            for ti in range(4):
                r0 = t0 + ti * P
                nb_, bt = banks.next()
                mm_group(pr, bt[:, 0:ncols], nb_,
                         [(hT[:, kc, ti * P:(ti + 1) * P], wt[:, slot, kc, 0:ncols], (("hT", ti), wkey))
                          for kc in range(KC)])
                es = ev_slot()
                if kind == "avg":
                    pr.add("act", lambda e, bt=bt, es=es: e.copy(out=evbf[:, es, 0:256], in_=bt[:, 0:256]),
                           reads=(nb_,), writes=(("evbf", es),))
                    pr.add("act", lambda e, bt=bt, es=es: e.copy(out=ev32[:, es, 0:16], in_=bt[:, 256:272]),
                           reads=(nb_,), writes=(("ev32", es),))
                    outs.append(pr.dma("sp", av_o[r0:r0 + P, :], evbf[:, es, 0:256],
                                       reads=(("evbf", es),), writes=(("out", len(outs)),)))
                    gt_dst = gt_o[:, r0 // P, :] if gt_tiled else gt_o[r0:r0 + P, :]
                    outs.append(pr.dma("sp", gt_dst, ev32[:, es, 0:16],
                                       reads=(("ev32", es),), writes=(("out", len(outs)),)))
                elif kind.startswith("mv"):
                    c = int(kind[2]) * 512
                    pr.add("act", lambda e, bt=bt, es=es: e.copy(out=evbf[:, es, :], in_=bt[:]),
                           reads=(nb_,), writes=(("evbf", es),))
                    outs.append(pr.dma("sp", mv_o[r0:r0 + P, c:c + 512], evbf[:, es, :],
                                       reads=(("evbf", es),), writes=(("out", len(outs)),)))
                else:
                    c = int(kind[2]) * 512
                    pr.add("act", lambda e, bt=bt, es=es: e.activation(out=ev32[:, es, :], in_=bt[:], func=AF.Sigmoid),
                           reads=(nb_,), writes=(("ev32", es),))
                    outs.append(pr.dma("sp", smo_o[r0:r0 + P, c:c + 512], ev32[:, es, :],
                                       reads=(("ev32", es),), writes=(("out", len(outs)),)))

    for dst, src, k in extra_dmas:
        pr.dma("pool", dst, src, writes=(k,))
    pr.flush()


def rstd_chain(pr, ss_c, ssk, rs_c, rsk, n):
    pr.add("dve", lambda e: e.tensor_scalar(out=rs_c, in0=ss_c, scalar1=1.0 / n, scalar2=EPS,
                                            op0=ALU.mult, op1=ALU.add), reads=(ssk,), writes=(rsk,))
    pr.add("act", lambda e: e.activation(out=rs_c, in_=rs_c, func=AF.Sqrt), reads=(rsk,), writes=(rsk,))
    pr.add("dve", lambda e: e.reciprocal(out=rs_c, in_=rs_c), reads=(rsk,), writes=(rsk,))


def build_k3():
    nc = bass.Bass("TRN2", target_bir_lowering=False)
    d = {
        "mixT": nc.dram_tensor("mixT", [D, TOK], BF16, kind="ExternalInput").ap(),
        "x": nc.dram_tensor("x", [TOK, D], F32, kind="ExternalInput").ap(),
        "w_out": nc.dram_tensor("w_out", [D, D], F32, kind="ExternalInput").ap(),
        "w_up": nc.dram_tensor("w_up", [D, DFF], F32, kind="ExternalInput").ap(),
        "w_down": nc.dram_tensor("w_down", [DFF, D], F32, kind="ExternalInput").ap(),
        "g_pm": nc.dram_tensor("g_pm", [P, D], F32, kind="ExternalInput").ap(),
        "g_pl": nc.dram_tensor("g_pl", [P, D], F32, kind="ExternalInput").ap(),
        "gcol": nc.dram_tensor("gcol", [P, KC], F32, kind="ExternalInput").ap(),
        "ident": nc.dram_tensor("ident", [P, P], F32, kind="ExternalInput").ap(),
        "y": nc.dram_tensor("y", [TOK, D], F32, kind="ExternalOutput").ap(),
    }
    mixT_r = d["mixT"].rearrange("(kc p) t -> p kc t", p=P)
    d["mix_blk"] = lambda tb: mixT_r[:, :, tb * 512:(tb + 1) * 512]
    pr = Prog(nc)
    emit_k3(pr, d, TOK)
    pr.emit()
    return nc


def emit_k3(pr, d, ntok):
    x, w_out, w_up, w_down = d["x"], d["w_out"], d["w_up"], d["w_down"]
    gat = d.get("gather")
    extra_dmas = list(d.get("extra_dmas") or [])
    extra_every = 4
    gpm_d, gpl_d, gcol_d, ident_d, y_o = d["g_pm"], d["g_pl"], d["gcol"], d["ident"], d["y"]
    ident_bf = pr.sb("ident_bf", [P, P], BF16)
    gcol = pr.sb("gcol_sb", [P, KC], F32)
    gpm = pr.sb("gpm_sb", [P, D], F32)
    gpl = pr.sb("gpl_sb", [P, D], F32)
    scr = {
        "ss": pr.sb("ss", [P, 2], F32),
        "rstd": pr.sb("rstd", [P, 2], F32),
        "xs": pr.sb("xs", [P, 2, D], BF16),
    }
    ss4 = pr.sb("ss4", [P, 4, 4], F32)
    ssr = pr.sb("ssr", [P, 4], F32)
    rs2 = pr.sb("rs2", [P, 4], F32)
    mixb = pr.sb("mixb", [P, KC, 512], BF16)
    h2T = pr.sb("h2T", [P, KC, 512], BF16)
    NW = 3
    wt = pr.sb("wt", [P, NW, KC, 512], BF16)
    x1b = pr.sb("x1b", [P, 4, D], F32)
    yb = pr.sb("yb", [P, 4, D], F32)
    uT = pr.sb("uT", [P, 32, 512], BF16)
    r32 = pr.sb("r32", [P, 2, 512], F32)
    pa = Banks(pr, ["pa0", "pa1"])
    pd = [("pd%d" % i, pr.ps("pd%d" % i, [P, 512], F32)) for i in range(4)]
    tb_t = [("pt%d" % i, pr.ps("pt%d" % i, [P, 1024], BF16)) for i in range(2)]

    class TB:
        i = 0

        def next(self):
            t = tb_t[self.i % 2]
            self.i += 1
            return t
    tbanks = TB()

    pr.dma("pool", ident_bf[:], ident_d, writes=("ident",))
    pr.dma("sp", gcol[:], gcol_d, writes=("gcol",))
    pr.dma("sp", gpm[:], gpm_d, writes=("gpm",))
    pr.dma("sp", gpl[:], gpl_d, writes=("gpl",))

    w_out_r = w_out.rearrange("(kc p) c -> p kc c", p=P)
    w_up_r = w_up.rearrange("(kc p) c -> p kc c", p=P)
    w_down_r = w_down.rearrange("(fc p) c -> p fc c", p=P)
    mixkeys = tuple(("mixb", kc) for kc in range(KC))
    if gat is not None:
        I32 = mybir.dt.int32
        xidx = pr.sb("xidx", [P, ntok // P], I32)
        midx = pr.sb("midx", [P, (ntok // 512) * KC], I32)
        pr.dma("sp", xidx[:], gat["xidx"], writes=("xidx",))
        pr.dma("sp", midx[:], gat["midx"], writes=("midx",))
    wcount = [0]
    rc = [0]
    outs = []

    def load_w(src):
        slot = wcount[0] % NW
        wcount[0] += 1
        key = ("wt", slot)
        pr.dma("pool", wt[:, slot, :, :], src, writes=(key,))
        if extra_dmas and wcount[0] % extra_every == 0:
            dst_, src_, k_ = extra_dmas.pop(0)
            pr.dma("pool", dst_, src_, writes=(k_,))
        return slot, key

    def r32_slot():
        s = rc[0] % 2
        rc[0] += 1
        return s

    def sumsq(src_ap, src_key, acc_ap, acc_key):
        rs = r32_slot()
        pr.add("act", lambda e: e.activation(out=r32[:, rs, :], in_=src_ap, func=AF.Square, accum_out=acc_ap),
               reads=(src_key,), writes=(("r32", rs), acc_key))

    def norm_residual(ti, g_sb, gkey, base_ap, base_key, dst_ap, dst_key):
        ss_c = ssr[:, ti:ti + 1]
        rs_c = rs2[:, ti:ti + 1]
        ybk = [("yb", ti, cb) for cb in range(4)]
        pr.add("dve", lambda e: e.reduce_sum(out=ss_c, in_=ss4[:, ti, :], axis=AX.X),
               reads=tuple(("ss4", ti, cb) for cb in range(4)), writes=(("ssr", ti),))
        rstd_chain(pr, ss_c, ("ssr", ti), rs_c, ("rs2", ti), D)
        pr.add("dve", lambda e: e.scalar_tensor_tensor(out=yb[:, ti, :], in0=yb[:, ti, :], scalar=rs_c, in1=g_sb[:],
                                                       op0=ALU.mult, op1=ALU.mult),
               reads=tuple(ybk) + (("rs2", ti), gkey), writes=tuple(ybk))
        pr.add("dve", lambda e: e.tensor_tensor(out=dst_ap, in0=base_ap, in1=yb[:, ti, :], op=ALU.add),
               reads=tuple(ybk) + (base_key,), writes=(dst_key,))

    for tb in range(ntok // 512):
        t0 = tb * 512
        if gat is None:
            pr.dma("sp", mixb[:], d["mix_blk"](tb), writes=mixkeys)
            for ti in range(4):
                pr.dma("sp", x1b[:, ti, :], x[t0 + ti * P:t0 + (ti + 1) * P, :], writes=(("x1b", ti),))
        else:
            for kc in range(KC):
                pr.add("pool", lambda e, kc=kc, col=tb * KC + kc: e.indirect_dma_start(
                    out=mixb[:, kc, :], out_offset=None, in_=gat["mix_flat"],
                    in_offset=bass.IndirectOffsetOnAxis(midx[:, col:col + 1], 0)),
                    reads=("midx",), writes=(("mixb", kc),), dma=True)
            for ti in range(4):
                pr.add("pool", lambda e, ti=ti, col=tb * 4 + ti: e.indirect_dma_start(
                    out=x1b[:, ti, :], out_offset=None, in_=x,
                    in_offset=bass.IndirectOffsetOnAxis(xidx[:, col:col + 1], 0)),
                    reads=("xidx",), writes=(("x1b", ti),), dma=True)
        for cb in range(4):
            slot, wkey = load_w(w_out_r[:, :, cb * 512:(cb + 1) * 512])
            for ti in range(4):
                nb_, bt = pa.next()
                mm_group(pr, bt[:], nb_, [(mixb[:, kc, ti * P:(ti + 1) * P], wt[:, slot, kc, :], (("mixb", kc), wkey))
                                          for kc in range(KC)])
                ypiece = yb[:, ti, cb * 512:(cb + 1) * 512]
                pr.add("act", lambda e, bt=bt, ypiece=ypiece: e.copy(out=ypiece, in_=bt[:]),
                       reads=(nb_,), writes=(("yb", ti, cb),))
                sumsq(ypiece, ("yb", ti, cb), ss4[:, ti, cb:cb + 1], ("ss4", ti, cb))
        for ti in range(4):
            norm_residual(ti, gpm, "gpm", x1b[:, ti, :], ("x1b", ti), x1b[:, ti, :], ("x1b", ti))
            rmsnorm_to_featmajor(pr, tbanks, x1b[:, ti, :], ("x1b", ti), gcol, h2T, ("h2T", ti), ti * P,
                                 ident_bf, scr, ti)
        hkeys = tuple(("h2T", ti) for ti in range(4))

        for hf in range(2):
            for fgl in range(8):
                fg = hf * 8 + fgl
                slot, wkey = load_w(w_up_r[:, :, fg * 512:(fg + 1) * 512])
                for j in range(4):
                    fcl = fgl * 4 + j
                    nb_, bt = pa.next()
                    mm_group(pr, bt[:], nb_, [(wt[:, slot, kc, j * P:(j + 1) * P], h2T[:, kc, :], hkeys + (wkey,))
                                              for kc in range(KC)])
                    rs = r32_slot()
                    pr.add("act", lambda e, bt=bt, rs=rs: e.activation(out=r32[:, rs, :], in_=bt[:], func=AF.Relu),
                           reads=(nb_,), writes=(("r32", rs),))
                    sq_eng = "dve"
                    pr.add(sq_eng, lambda e, rs=rs, fcl=fcl: e.tensor_tensor(out=uT[:, fcl, :], in0=r32[:, rs, :],
                                                                            in1=r32[:, rs, :], op=ALU.mult),
                           reads=(("r32", rs),), writes=(("uT", fcl),))
            for cb in range(4):
                for qi in range(2):
                    q = hf * 2 + qi
                    slot, wkey = load_w(w_down_r[:, q * 16:(q + 1) * 16, cb * 512:(cb + 1) * 512])
                    for ti in range(4):
                        pname, pt = pd[ti]
                        for f16 in range(16):
                            fcl = qi * 16 + f16
                            pr.add("pe", lambda e, pt=pt, fcl=fcl, f16=f16, ti=ti, slot=slot, qi=qi: e.matmul(
                                pt[:], uT[:, fcl, ti * P:(ti + 1) * P], wt[:, slot, f16, :],
                                start=(qi == 0 and f16 == 0), stop=(qi == 1 and f16 == 15)),
                                reads=(("uT", fcl), wkey), writes=(pname,))
                for ti in range(4):
                    pname, pt = pd[ti]
                    ypiece = yb[:, ti, cb * 512:(cb + 1) * 512]
                    if hf == 0:
                        pr.add("act", lambda e, pt=pt, ypiece=ypiece: e.copy(out=ypiece, in_=pt[:]),
                               reads=(pname,), writes=(("yb", ti, cb),))
                    else:
                        pr.add("dve", lambda e, pt=pt, ypiece=ypiece: e.tensor_tensor(out=ypiece, in0=ypiece, in1=pt[:],
                                                                                      op=ALU.add),
                               reads=(pname, ("yb", ti, cb)), writes=(("yb", ti, cb),))
                        sumsq(ypiece, ("yb", ti, cb), ss4[:, ti, cb:cb + 1], ("ss4", ti, cb))
        for ti in range(4):
            norm_residual(ti, gpl, "gpl", x1b[:, ti, :], ("x1b", ti), yb[:, ti, :], ("ybo", ti))
            outs.append(pr.dma("sp", y_o[t0 + ti * P:t0 + (ti + 1) * P, :], yb[:, ti, :],
                               reads=(("ybo", ti),) + tuple(("yb", ti, cb) for cb in range(4)),
                               writes=(("out", len(outs)),)))

    for dst_, src_, k_ in extra_dmas:
        pr.dma("pool", dst_, src_, writes=(k_,))
    pr.flush()


def run_k3(mixT_cores, x_flat, w_out_l, w_up_l, w_down_l, g_pm, g_pre_mlp, g_pl):
    nc = _get("k3", build_k3)
    gcol = np.ascontiguousarray(g_pre_mlp.reshape(KC, P).T)
    gpm = np.ascontiguousarray(np.broadcast_to(g_pm[None, :], (P, D)))
    gpl = np.ascontiguousarray(np.broadcast_to(g_pl[None, :], (P, D)))
    ident = np.eye(P, dtype=np.float32)
    in_maps = []
    for c in range(NCORE):
        in_maps.append({
            "mixT": mixT_cores[c],
            "x": np.ascontiguousarray(x_flat[c * TOK:(c + 1) * TOK]),
            "w_out": w_out_l, "w_up": w_up_l, "w_down": w_down_l,
            "g_pm": gpm, "g_pl": gpl, "gcol": gcol, "ident": ident,
        })
    res = run_bass_kernel_spmd(nc, in_maps, core_ids=list(range(NCORE)), **_RUNKW)
    _LAST["t"] = res.exec_time_ns
    return [r["y"] for r in res.results]


NCH = S // P
ATT_SCALE = float(128 ** -0.5)
QK_SCALE = float(128 ** -0.5)


def build_k2():
    nc = bass.Bass("TRN2", target_bir_lowering=False)

    def din(name, shape, dt):
        return nc.dram_tensor(name, list(shape), dt, kind="ExternalInput").ap()
    v = {
        "aq4": din("aq2", [P, NCH * 256], BF16).rearrange("p (n h t) -> p n h t", h=2, t=P),
        "ak": din("ak", [P, S], BF16),
        "av3": din("av3", [P, NCH * P], BF16).rearrange("p (n d) -> p n d", d=P),
        "mq": din("mq", [P, S], F32),
        "mk": din("mk", [P, S], F32),
        "mv3": din("mv3", [P, NCH * 256], BF16).rearrange("p (n e) -> p n e", e=256),
        "smo3": din("smo3", [P, NCH * 256], F32).rearrange("p (n e) -> p n e", e=256),
        "g4": din("g4", [P, 4 * NCH], F32).rearrange("p (g n) -> p g n", g=4),
        "gb4": din("gb4", [P, 4], F32),
        "cw": din("cw", [P, 6], F32),
        "gn": din("gn", [P, 256], F32),
        "sink2": din("sink2", [P, 256], F32),
        "identf": din("identf", [P, P], F32),
        "le": din("le", [P, P], F32),
        "ge": din("ge", [P, P], F32),
        "nmp": din("nmp", [P, 256], F32),
        "nmn": din("nmn", [P, 256], F32),
        "attT_r": nc.dram_tensor("attT", [256, S], BF16, kind="ExternalOutput").ap().rearrange("(h d) t -> d h t", h=2),
        "memT_r": nc.dram_tensor("memT", [256, S], BF16, kind="ExternalOutput").ap().rearrange("(h e) t -> e h t", h=2),
        "hb": nc.dram_tensor("hb_scratch", [NCH, P, 256], F32).ap(),
        "hfb": nc.dram_tensor("hfb_scratch", [2, NCH, P, 258], F32).ap(),
    }
    v["att_dst"] = lambda n: v["attT_r"][:, :, n * P:(n + 1) * P]
    v["mem_dst"] = lambda n: v["memT_r"][:, :, n * P:(n + 1) * P]
    pr = Prog(nc)
    emit_k2(pr, v, None)
    pr.emit()
    return nc


def emit_k2_v1(pr, v, gate_j):
    ak_d, mq_d, mk_d = v["ak"], v["mq"], v["mk"]
    gb4_d, cw_d, gn_d, sink_d = v["gb4"], v["cw"], v["gn"], v["sink2"]
    identf_d, le_d, ge_d, nmp_d, nmn_d = v["identf"], v["le"], v["ge"], v["nmp"], v["nmn"]
    hb_d = v["hb"]

    def sb(name, shape, dt):
        return pr.sb(name + "_s", shape, dt)
    nc = pr.nc
    identf = sb("identf", [P, P], F32)
    ident_bf = sb("ident_bf", [P, P], BF16)
    onesf = sb("onesf", [P, P], F32)
    ones_bf = sb("ones_bf", [P, P], BF16)
    le = sb("le", [P, P], F32)
    ge = sb("ge", [P, P], F32)
    nmp = sb("nmp", [P, 256], BF16)
    nmn = sb("nmn", [P, 256], BF16)
    gb4 = sb("gb4", [P, 4], F32)
    negb = sb("negb", [P, 4], F32)
    cw = sb("cw", [P, 6], F32)
    gn = sb("gn", [P, 256], F32)
    esink = sb("esink", [P, 256], F32)
    g4 = sb("g4", [P, 4, NCH], F32)
    li = sb("li", [P, 2, NCH], F32)
    lf = sb("lf", [P, 2, NCH], F32)
    cg = sb("cg", [P, 4, NCH], F32)
    ecol = sb("ecol", [P, 2, NCH], F32)
    wend = sb("wend", [P, 2, NCH], F32)
    eg = sb("eg", [P, 2, NCH], F32)
    gtmp = sb("gtmp", [P, 2, NCH], F32)
    ak = sb("ak", [P, S], BF16)
    av3 = sb("av3", [P, NCH, P], BF16)
    kT = sb("kT", [P, S], BF16)
    qs = [sb("qs_f", [P, S], BF16), sb("qs_b", [P, S], BF16)]
    kw = [sb("kw_f", [P, NCH, P], BF16), sb("kw_b", [P, NCH, P], BF16)]
    vext = sb("vext", [P, NCH, 258], BF16)
    PSZ = 1024
    NPC = S // PSZ
    xq = sb("xq", [P, PSZ + 2], F32)
    yq = sb("yq", [P, PSZ], F32)
    xk = sb("xk", [P, PSZ + 2], F32)
    yk = sb("yk", [P, PSZ], F32)
    lfbc = sb("lfbc", [P, 4, P], F32)
    eb = sb("eb", [P, 2, 512], F32)
    cst = [sb("C_f", [P, 257], F32), sb("C_b", [P, 257], F32)]
    cbf = [sb("Cbf_f", [P, 257], BF16), sb("Cbf_b", [P, 257], BF16)]
    ws = sb("ws", [P, 2, P], BF16)
    rr = sb("rr", [P, 4], F32)
    hbt = sb("hbt", [P, 2, 256], F32)
    hsum = sb("hsum", [P, 2, 256], F32)
    smo_t = sb("smo_t", [P, 2, 256], F32)
    obf = sb("obf", [P, 2, 256], BF16)
    junk = sb("junk", [P, 256], F32)
    ssn = sb("ssn", [P, 2], F32)
    rsn = sb("rsn", [P, 2], F32)
    mst = sb("mst", [P, 2, 256], BF16)
    qblk = sb("qblk", [P, 2, 256], BF16)
    pT_sb = sb("pT_sb", [P, 2, 3, 256], BF16)
    dent = sb("dent", [P, 2, 256], F32)
    ast = sb("ast", [P, 2, 256], BF16)

    pS = pr.ps("pS", [P, 512], F32)
    pH = [pr.ps("pH0", [P, 512], F32), pr.ps("pH1", [P, 512], F32)]
    pC = pr.ps("pC", [P, 512], F32)
    pT = pr.ps("pT", [P, 1024], BF16)
    pA = [pr.ps("pA0", [P, 512], F32), pr.ps("pA1", [P, 512], F32)]
    pO = pr.ps("pO", [P, 512], F32)

    pr.dma("sp", identf[:], identf_d, writes=("identf",))
    pr.dma("pool", ident_bf[:], identf_d, writes=("ident",))
    pr.dma("sp", le[:], le_d, writes=("le",))
    pr.dma("sp", ge[:], ge_d, writes=("ge",))
    pr.dma("pool", nmp[:], nmp_d, writes=("nmp",))
    pr.dma("pool", nmn[:], nmn_d, writes=("nmn",))
    pr.dma("sp", gb4[:], gb4_d, writes=("gb4",))
    pr.dma("sp", cw[:], cw_d, writes=("cw",))
    pr.dma("sp", gn[:], gn_d, writes=("gn",))
    pr.dma("sp", esink[:], sink_d, writes=("esink",))
    if gate_j is None:
        pr.dma("sp", g4[:], v["g4"], writes=("g4",))
    else:
        gall = sb("gall", [P, NCH, 16], F32)
        pr.dma("sp", gall[:], v["gall"], writes=("gall",))
        for g in range(4):
            pr.op("dve", "tensor_copy", reads=("gall",), writes=("g4",), out=g4[:, g, :],
                  in_=gall[:, :, g * 4 + gate_j])
    pr.dma("sp", ak[:], ak_d, writes=("ak",))
    pr.dma("sp", av3[:], v["av3"], writes=("av3",))
    pr.dma("sp", vext[:, :, 0:256], v["mv3"], writes=("vext",))
    pr.op("dve", "memset", writes=("vext1",), ap=vext[:, :, 256:257], constant=1.0)
    pr.op("dve", "memset", writes=("onesf",), ap=onesf[:], constant=1.0)
    pr.op("dve", "memset", writes=("ones_bf",), ap=ones_bf[:], constant=1.0)
    for di in range(2):
        pr.op("dve", "memset", writes=(("C", di),), ap=cst[di][:], constant=0.0)
        pr.op("dve", "memset", writes=(("Cbf", di),), ap=cbf[di][:], constant=0.0)
    pr.op("act", "activation", reads=("esink",), writes=("esink",), out=esink[:], in_=esink[:], func=AF.Exp)

    pr.op("dve", "tensor_scalar", reads=("gb4",), writes=("negb",), out=negb[:], in0=gb4[:], scalar1=-1.0,
          scalar2=None, op0=ALU.mult)
    for di in range(2):
        pr.op("dve", "tensor_scalar", reads=("g4", "gb4"), writes=(("li", di),), out=li[:, di, :],
              in0=g4[:, 2 * di, :], scalar1=gb4[:, 2 * di:2 * di + 1], scalar2=None, op0=ALU.add)
        pr.op("act", "activation", reads=("g4", "negb"), writes=(("gtmp", di),), out=gtmp[:, di, :],
              in_=g4[:, 2 * di + 1, :], func=AF.Exp, scale=-1.0, bias=negb[:, 2 * di + 1:2 * di + 2])
        pr.op("act", "activation", reads=(("gtmp", di),), writes=(("gtmp", di),), out=gtmp[:, di, :],
              in_=gtmp[:, di, :], func=AF.Ln, bias=1.0)
        pr.op("dve", "tensor_scalar", reads=(("gtmp", di),), writes=(("lf", di),), out=lf[:, di, :],
              in0=gtmp[:, di, :], scalar1=-1.0, scalar2=None, op0=ALU.mult)
    for di in range(2):
        tri = le if di == 0 else ge
        trik = "le" if di == 0 else "ge"
        pr.op("pe", "matmul", reads=(("lf", di), trik), writes=("pA0",), out=pA[0][:, di * 128:di * 128 + 64],
              lhsT=tri[:], rhs=lf[:, di, :], start=True, stop=True)
        pr.op("pe", "matmul", reads=(("lf", di), "onesf"), writes=("pA0",), out=pA[0][:, di * 128 + 64:di * 128 + 128],
              lhsT=onesf[:], rhs=lf[:, di, :], start=True, stop=True)
    pr.op("act", "copy", reads=("pA0",), writes=("cg",), out=cg[:].rearrange("p g n -> p (g n)"), in_=pA[0][:, 0:256])
    for di in range(2):
        bc = cg[:, 2 * di, :]
        gt_ = cg[:, 2 * di + 1, :]
        pr.op("dve", "tensor_tensor", reads=(("li", di), "cg"), writes=(("gtmp", di),), out=gtmp[:, di, :],
              in0=li[:, di, :], in1=bc, op=ALU.subtract)
        pr.op("act", "activation", reads=(("gtmp", di),), writes=(("ecol", di),), out=ecol[:, di, :],
              in_=gtmp[:, di, :], func=AF.Exp)
        pr.op("dve", "tensor_tensor", reads=(("gtmp", di), "cg"), writes=(("gtmp", di),), out=gtmp[:, di, :],
              in0=gtmp[:, di, :], in1=gt_, op=ALU.add)
        pr.op("act", "activation", reads=(("gtmp", di),), writes=(("wend", di),), out=wend[:, di, :],
              in_=gtmp[:, di, :], func=AF.Exp)
        pr.op("act", "activation", reads=("cg",), writes=(("eg", di),), out=eg[:, di, :], in_=gt_, func=AF.Exp)

    def conv_piece(src_d, xb, xkey, yb_, ykey, w0, pc):
        p0 = pc * PSZ
        lo = p0 - 1 if pc > 0 else 0
        hi = p0 + PSZ + 1 if pc < NPC - 1 else S
        c_lo = 0 if pc > 0 else 1
        if pc == 0:
            pr.op("dve", "memset", writes=(xkey,), ap=xb[:, 0:1], constant=0.0)
        if pc == NPC - 1:
            pr.op("dve", "memset", writes=(xkey,), ap=xb[:, PSZ + 1:PSZ + 2], constant=0.0)
        pr.dma("sp", xb[:, c_lo:c_lo + (hi - lo)], src_d[:, lo:hi], writes=(xkey,))
        pr.op("dve", "tensor_scalar", reads=(xkey, "cw"), writes=(ykey,), out=yb_[:], in0=xb[:, 1:PSZ + 1],
              scalar1=cw[:, w0 + 1:w0 + 2], scalar2=None, op0=ALU.mult)
        pr.op("dve", "scalar_tensor_tensor", reads=(xkey, "cw", ykey), writes=(ykey,), out=yb_[:], in0=xb[:, 0:PSZ],
              scalar=cw[:, w0:w0 + 1], in1=yb_[:], op0=ALU.mult, op1=ALU.add)
        pr.op("dve", "scalar_tensor_tensor", reads=(xkey, "cw", ykey), writes=(ykey,), out=yb_[:], in0=xb[:, 2:PSZ + 2],
              scalar=cw[:, w0 + 2:w0 + 3], in1=yb_[:], op0=ALU.mult, op1=ALU.add)
        pr.op("act", "activation", reads=(ykey,), writes=(ykey,), out=yb_[:], in_=yb_[:], func=AF.Silu)

    ebc = [0]
    for pc in range(NPC):
        p0 = pc * PSZ
        conv_piece(mq_d, xq, "xq", yq, "yq", 0, pc)
        conv_piece(mk_d, xk, "xk", yk, "yk", 3, pc)
        pr.op("act", "copy", reads=("yk",), writes=("kT",), out=kT[:, p0:p0 + PSZ], in_=yk[:])
        for sp_ in range(PSZ // 512):
            for di in range(2):
                tri = le if di == 0 else ge
                trik = "le" if di == 0 else "ge"
                bank = pA[ebc[0] % 2]
                bkey = "pA%d" % (ebc[0] % 2)
                es = ebc[0] % 2
                ebc[0] += 1
                for c4 in range(4):
                    n = pc * (PSZ // P) + sp_ * 4 + c4
                    pr.op("dve", "tensor_scalar", reads=("onesf", ("lf", di)), writes=(("lfbc", c4),), out=lfbc[:, c4, :],
                          in0=onesf[:], scalar1=lf[:, di, n:n + 1], scalar2=None, op0=ALU.mult)
                    pr.op("pe", "matmul", reads=(("lfbc", c4), trik), writes=(bkey,), out=bank[:, c4 * P:(c4 + 1) * P],
                          lhsT=lfbc[:, c4, :], rhs=tri[:], start=True, stop=True)
                pr.op("act", "activation", reads=(bkey,), writes=(("eb", es),), out=eb[:, es, :], in_=bank[:], func=AF.Exp)
                t0 = p0 + sp_ * 512
                pr.op("dve", "scalar_tensor_tensor", reads=("yq", ("eb", es)), writes=(("qs", di),),
                      out=qs[di][:, t0:t0 + 512], in0=yq[:, sp_ * 512:(sp_ + 1) * 512], scalar=QK_SCALE, in1=eb[:, es, :],
                      op0=ALU.mult, op1=ALU.mult)
        for c16 in range(PSZ // P):
            n = pc * (PSZ // P) + c16
            pr.op("pe", "transpose", reads=("yk", "identf"), writes=("pO",), out=pO[:, 0:P],
                  in_=yk[:, c16 * P:(c16 + 1) * P], identity=identf[:])
            for di in range(2):
                pr.op("act", "activation", reads=("pO", ("wend", di)), writes=(("kw", di),), out=kw[di][:, n, :],
                      in_=pO[:, 0:P], func=AF.Copy, scale=wend[:, di, n:n + 1])

    outs = []

    def mlstm_common(di, n):
        tok = slice(n * P, (n + 1) * P)
        msk = le if di == 0 else ge
        mskk = "le" if di == 0 else "ge"
        bank = pH[n % 2]
        bkey = "pH%d" % (n % 2)
        pr.op("pe", "matmul", reads=("kT", ("qs", di)), writes=("pS",), out=pS[:, 0:P], lhsT=kT[:, tok],
              rhs=qs[di][:, tok], start=True, stop=True)
        pr.op("dve", "scalar_tensor_tensor", reads=("pS", ("ecol", di), mskk), writes=(("ws", di),), out=ws[:, di, :],
              in0=pS[:, 0:P], scalar=ecol[:, di, n:n + 1], in1=msk[:], op0=ALU.mult, op1=ALU.mult)
        pr.op("pe", "matmul", reads=(("ws", di), "vext", "vext1"), writes=(bkey,), out=bank[:, 0:257], lhsT=ws[:, di, :],
              rhs=vext[:, n, 0:257], start=True, stop=False)
        pr.op("pe", "matmul", reads=(("qs", di), ("Cbf", di)), writes=(bkey,), out=bank[:, 0:257], lhsT=qs[di][:, tok],
              rhs=cbf[di][:], start=False, stop=True)
        pr.op("pe", "matmul", reads=(("kw", di), "vext", "vext1"), writes=("pC",), out=pC[:, 0:257], lhsT=kw[di][:, n, :],
              rhs=vext[:, n, 0:257], start=True, stop=True)
        pr.op("dve", "scalar_tensor_tensor", reads=("pC", ("eg", di), ("C", di)), writes=(("C", di),), out=cst[di][:],
              in0=cst[di][:], scalar=eg[:, di, n:n + 1], in1=pC[:, 0:257], op0=ALU.mult, op1=ALU.add)
        pr.op("act", "copy", reads=(("C", di),), writes=(("Cbf", di),), out=cbf[di][:], in_=cst[di][:])
        rc = rr[:, di:di + 1]
        pr.op("act", "activation", reads=(bkey,), writes=(("rr", di),), out=rc, in_=bank[:, 256:257], func=AF.Abs)
        pr.op("dve", "tensor_scalar", reads=(("rr", di),), writes=(("rr", di),), out=rc, in0=rc, scalar1=1.0,
              scalar2=None, op0=ALU.max)
        pr.op("dve", "reciprocal", reads=(("rr", di),), writes=(("rr", di),), out=rc, in_=rc)
        return bank, bkey, rc

    for n in range(NCH - 1, -1, -1):
        bank, bkey, rc = mlstm_common(1, n)
        s2 = n % 2
        pr.op("dve", "tensor_scalar", reads=(bkey, ("rr", 1)), writes=(("hbt", s2),), out=hbt[:, s2, :],
              in0=bank[:, 0:256], scalar1=rc, scalar2=None, op0=ALU.mult)
        pr.dma("sp", hb_d[n], hbt[:, s2, :], reads=(("hbt", s2),), writes=(("hbd", n),))

    aq4 = v["aq4"]
    smo3_r = v["smo3"]
    attT_r = v["attT_r"]
    memT_r = v["memT_r"]
    for n in range(NCH):
        s2 = n % 2
        tok = slice(n * P, (n + 1) * P)
        pr.dma("sp", qblk[:, s2, :].rearrange("p (h t) -> p h t", h=2), aq4[:, n, :, :], writes=(("qblk", s2),))
        kbs = [kb for kb in (n - 1, n, n + 1) if 0 <= kb < NCH]
        for i, kb in enumerate(kbs):
            bank = pA[i // 2]
            bkey = "pA%d" % (i // 2)
            reg = bank[:, (i % 2) * 256:(i % 2) * 256 + 256]
            masked = kb != n
            pr.op("pe", "matmul", reads=("ak", ("qblk", s2)), writes=(bkey,), out=reg, lhsT=ak[:, kb * P:(kb + 1) * P],
                  rhs=qblk[:, s2, :], start=True, stop=not masked)
            if masked:
                nm = nmp if kb < n else nmn
                nmk = "nmp" if kb < n else "nmn"
                pr.op("pe", "matmul", reads=("ident", nmk), writes=(bkey,), out=reg, lhsT=ident_bf[:], rhs=nm[:],
                      start=False, stop=True)
            pr.op("act", "activation", reads=(bkey,), writes=(("pT_sb", s2, i),), out=pT_sb[:, s2, i, :], in_=reg,
                  func=AF.Exp, scale=ATT_SCALE)
        for i, kb in enumerate(kbs):
            pr.op("pe", "matmul", reads=("av3", ("pT_sb", s2, i)), writes=("pO",), out=pO[:, 0:256], lhsT=av3[:, kb, :],
                  rhs=pT_sb[:, s2, i, :], start=(i == 0), stop=(i == len(kbs) - 1))
        for i, kb in enumerate(kbs):
            pr.op("pe", "matmul", reads=("ones_bf", ("pT_sb", s2, i)), writes=("pO",), out=pO[:, 256:512], lhsT=ones_bf[:],
                  rhs=pT_sb[:, s2, i, :], start=(i == 0), stop=(i == len(kbs) - 1))
        pr.op("dve", "tensor_tensor", reads=("pO", "esink"), writes=(("dent", s2),), out=dent[:, s2, :], in0=pO[:, 256:512],
              in1=esink[:], op=ALU.add)
        pr.op("dve", "reciprocal", reads=(("dent", s2),), writes=(("dent", s2),), out=dent[:, s2, :], in_=dent[:, s2, :])
        pr.op("dve", "tensor_tensor", reads=("pO", ("dent", s2)), writes=(("ast", s2),), out=ast[:, s2, :], in0=pO[:, 0:256],
              in1=dent[:, s2, :], op=ALU.mult)
        outs.append(pr.dma("sp", attT_r[:, :, tok], ast[:, s2, :].rearrange("d (h t) -> d h t", h=2),
                           reads=(("ast", s2),), writes=(("out", len(outs)),)))

        pr.dma("sp", hbt[:, s2, :], hb_d[n], reads=(("hbd", n),), writes=(("hbt", s2),))
        pr.dma("sp", smo_t[:, s2, :], smo3_r[:, n, :], writes=(("smo", s2),))
        bank, bkey, rc = mlstm_common(0, n)
        pr.op("dve", "scalar_tensor_tensor", reads=(bkey, ("rr", 0), ("hbt", s2)), writes=(("hsum", s2),),
              out=hsum[:, s2, :], in0=bank[:, 0:256], scalar=rc, in1=hbt[:, s2, :], op0=ALU.mult, op1=ALU.add)
        ss_c = ssn[:, s2:s2 + 1]
        rs_c = rsn[:, s2:s2 + 1]
        pr.op("act", "activation", reads=(("hsum", s2),), writes=("junk", ("ssn", s2)), out=junk[:], in_=hsum[:, s2, :],
              func=AF.Square, accum_out=ss_c)
        rstd_chain(pr, ss_c, ("ssn", s2), rs_c, ("rsn", s2), 256)
        pr.op("dve", "tensor_tensor", reads=(("smo", s2), "gn"), writes=(("smo", s2),), out=smo_t[:, s2, :],
              in0=smo_t[:, s2, :], in1=gn[:], op=ALU.mult)
        pr.op("dve", "scalar_tensor_tensor", reads=(("hsum", s2), ("rsn", s2), ("smo", s2)), writes=(("obf", s2),),
              out=obf[:, s2, :], in0=hsum[:, s2, :], scalar=rs_c, in1=smo_t[:, s2, :], op0=ALU.mult, op1=ALU.mult)
        for h2 in range(2):
            pr.op("pe", "transpose", reads=(("obf", s2), "ident"), writes=("pT",), out=pT[:, h2 * P:(h2 + 1) * P],
                  in_=obf[:, s2, h2 * P:(h2 + 1) * P], identity=ident_bf[:])
        pr.op("act", "copy", reads=("pT",), writes=(("mst", s2),), out=mst[:, s2, :], in_=pT[:, 0:256])
        outs.append(pr.dma("sp", memT_r[:, :, tok], mst[:, s2, :].rearrange("e (h t) -> e h t", h=2),
                           reads=(("mst", s2),), writes=(("out", len(outs)),)))

    pr.flush()


def emit_k2(pr, v, gate_j):
    for dst, src, k in v.get("extra_dmas", []):
        pr.dma("pool", dst, src, writes=(k,))
    ak_d, mq_d, mk_d = v["ak"], v["mq"], v["mk"]
    gb4_d, cw_d, gn_d, sink_d = v["gb4"], v["cw"], v["gn"], v["sink2"]
    identf_d, le_d, ge_d, nmp_d, nmn_d = v["identf"], v["le"], v["ge"], v["nmp"], v["nmn"]

    def sb(name, shape, dt):
        return pr.sb(name + "_s", shape, dt)
    nc = pr.nc
    identf = sb("identf", [P, P], F32)
    ident_bf = sb("ident_bf", [P, P], BF16)
    onesf = sb("onesf", [P, P], F32)
    ones_bf = sb("ones_bf", [P, P], BF16)
    le = sb("le", [P, P], F32)
    ge = sb("ge", [P, P], F32)
    nmp = sb("nmp", [P, 256], BF16)
    nmn = sb("nmn", [P, 256], BF16)
    gb4 = sb("gb4", [P, 4], F32)
    negb = sb("negb", [P, 4], F32)
    cw = sb("cw", [P, 6], F32)
    gn = sb("gn", [P, 256], F32)
    esink = sb("esink", [P, 256], F32)
    g4 = sb("g4", [P, 4, NCH], F32)
    li = sb("li", [P, 2, NCH], F32)
    lf = sb("lf", [P, 2, NCH], F32)
    cg = sb("cg", [P, 4, NCH], F32)
    ecol = sb("ecol", [P, 2, NCH], F32)
    wend = sb("wend", [P, 2, NCH], F32)
    eg = sb("eg", [P, 2, NCH], F32)
    gtmp = sb("gtmp", [P, 2, NCH], F32)
    ak = sb("ak", [P, S], BF16)
    av3 = sb("av3", [P, NCH, P], BF16)
    kT = sb("kT", [P, S], BF16)
    qs = [sb("qs_f", [P, S], BF16), sb("qs_b", [P, S], BF16)]
    kw = [sb("kw_f", [P, NCH, P], BF16), sb("kw_b", [P, NCH, P], BF16)]
    vext = sb("vext", [P, NCH, 258], BF16)
    PSZ = 512
    NPC = S // PSZ
    xq = sb("xq", [P, PSZ + 2], F32)
    yq = sb("yq", [P, PSZ], F32)
    xk = sb("xk", [P, PSZ + 2], F32)
    yk = sb("yk", [P, PSZ], F32)
    lfbc = sb("lfbc", [P, 2, 4, P], F32)
    eb = sb("eb", [P, 2, 512], F32)
    cst = [sb("C_f", [P, 257], F32), sb("C_b", [P, 257], F32)]
    cbf = [sb("Cbf_f", [P, 257], BF16), sb("Cbf_b", [P, 257], BF16)]
    ws = sb("ws", [P, 2, P], BF16)
    rr = sb("rr", [P, 4], F32)
    hbt = sb("hbt", [P, 2, 2, 258], F32)
    NCS = 3
    hcm = sb("hcm", [P, NCS, 2, 258], F32)
    rr2 = sb("rr2", [P, NCS, 2], F32)
    hsum = sb("hsum", [P, NCS, 256], F32)
    smo_t = sb("smo_t", [P, NCS, 256], F32)
    obf = sb("obf", [P, NCS, 256], BF16)
    junk = sb("junk", [P, NCS, 256], F32)
    ssn = sb("ssn", [P, NCS], F32)
    rsn = sb("rsn", [P, NCS], F32)
    mst = sb("mst", [P, NCS, 256], BF16)
    qblk = sb("qblk", [P, 2, 256], BF16)
    pT_sb = sb("pT_sb", [P, 2, 3, 256], BF16)
    dent = sb("dent", [P, 2, 256], F32)
    ast = sb("ast", [P, 2, 256], BF16)

    pX = [pr.ps("pXf", [P, 512], F32), pr.ps("pXb", [P, 512], F32)]
    pH = [pr.ps("pHf", [P, 512], F32), pr.ps("pHb", [P, 512], F32)]
    pT = pr.ps("pT", [P, 1024], BF16)
    hfb_d = v["hfb"]
    pA = [pr.ps("pA0", [P, 512], F32), pr.ps("pA1", [P, 512], F32)]
    pO = pr.ps("pO", [P, 512], F32)

    pr.dma("sp", identf[:], identf_d, writes=("identf",))
    pr.dma("pool", ident_bf[:], identf_d, writes=("ident",))
    pr.dma("sp", le[:], le_d, writes=("le",))
    pr.dma("sp", ge[:], ge_d, writes=("ge",))
    pr.dma("pool", nmp[:], nmp_d, writes=("nmp",))
    pr.dma("pool", nmn[:], nmn_d, writes=("nmn",))
    pr.dma("sp", gb4[:], gb4_d, writes=("gb4",))
    pr.dma("sp", cw[:], cw_d, writes=("cw",))
    pr.dma("sp", gn[:], gn_d, writes=("gn",))
    pr.dma("sp", esink[:], sink_d, writes=("esink",))
    if gate_j is None:
        pr.dma("sp", g4[:], v["g4"], writes=("g4",))
    else:
        gall = sb("gall", [P, NCH, 16], F32)
        pr.dma("sp", gall[:], v["gall"], writes=("gall",))
        for g in range(4):
            pr.op("dve", "tensor_copy", reads=("gall",), writes=("g4",), out=g4[:, g, :],
                  in_=gall[:, :, g * 4 + gate_j])
    pr.dma("sp", ak[:], ak_d, writes=("ak",))
    pr.dma("sp", av3[:], v["av3"], writes=("av3",))
    pr.dma("sp", vext[:, :, 0:256], v["mv3"], writes=("vext",))
    pr.op("dve", "memset", writes=("vext1",), ap=vext[:, :, 256:257], constant=1.0)
    pr.op("dve", "memset", writes=("onesf",), ap=onesf[:], constant=1.0)
    pr.op("dve", "memset", writes=("ones_bf",), ap=ones_bf[:], constant=1.0)
    for di in range(2):
        pr.op("dve", "memset", writes=(("C", di),), ap=cst[di][:], constant=0.0)
        pr.op("dve", "memset", writes=(("Cbf", di),), ap=cbf[di][:], constant=0.0)
    pr.op("act", "activation", reads=("esink",), writes=("esink",), out=esink[:], in_=esink[:], func=AF.Exp)

    def conv_piece(src_d, xb, xkey, yb_, ykey, w0, pc):
        p0 = pc * PSZ
        lo = p0 - 1 if pc > 0 else 0
        hi = p0 + PSZ + 1 if pc < NPC - 1 else S
        c_lo = 0 if pc > 0 else 1
        if pc == 0:
            pr.op("dve", "memset", writes=(xkey,), ap=xb[:, 0:1], constant=0.0)
        if pc == NPC - 1:
            pr.op("dve", "memset", writes=(xkey,), ap=xb[:, PSZ + 1:PSZ + 2], constant=0.0)
        pr.dma("sp", xb[:, c_lo:c_lo + (hi - lo)], src_d[:, lo:hi], writes=(xkey,))
        pr.op("dve", "tensor_scalar", reads=(xkey, "cw"), writes=(ykey,), out=yb_[:], in0=xb[:, 1:PSZ + 1],
              scalar1=cw[:, w0 + 1:w0 + 2], scalar2=None, op0=ALU.mult)
        pr.op("dve", "scalar_tensor_tensor", reads=(xkey, "cw", ykey), writes=(ykey,), out=yb_[:], in0=xb[:, 0:PSZ],
              scalar=cw[:, w0:w0 + 1], in1=yb_[:], op0=ALU.mult, op1=ALU.add)
        pr.op("dve", "scalar_tensor_tensor", reads=(xkey, "cw", ykey), writes=(ykey,), out=yb_[:], in0=xb[:, 2:PSZ + 2],
              scalar=cw[:, w0 + 2:w0 + 3], in1=yb_[:], op0=ALU.mult, op1=ALU.add)
        pr.op("act", "activation", reads=(ykey,), writes=(ykey,), out=yb_[:], in_=yb_[:], func=AF.Silu)


    def preproc():
        pr.op("dve", "tensor_scalar", reads=("gb4",), writes=("negb",), out=negb[:], in0=gb4[:], scalar1=-1.0,
              scalar2=None, op0=ALU.mult)
        for di in range(2):
            pr.op("dve", "tensor_scalar", reads=("g4", "gb4"), writes=(("li", di),), out=li[:, di, :],
                  in0=g4[:, 2 * di, :], scalar1=gb4[:, 2 * di:2 * di + 1], scalar2=None, op0=ALU.add)
            pr.op("act", "activation", reads=("g4", "negb"), writes=(("gtmp", di),), out=gtmp[:, di, :],
                  in_=g4[:, 2 * di + 1, :], func=AF.Exp, scale=-1.0, bias=negb[:, 2 * di + 1:2 * di + 2])
            pr.op("act", "activation", reads=(("gtmp", di),), writes=(("gtmp", di),), out=gtmp[:, di, :],
                  in_=gtmp[:, di, :], func=AF.Ln, bias=1.0)
            pr.op("dve", "tensor_scalar", reads=(("gtmp", di),), writes=(("lf", di),), out=lf[:, di, :],
                  in0=gtmp[:, di, :], scalar1=-1.0, scalar2=None, op0=ALU.mult)
        for di in range(2):
            tri = le if di == 0 else ge
            trik = "le" if di == 0 else "ge"
            pr.op("pe", "matmul", reads=(("lf", di), trik), writes=(("pH", 0),), out=pH[0][:, di * 128:di * 128 + 64],
                  lhsT=tri[:], rhs=lf[:, di, :], start=True, stop=True)
            pr.op("pe", "matmul", reads=(("lf", di), "onesf"), writes=(("pH", 0),), out=pH[0][:, di * 128 + 64:di * 128 + 128],
                  lhsT=onesf[:], rhs=lf[:, di, :], start=True, stop=True)
        pr.op("act", "copy", reads=(("pH", 0),), writes=("cg",), out=cg[:].rearrange("p g n -> p (g n)"), in_=pH[0][:, 0:256])
        for di in range(2):
            bc = cg[:, 2 * di, :]
            gt_ = cg[:, 2 * di + 1, :]
            pr.op("dve", "tensor_tensor", reads=(("li", di), "cg"), writes=(("gtmp", di),), out=gtmp[:, di, :],
                  in0=li[:, di, :], in1=bc, op=ALU.subtract)
            pr.op("act", "activation", reads=(("gtmp", di),), writes=(("ecol", di),), out=ecol[:, di, :],
                  in_=gtmp[:, di, :], func=AF.Exp)
            pr.op("dve", "tensor_tensor", reads=(("gtmp", di), "cg"), writes=(("gtmp", di),), out=gtmp[:, di, :],
                  in0=gtmp[:, di, :], in1=gt_, op=ALU.add)
            pr.op("act", "activation", reads=(("gtmp", di),), writes=(("wend", di),), out=wend[:, di, :],
                  in_=gtmp[:, di, :], func=AF.Exp)
            pr.op("act", "activation", reads=("cg",), writes=(("eg", di),), out=eg[:, di, :], in_=gt_, func=AF.Exp)

        yield

    def q_stream():
        ebc = [0]
        for pc in range(NPC):
            p0 = pc * PSZ
            conv_piece(mq_d, xq, "xq", yq, "yq", 0, pc)
            yield
            for sp_ in range(PSZ // 512):
                for di in range(2):
                    tri = le if di == 0 else ge
                    trik = "le" if di == 0 else "ge"
                    bank = pX[ebc[0] % 2]
                    bkeys = (("pX", ebc[0] % 2),)
                    es = ebc[0] % 2
                    ebc[0] += 1
                    for c4 in range(4):
                        n = pc * (PSZ // P) + sp_ * 4 + c4
                        pr.op("dve", "tensor_scalar", reads=("onesf", ("lf", di)), writes=(("lfbc", di, c4),),
                              out=lfbc[:, di, c4, :], in0=onesf[:], scalar1=lf[:, di, n:n + 1], scalar2=None, op0=ALU.mult)
                        pr.op("pe", "matmul", reads=(("lfbc", di, c4), trik), writes=bkeys, out=bank[:, c4 * P:(c4 + 1) * P],
                              lhsT=lfbc[:, di, c4, :], rhs=tri[:], start=True, stop=True)
                    yield
                    pr.op("act", "activation", reads=bkeys, writes=(("eb", es),), out=eb[:, es, :], in_=bank[:], func=AF.Exp)
                    yield
                    t0 = p0 + sp_ * 512
                    pr.op("dve", "scalar_tensor_tensor", reads=("yq", ("eb", es)), writes=(("qs", di),),
                          out=qs[di][:, t0:t0 + 512], in0=yq[:, sp_ * 512:(sp_ + 1) * 512], scalar=QK_SCALE, in1=eb[:, es, :],
                          op0=ALU.mult, op1=ALU.mult)
                    yield

    def k_stream():
        for pc in range(NPC):
            p0 = pc * PSZ
            conv_piece(mk_d, xk, "xk", yk, "yk", 3, pc)
            yield
            pr.op("act", "copy", reads=("yk",), writes=("kT",), out=kT[:, p0:p0 + PSZ], in_=yk[:])
            for c16 in range(PSZ // P):
                n = pc * (PSZ // P) + c16
                pr.op("pe", "transpose", reads=("yk", "identf"), writes=(("pH", 1),), out=pH[1][:, 0:P],
                      in_=yk[:, c16 * P:(c16 + 1) * P], identity=identf[:])
                yield
                for di in range(2):
                    pr.op("act", "activation", reads=(("pH", 1), ("wend", di)), writes=(("kw", di),), out=kw[di][:, n, :],
                          in_=pH[1][:, 0:P], func=AF.Copy, scale=wend[:, di, n:n + 1])
                yield

    outs = []
    smo3_r = v["smo3"]
    aq4 = v["aq4"]
    att_dst = v["att_dst"]
    mem_dst = v["mem_dst"]

    progress = [0, 0]
    LAG = 3

    def scan_stream(di):
        order = range(NCH) if di == 0 else range(NCH - 1, -1, -1)
        msk = le if di == 0 else ge
        mskk = "le" if di == 0 else "ge"
        X = pX[di]
        H = pH[di]
        for n in order:
            tok = slice(n * P, (n + 1) * P)
            s2 = n % 2
            pr.op("pe", "matmul", reads=("kT", ("qs", di)), writes=(("pX", di),), out=X[:, 0:P], lhsT=kT[:, tok],
                  rhs=qs[di][:, tok], start=True, stop=True)
            yield
            pr.op("dve", "scalar_tensor_tensor", reads=(("pX", di), ("ecol", di), mskk), writes=(("ws", di),),
                  out=ws[:, di, :], in0=X[:, 0:P], scalar=ecol[:, di, n:n + 1], in1=msk[:], op0=ALU.mult, op1=ALU.mult)
            yield
            pr.op("pe", "matmul", reads=(("ws", di), "vext", "vext1"), writes=(("pH", di),), out=H[:, 0:257],
                  lhsT=ws[:, di, :], rhs=vext[:, n, 0:257], start=True, stop=False)
            pr.op("pe", "matmul", reads=(("qs", di), ("Cbf", di)), writes=(("pH", di),), out=H[:, 0:257],
                  lhsT=qs[di][:, tok], rhs=cbf[di][:], start=False, stop=True)
            pr.op("pe", "matmul", reads=(("kw", di), "vext", "vext1"), writes=(("pX", di),), out=X[:, 128:385],
                  lhsT=kw[di][:, n, :], rhs=vext[:, n, 0:257], start=True, stop=True)
            yield
            pr.op("dve", "scalar_tensor_tensor", reads=(("pX", di), ("eg", di), ("C", di)), writes=(("C", di),),
                  out=cst[di][:], in0=cst[di][:], scalar=eg[:, di, n:n + 1], in1=X[:, 128:385], op0=ALU.mult, op1=ALU.add)
            yield
            pr.op("pool", "tensor_copy", reads=(("C", di),), writes=(("Cbf", di),), out=cbf[di][:], in_=cst[di][:])
            yield
            pr.op("act", "copy", reads=(("pH", di),), writes=(("hbt", di, s2),), out=hbt[:, di, s2, 0:257], in_=H[:, 0:257])
            pr.dma("pool", hfb_d[di, n, :, 0:257], hbt[:, di, s2, 0:257], reads=(("hbt", di, s2),), writes=(("hfd", di, n),))
            progress[di] += 1
            yield

    def attention_stream():
        for n in range(NCH):
            s2 = n % 2
            tok = slice(n * P, (n + 1) * P)
            pr.dma("sp", qblk[:, s2, :].rearrange("p (h t) -> p h t", h=2), aq4[:, n, :, :], writes=(("qblk", s2),))
            kbs = [kb for kb in (n - 1, n, n + 1) if 0 <= kb < NCH]
            for i, kb in enumerate(kbs):
                bank = pA[i // 2]
                bkey = "pA%d" % (i // 2)
                reg = bank[:, (i % 2) * 256:(i % 2) * 256 + 256]
                masked = kb != n
                pr.op("pe", "matmul", reads=("ak", ("qblk", s2)), writes=(bkey,), out=reg, lhsT=ak[:, kb * P:(kb + 1) * P],
                      rhs=qblk[:, s2, :], start=True, stop=not masked)
                if masked:
                    nm = nmp if kb < n else nmn
                    nmk = "nmp" if kb < n else "nmn"
                    pr.op("pe", "matmul", reads=("ident", nmk), writes=(bkey,), out=reg, lhsT=ident_bf[:], rhs=nm[:],
                          start=False, stop=True)
                yield
                pr.op("act", "activation", reads=(bkey,), writes=(("pT_sb", s2, i),), out=pT_sb[:, s2, i, :], in_=reg,
                      func=AF.Exp, scale=ATT_SCALE)
                yield
            for i, kb in enumerate(kbs):
                pr.op("pe", "matmul", reads=("av3", ("pT_sb", s2, i)), writes=("pO",), out=pO[:, 0:256], lhsT=av3[:, kb, :],
                      rhs=pT_sb[:, s2, i, :], start=(i == 0), stop=(i == len(kbs) - 1))
            for i, kb in enumerate(kbs):
                pr.op("pe", "matmul", reads=("ones_bf", ("pT_sb", s2, i)), writes=("pO",), out=pO[:, 256:512],
                      lhsT=ones_bf[:], rhs=pT_sb[:, s2, i, :], start=(i == 0), stop=(i == len(kbs) - 1))
            yield
            pr.op("dve", "tensor_tensor", reads=("pO", "esink"), writes=(("dent", s2),), out=dent[:, s2, :],
                  in0=pO[:, 256:512], in1=esink[:], op=ALU.add)
            pr.op("dve", "reciprocal", reads=(("dent", s2),), writes=(("dent", s2),), out=dent[:, s2, :], in_=dent[:, s2, :])
            yield
            pr.op("dve", "tensor_tensor", reads=("pO", ("dent", s2)), writes=(("ast", s2),), out=ast[:, s2, :],
                  in0=pO[:, 0:256], in1=dent[:, s2, :], op=ALU.mult)
            outs.append(pr.dma("sp", att_dst(n), ast[:, s2, :].rearrange("d (h t) -> d h t", h=2),
                               reads=(("ast", s2),), writes=(("out", len(outs)),)))
            yield

    comb_order = sorted(range(NCH), key=lambda n: (max(n, NCH - 1 - n), n))

    def combine_stream(r):
        for n in comb_order[r::NCS]:
            need = min(NCH, max(n, NCH - 1 - n) + 1 + LAG)
            while min(progress) < need:
                yield
            tok = slice(n * P, (n + 1) * P)
            for di in range(2):
                pr.dma("pool", hcm[:, r, di, 0:257], hfb_d[di, n, :, 0:257], reads=(("hfd", di, n),), writes=(("hcm", r, di),))
            pr.dma("sp", smo_t[:, r, :], smo3_r[:, n, :], writes=(("smo", r),))
            yield
            pr.op("act", "activation", reads=(("hcm", r, 0), ("hcm", r, 1)), writes=(("rr2", r),), out=rr2[:, r, :],
                  in_=hcm[:, r, :, 256], func=AF.Abs)
            yield
            pr.op("dve", "tensor_scalar", reads=(("rr2", r),), writes=(("rr2", r),), out=rr2[:, r, :], in0=rr2[:, r, :],
                  scalar1=1.0, scalar2=None, op0=ALU.max)
            pr.op("dve", "reciprocal", reads=(("rr2", r),), writes=(("rr2", r),), out=rr2[:, r, :], in_=rr2[:, r, :])
            yield
            pr.op("dve", "tensor_scalar", reads=(("hcm", r, 0), ("rr2", r)), writes=(("hsum", r),), out=hsum[:, r, :],
                  in0=hcm[:, r, 0, 0:256], scalar1=rr2[:, r, 0:1], scalar2=None, op0=ALU.mult)
            pr.op("dve", "scalar_tensor_tensor", reads=(("hcm", r, 1), ("rr2", r), ("hsum", r)), writes=(("hsum", r),),
                  out=hsum[:, r, :], in0=hcm[:, r, 1, 0:256], scalar=rr2[:, r, 1:2], in1=hsum[:, r, :],
                  op0=ALU.mult, op1=ALU.add)
            yield
            ss_c = ssn[:, r:r + 1]
            rs_c = rsn[:, r:r + 1]
            pr.op("act", "activation", reads=(("hsum", r),), writes=(("junk", r), ("ssn", r)), out=junk[:, r, :],
                  in_=hsum[:, r, :], func=AF.Square, accum_out=ss_c)
            pr.op("dve", "tensor_tensor", reads=(("smo", r), "gn"), writes=(("smo", r),), out=smo_t[:, r, :],
                  in0=smo_t[:, r, :], in1=gn[:], op=ALU.mult)
            yield
            pr.op("dve", "tensor_scalar", reads=(("ssn", r),), writes=(("rsn", r),), out=rs_c, in0=ss_c,
                  scalar1=1.0 / 256, scalar2=EPS, op0=ALU.mult, op1=ALU.add)
            yield
            pr.op("act", "activation", reads=(("rsn", r),), writes=(("rsn", r),), out=rs_c, in_=rs_c, func=AF.Sqrt)
            yield
            pr.op("dve", "reciprocal", reads=(("rsn", r),), writes=(("rsn", r),), out=rs_c, in_=rs_c)
            yield
            pr.op("dve", "scalar_tensor_tensor", reads=(("hsum", r), ("rsn", r), ("smo", r)), writes=(("obf", r),),
                  out=obf[:, r, :], in0=hsum[:, r, :], scalar=rs_c, in1=smo_t[:, r, :], op0=ALU.mult, op1=ALU.mult)
            yield
            for h2 in range(2):
                pr.op("pe", "transpose", reads=(("obf", r), "ident"), writes=("pT",),
                      out=pT[:, r * 256 + h2 * P:r * 256 + (h2 + 1) * P], in_=obf[:, r, h2 * P:(h2 + 1) * P],
                      identity=ident_bf[:])
            yield
            pr.op("act", "copy", reads=("pT",), writes=(("mst", r),), out=mst[:, r, :], in_=pT[:, r * 256:(r + 1) * 256])
            outs.append(pr.dma("sp", mem_dst(n), mst[:, r, :].rearrange("e (h t) -> e h t", h=2),
                               reads=(("mst", r),), writes=(("out", len(outs)),)))
            yield

    pre = preproc()
    active = [pre, attention_stream()]
    while active:
        for g in list(active):
            try:
                next(g)
            except StopIteration:
                active.remove(g)
                if g is pre:
                    qg, kg = q_stream(), k_stream()
                    pend = {id(qg), id(kg)}
                    active.extend([qg, kg])
                elif "pend" in dir() and id(g) in pend:
                    pend.discard(id(g))
                    if not pend:
                        active.extend([scan_stream(0), scan_stream(1)] + [combine_stream(r) for r in range(NCS)])
    pr.flush()


def _wx_index():
    cols = [np.arange(0, 1280), np.arange(1536, 2560), np.arange(1280, 1536), np.arange(4608, 4624),
            np.arange(2560, 4608)]
    return np.concatenate(cols)


def _rope_tables():
    half = 64
    inv_freq = (np.float32(10000.0) ** (-np.arange(half, dtype=np.float32) / np.float32(half))).astype(np.float32)
    pos = np.arange(S, dtype=np.float32)
    ang = (pos[:, None] * inv_freq[None, :]).astype(np.float32)
    cos = np.cos(ang).astype(np.float32)
    sin = np.sin(ang).astype(np.float32)
    cosT = np.concatenate([cos, cos], axis=1).T
    sinT = np.concatenate([-sin, sin], axis=1).T
    return np.ascontiguousarray(cosT), np.ascontiguousarray(sinT)


_CACHE = {}
_RUNKW = {}
_LAST = {}


def _get(name, builder):
    if name not in _CACHE:
        _CACHE[name] = builder()
    return _CACHE[name]


def run_k1(x_flat, w_in_l, g_pre_l):
    nc = _get("k1", build_k1)
    wx = np.ascontiguousarray(w_in_l[:, _get("wxi", _wx_index)])
    cosT, sinT = _get("rope", _rope_tables)
    gcol = np.ascontiguousarray(g_pre_l.reshape(KC, P).T)
    ident = np.eye(P, dtype=np.float32)
    in_maps = []
    for c in range(NCORE):
        p0 = (c % 4) * TOK
        in_maps.append({
            "x": np.ascontiguousarray(x_flat[c * TOK:(c + 1) * TOK]),
            "w_in": wx, "gcol": gcol,
            "cosT": np.ascontiguousarray(cosT[:, p0:p0 + TOK]),
            "sinT": np.ascontiguousarray(sinT[:, p0:p0 + TOK]),
            "ident": ident,
        })
    res = run_bass_kernel_spmd(nc, in_maps, core_ids=list(range(NCORE)), **_RUNKW)
    _LAST["t"] = res.exec_time_ns
    return res.results


def _k2_consts():
    r = np.arange(P)
    le = (r[:, None] <= r[None, :]).astype(np.float32)
    ge = (r[:, None] >= r[None, :]).astype(np.float32)
    nmp1 = np.where(r[:, None] < r[None, :], np.float32(-30000.0), np.float32(0.0)).astype(np.float32)
    nmn1 = np.where(r[:, None] > r[None, :], np.float32(-30000.0), np.float32(0.0)).astype(np.float32)
    return {
        "identf": np.eye(P, dtype=np.float32), "le": le, "ge": ge,
        "nmp": np.ascontiguousarray(np.concatenate([nmp1, nmp1], axis=1)),
        "nmn": np.ascontiguousarray(np.concatenate([nmn1, nmn1], axis=1)),
    }


def run_k2(k1, conv_w_l, gate_bias_l, ml_norm_g_l, attn_sink_l):
    nc = _get("k2", build_k2)
    consts = _get("k2c", _k2_consts)
    ca = np.ascontiguousarray
    in_maps = []
    for c in range(NCORE):
        b, j = c // 4, c % 4
        kv = j // 2
        cores = k1[4 * b:4 * b + 4]
        AQ = np.concatenate([r["aq"][2 * j * P:(2 * j + 2) * P] for r in cores], axis=1)
        AK = np.concatenate([r["ak"][kv * P:(kv + 1) * P] for r in cores], axis=1)
        AV = np.concatenate([r["av"][:, kv * P:(kv + 1) * P] for r in cores], axis=0)
        MQ = np.concatenate([r["mqk"][j * P:(j + 1) * P] for r in cores], axis=1)
        MK = np.concatenate([r["mqk"][512 + j * P:512 + (j + 1) * P] for r in cores], axis=1)
        MV = np.concatenate([r["mv"][:, j * 256:(j + 1) * 256] for r in cores], axis=0)
        SMO = np.concatenate([r["smo"][:, j * 256:(j + 1) * 256] for r in cores], axis=0)
        GT = np.concatenate([r["gt"][:, [j, 4 + j, 8 + j, 12 + j]] for r in cores], axis=0)
        m = dict(consts)
        m["aq2"] = ca(AQ.reshape(2, P, NCH, P).transpose(1, 2, 0, 3).reshape(P, NCH * 256))
        m["ak"] = ca(AK)
        m["av3"] = ca(AV.reshape(NCH, P, P).transpose(1, 0, 2).reshape(P, NCH * P))
        m["mq"] = ca(MQ)
        m["mk"] = ca(MK)
        m["mv3"] = ca(MV.reshape(NCH, P, 256).transpose(1, 0, 2).reshape(P, NCH * 256))
        m["smo3"] = ca(SMO.reshape(NCH, P, 256).transpose(1, 0, 2).reshape(P, NCH * 256))
        m["g4"] = ca(GT.reshape(NCH, P, 4).transpose(1, 2, 0).reshape(P, 4 * NCH))
        m["gb4"] = ca(np.broadcast_to(gate_bias_l[[j, 4 + j, 8 + j, 12 + j]][None, :], (P, 4)))
        cwq = conv_w_l[:, j * P:(j + 1) * P].T
        cwk = conv_w_l[:, 512 + j * P:512 + (j + 1) * P].T
        m["cw"] = ca(np.concatenate([cwq, cwk], axis=1))
        m["gn"] = ca(np.broadcast_to(ml_norm_g_l[j * 256:(j + 1) * 256][None, :], (P, 256)))
        m["sink2"] = ca(np.broadcast_to(np.repeat(attn_sink_l[2 * j:2 * j + 2], P)[None, :], (P, 256)))
        in_maps.append(m)
    res = run_bass_kernel_spmd(nc, in_maps, core_ids=list(range(NCORE)), **_RUNKW)
    _LAST["t"] = res.exec_time_ns
    out = res.results
    mixT = []
    for c in range(NCORE):
        b, part = c // 4, c % 4
        sl = slice(part * TOK, (part + 1) * TOK)
        att = [out[4 * b + j]["attT"][:, sl] for j in range(4)]
        mem = [out[4 * b + j]["memT"][:, sl] for j in range(4)]
        mixT.append(ca(np.concatenate(att + mem, axis=0)))
    return mixT


def kernel_unfused(x, w_in, conv_w, gate_bias, ml_norm_g, attn_sink, w_out,
                   g_pre_mix, g_post_mix, g_pre_mlp, g_post_mlp, w_up, w_down):
    f = lambda a: np.ascontiguousarray(np.asarray(a, dtype=np.float32))
    x = f(x)
    xf = x.reshape(B * S, D)
    depth = w_in.shape[0]
    for l in range(depth):
        k1 = run_k1(xf, f(w_in[l]), f(g_pre_mix[l]))
        mixT = run_k2(k1, f(conv_w[l]), f(gate_bias[l]), f(ml_norm_g[l]), f(attn_sink[l]))
        del k1
        ys = run_k3(mixT, xf, f(w_out[l]), f(w_up[l]), f(w_down[l]), f(g_post_mix[l]), f(g_pre_mlp[l]),
                    f(g_post_mlp[l]))
        xf = np.concatenate(ys, axis=0)
    return xf.reshape(B, S, D).astype(np.float32)


DEPTH = 2


def build_fused():
    nc = bass.Bass("TRN2", target_bir_lowering=False)

    def din(name, shape, dt=F32):
        return nc.dram_tensor(name, list(shape), dt, kind="ExternalInput").ap()

    def dint(name, shape, dt):
        return nc.dram_tensor(name, list(shape), dt).ap()
    x_in = din("x", [S, D])
    w_in = din("w_in", [DEPTH, D, WX_COLS])
    w_out = din("w_out", [DEPTH, D, D])
    w_up = din("w_up", [DEPTH, D, DFF])
    w_down = din("w_down", [DEPTH, DFF, D])
    gcol1 = din("gcol1", [DEPTH, P, KC])
    gcol2 = din("gcol2", [DEPTH, P, KC])
    g_pm = din("g_pm", [DEPTH, P, D])
    g_pl = din("g_pl", [DEPTH, P, D])
    cosT = din("cosT", [P, S])
    sinT = din("sinT", [P, S])
    ident = din("ident", [P, P])
    gb4 = din("gb4", [DEPTH, 4, P, 4])
    cw = din("cw", [DEPTH, 4, P, 6])
    gn = din("gn", [DEPTH, 4, P, 256])
    sink2 = din("sink2", [DEPTH, 4, P, 256])
    le = din("le", [P, P])
    ge = din("ge", [P, P])
    nmp = din("nmp", [P, 256])
    nmn = din("nmn", [P, 256])
    y_out = nc.dram_tensor("y", [TOK, D], F32, kind="ExternalOutput").ap()
    xidx_d = din("xidx", [P, TOK // P], mybir.dt.int32)
    midx_d = din("midx", [P, (TOK // 512) * KC], mybir.dt.int32)
    aq = dint("aq_s", [1024, S], BF16)
    ak = dint("ak_s", [256, S], BF16)
    mqk = dint("mqk_s", [1024, S], F32)
    av = dint("av_s", [S, 256], BF16)
    mv = dint("mv_s", [S, 1024], BF16)
    smo = dint("smo_s", [S, 1024], F32)
    gt = dint("gt_s", [P, NCH, 16], F32)
    NBLK = S // 512
    mixB = dint("mixB_s", [NBLK, D, 512], BF16)
    x1 = dint("x1_s", [S, D], F32)
    hb = dint("hb_s", [NCH, P, 256], F32)
    hfb = dint("hfb_s", [2, NCH, P, 258], F32)

    wb = []
    casts = []
    for l in range(DEPTH):
        wl = {"w_in": dint("wb_in%d" % l, [D, WX_COLS], BF16), "w_out": dint("wb_out%d" % l, [D, D], BF16),
              "w_up": dint("wb_up%d" % l, [D, DFF], BF16), "w_down": dint("wb_down%d" % l, [DFF, D], BF16)}
        wb.append(wl)
        cl = {}
        for nm, src in (("w_in", w_in[l]), ("w_out", w_out[l]), ("w_up", w_up[l]), ("w_down", w_down[l])):
            rows = src.shape[0]
            cl[nm] = [(wl[nm][r0:r0 + P, :], src[r0:r0 + P, :], ("wcast", l, nm, r0)) for r0 in range(0, rows, P)]
        casts.append(cl)

    pr = Prog(nc)
    for dst, src, k in casts[0]["w_in"]:
        pr.dma("pool", dst, src, writes=(k,))
    pr.flush()
    rest0 = casts[0]["w_out"] + casts[0]["w_up"] + casts[0]["w_down"]
    all1 = casts[1]["w_in"] + casts[1]["w_out"] + casts[1]["w_up"] + casts[1]["w_down"]
    q1 = (len(all1) + 3) // 4
    for l in range(DEPTH):
        xsrc = x_in if l == 0 else x1
        ydst = x1 if l == 0 else y_out
        emit_k1(pr, {"x": xsrc, "w": wb[l]["w_in"], "gcol": gcol1[l], "cos": cosT, "sin": sinT, "ident": ident,
                     "aq": aq, "ak": ak, "mqk": mqk, "av": av, "mv": mv, "smo": smo, "gt": gt}, S, True,
                extra_dmas=(rest0 if l == 0 else None))
        for j in range(4):
            kv = j // 2
            v = {
                "aq4": aq[2 * j * P:(2 * j + 2) * P, :].rearrange("(h d) (n t) -> d n h t", h=2, t=P),
                "ak": ak[kv * P:(kv + 1) * P, :],
                "av3": av[:, kv * P:(kv + 1) * P].rearrange("(n t) d -> t n d", t=P),
                "mq": mqk[j * P:(j + 1) * P, :],
                "mk": mqk[512 + j * P:512 + (j + 1) * P, :],
                "mv3": mv[:, j * 256:(j + 1) * 256].rearrange("(n t) e -> t n e", t=P),
                "smo3": smo[:, j * 256:(j + 1) * 256].rearrange("(n t) e -> t n e", t=P),
                "gall": gt,
                "gb4": gb4[l, j], "cw": cw[l, j], "gn": gn[l, j], "sink2": sink2[l, j],
                "identf": ident, "le": le, "ge": ge, "nmp": nmp, "nmn": nmn,
                "att_dst": (lambda n, j=j: mixB[n // 4, 2 * j * P:(2 * j + 2) * P, (n % 4) * P:(n % 4 + 1) * P]
                            .rearrange("(h d) t -> d h t", h=2)),
                "mem_dst": (lambda n, j=j: mixB[n // 4, 1024 + j * 256:1024 + (j + 1) * 256, (n % 4) * P:(n % 4 + 1) * P]
                            .rearrange("(h e) t -> e h t", h=2)),
                "hb": hb, "hfb": hfb,
                "extra_dmas": [],
            }
            emit_k2(pr, v, j)
        k3d = {"x": xsrc, "w_out": wb[l]["w_out"], "w_up": wb[l]["w_up"], "w_down": wb[l]["w_down"],
               "g_pm": g_pm[l], "g_pl": g_pl[l], "gcol": gcol2[l], "ident": ident, "y": ydst,
               "mix_blk": lambda tb: mixB[tb].rearrange("(kc p) t -> p kc t", p=P),
               "extra_dmas": (all1 if l == 0 else None)}
        if l == DEPTH - 1:
            k3d["gather"] = {"xidx": xidx_d, "midx": midx_d, "mix_flat": mixB.rearrange("b f t -> (b f) t")}
            emit_k3(pr, k3d, TOK)
        else:
            emit_k3(pr, k3d, S)
    pr.emit()
    return nc


def kernel(x, w_in, conv_w, gate_bias, ml_norm_g, attn_sink, w_out,
           g_pre_mix, g_post_mix, g_pre_mlp, g_post_mlp, w_up, w_down):
    f = lambda a: np.ascontiguousarray(np.asarray(a, dtype=np.float32))
    ca = np.ascontiguousarray
    nc = _get("fused", build_fused)
    x = f(x)
    w_in, conv_w, gate_bias, ml_norm_g, attn_sink = f(w_in), f(conv_w), f(gate_bias), f(ml_norm_g), f(attn_sink)
    g_pre_mix, g_post_mix, g_pre_mlp, g_post_mlp = f(g_pre_mix), f(g_post_mix), f(g_pre_mlp), f(g_post_mlp)
    wxi = _get("wxi", _wx_index)
    cosT, sinT = _get("rope", _rope_tables)
    consts = _get("k2c", _k2_consts)
    shared = {
        "w_in": ca(w_in[:, :, wxi]), "w_out": f(w_out), "w_up": f(w_up), "w_down": f(w_down),
        "gcol1": ca(g_pre_mix.reshape(DEPTH, KC, P).transpose(0, 2, 1)),
        "gcol2": ca(g_pre_mlp.reshape(DEPTH, KC, P).transpose(0, 2, 1)),
        "g_pm": ca(np.broadcast_to(g_post_mix[:, None, :], (DEPTH, P, D))),
        "g_pl": ca(np.broadcast_to(g_post_mlp[:, None, :], (DEPTH, P, D))),
        "cosT": cosT, "sinT": sinT, "ident": consts["identf"],
        "le": consts["le"], "ge": consts["ge"], "nmp": consts["nmp"], "nmn": consts["nmn"],
    }
    gb4 = np.zeros((DEPTH, 4, P, 4), np.float32)
    cw = np.zeros((DEPTH, 4, P, 6), np.float32)
    gn = np.zeros((DEPTH, 4, P, 256), np.float32)
    sink2 = np.zeros((DEPTH, 4, P, 256), np.float32)
    for l in range(DEPTH):
        for j in range(4):
            gb4[l, j] = gate_bias[l][[j, 4 + j, 8 + j, 12 + j]][None, :]
            cw[l, j, :, 0:3] = conv_w[l][:, j * P:(j + 1) * P].T
            cw[l, j, :, 3:6] = conv_w[l][:, 512 + j * P:512 + (j + 1) * P].T
            gn[l, j] = ml_norm_g[l][j * 256:(j + 1) * 256][None, :]
            sink2[l, j] = np.repeat(attn_sink[l][2 * j:2 * j + 2], P)[None, :]
    shared.update({"gb4": gb4, "cw": cw, "gn": gn, "sink2": sink2})
    in_maps = []
    pp = np.arange(P, dtype=np.int32)[:, None]
    for c in range(NCORE):
        part = c % 4
        m = dict(shared)
        m["x"] = ca(x[c // 4])
        m["xidx"] = ca((part * TOK + np.arange(TOK // P, dtype=np.int32)[None, :] * P + pp).astype(np.int32))
        tb = np.arange(TOK // 512, dtype=np.int32)[:, None]
        kc = np.arange(KC, dtype=np.int32)[None, :]
        rows = ((part * (TOK // 512) + tb) * D + kc * P).reshape(1, -1)
        m["midx"] = ca((rows + pp).astype(np.int32))
        in_maps.append(m)
    res = run_bass_kernel_spmd(nc, in_maps, core_ids=list(range(NCORE)), **_RUNKW)
    _LAST["t"] = res.exec_time_ns
    out = np.empty((B, S, D), np.float32)
    for c in range(NCORE):
        out[c // 4, (c % 4) * TOK:(c % 4 + 1) * TOK] = res.results[c]["y"]
    return out


kernel_fused = kernel
kernel = kernel_unfused
```
